# Optimizing a Trainium2 kernel written in Bass

```python
import math
import jax, jax.numpy as jnp
from jax import lax
import numpy as np

D_MODEL = 1024
BATCH = 32
SEQ = 2048
DEPTH = 2

N_A_LAYERS = (DEPTH + 1) // 2
N_B_LAYERS = DEPTH - N_A_LAYERS
HEAD_DIM = 64
N_HEADS = D_MODEL // HEAD_DIM
N_KV = 4
GQA_R = N_HEADS // N_KV
KV_DIM = N_KV * HEAD_DIM
NUM_BUCKETS = 32
MAX_DISTANCE = 2048
CMP_BLOCK = 32
CMP_STRIDE = 16
CMP_HIDDEN = 256
SEL_BLOCK = 64
SEL_TOPN = 8
SEL_Q_CHUNK = 16
FORCE_BONUS = 100.0
WIN = 256
BAND_BLOCK = 128
DILATIONS = ((128, 1), (512, 4), (2048, 16))
N_DIL = len(DILATIONS)
FFN_HIDDEN = -(-8 * D_MODEL // (3 * 256)) * 256
RMS_EPS = 1e-6
NEG_INF = -1e30
A_IN_DIM = D_MODEL + 6 * KV_DIM + 3 * N_HEADS
B_Q_DIM = N_DIL * D_MODEL
SHARED_KV_DIM = N_DIL * 2 * KV_DIM

kernel_name = "yoco_nsa_dilated_hybrid"


def rmsnorm(x, g):
    x32 = x.astype(jnp.float32)
    y = x32 * lax.rsqrt(jnp.mean(x32 * x32, axis=-1, keepdims=True) + RMS_EPS)
    return (y * g.astype(jnp.float32)).astype(x.dtype)


def t5_bucket(dist):
    max_exact = NUM_BUCKETS // 2
    d = jnp.maximum(dist, 0)
    log_ratio = jnp.log(jnp.maximum(d, max_exact).astype(jnp.float32) / max_exact) / math.log(MAX_DISTANCE / max_exact)
    large = max_exact + (log_ratio * (NUM_BUCKETS - max_exact)).astype(jnp.int32)
    return jnp.where(d < max_exact, d, jnp.minimum(large, NUM_BUCKETS - 1))


def masked_softmax(s, mask):
    s = jnp.where(mask, s.astype(jnp.float32), NEG_INF)
    m = jnp.max(s, axis=-1, keepdims=True)
    p = jnp.where(mask, jnp.exp(s - m), 0.0)
    return p, jnp.sum(p, axis=-1, keepdims=True), m


def banded_attention(q, k, v, max_dist, dist_scale, rel_bias):
    n, L, G, R, dh = q.shape
    nb = L // BAND_BLOCK
    n_prev = -(-max_dist // BAND_BLOCK)
    pad = n_prev * BAND_BLOCK
    kw_len = (n_prev + 1) * BAND_BLOCK

    def windows(t):
        tp = jnp.pad(t, ((0, 0), (pad, 0), (0, 0), (0, 0))).reshape(n, nb + n_prev, BAND_BLOCK, G, dh)
        return jnp.concatenate([tp[:, j:j + nb] for j in range(n_prev + 1)], axis=2)

    kw, vw = windows(k), windows(v)
    qb = q.reshape(n, nb, BAND_BLOCK, G, R, dh)
    s = jnp.einsum('nbqgrd,nbkgd->nbgrqk', qb, kw).astype(jnp.float32) * dh ** -0.5
    qi = jnp.arange(BAND_BLOCK)[:, None]
    ki = jnp.arange(kw_len)[None, :]
    dist = pad + qi - ki
    key_pos = jnp.arange(nb)[:, None, None] * BAND_BLOCK - pad + ki[None]
    mask = (dist >= 0) & (dist <= max_dist) & (key_pos >= 0)
    bias = rel_bias[t5_bucket(dist * dist_scale)].reshape(BAND_BLOCK, kw_len, G, R).transpose(2, 3, 0, 1)
    p, denom, m = masked_softmax(s + bias, mask[None, :, None, None])
    o = jnp.einsum('nbgrqk,nbkgd->nbqgrd', p, vw.astype(jnp.float32))
    o = o / denom[..., 0].transpose(0, 1, 4, 2, 3)[..., None]
    lse = (jnp.log(denom) + m)[..., 0].transpose(0, 1, 4, 2, 3).reshape(n, L, G, R)
    return o.reshape(n, L, G, R, dh), lse


def dilated_group_attention(q, k, v, dilation, window, rel_bias):
    B, S = q.shape[:2]
    L = S // dilation
    Lp = -(-L // BAND_BLOCK) * BAND_BLOCK

    def to_sub(t):
        rest = t.shape[2:]
        t = t.reshape((B, L, dilation) + rest)
        t = jnp.moveaxis(t, 2, 1).reshape((B * dilation, L) + rest)
        return jnp.pad(t, ((0, 0), (0, Lp - L)) + ((0, 0),) * len(rest))

    def from_sub(t):
        rest = t.shape[2:]
        t = t[:, :L].reshape((B, dilation, L) + rest)
        return jnp.moveaxis(t, 1, 2).reshape((B, S) + rest)

    o, lse = banded_attention(to_sub(q), to_sub(k), to_sub(v), window // dilation, dilation, rel_bias)
    return from_sub(o), from_sub(lse)


def nsa_attention(h, w_in, b_gate, pe_k, w1_k, w2_k, pe_v, w1_v, w2_v, w_out, rel_bias):
    B, S, _ = h.shape
    proj = h @ w_in
    splits = [D_MODEL + i * KV_DIM for i in range(7)]
    q, kc, vc, ks, vs, kwn, vwn, gl = jnp.split(proj, splits, axis=-1)
    q = q.reshape(B, S, N_KV, GQA_R, HEAD_DIM)
    kvh = lambda t: t.reshape(B, S, N_KV, HEAD_DIM)
    scale = HEAD_DIM ** -0.5
    t_pos = jnp.arange(S)

    n_c = (S - CMP_BLOCK) // CMP_STRIDE + 1
    starts = jnp.arange(n_c) * CMP_STRIDE
    tok_idx = starts[:, None] + jnp.arange(CMP_BLOCK)[None, :]

    def compress(t, pe, w1, w2):
        blk = t[:, tok_idx] + pe[:, None, :]
        blk = jnp.moveaxis(blk, 3, 2).reshape(B, n_c, N_KV, CMP_BLOCK * HEAD_DIM)
        return jax.nn.gelu(blk @ w1) @ w2

    k_cmp = compress(kvh(kc), pe_k, w1_k, w2_k)
    v_cmp = compress(kvh(vc), pe_v, w1_v, w2_v)
    dist_c = t_pos[:, None] - (starts + CMP_BLOCK - 1)[None, :]
    bias_c = rel_bias[t5_bucket(dist_c)].reshape(S, n_c, N_KV, GQA_R).transpose(2, 3, 0, 1)
    s = jnp.einsum('bsgrd,bngd->bgrsn', q, k_cmp).astype(jnp.float32) * scale
    p, denom, _ = masked_softmax(s + bias_c, dist_c >= 0)
    p_cmp = p / jnp.maximum(denom, 1e-30)
    o_cmp = jnp.einsum('bgrsn,bngd->bsgrd', p_cmp, v_cmp.astype(jnp.float32))

    n_sel = S // SEL_BLOCK
    top_n = min(SEL_TOPN, n_sel)
    ci = jnp.arange(n_c)[:, None] * CMP_STRIDE
    sj = jnp.arange(n_sel)[None, :] * SEL_BLOCK
    overlap = ((ci < sj + SEL_BLOCK) & (ci + CMP_BLOCK > sj)).astype(jnp.float32)
    imp = jnp.einsum('bgrsn,nj->bgsj', p_cmp, overlap)
    blk_j = jnp.arange(n_sel)[None, :]
    cur = (t_pos // SEL_BLOCK)[:, None]
    forced = (blk_j == 0) | (blk_j == cur) | (blk_j == cur - 1)
    score = jnp.where(forced, imp + FORCE_BONUS, jnp.where(blk_j <= cur, imp, -1.0))
    _, sel_idx = lax.top_k(score, top_n)

    k_blocks = kvh(ks).reshape(B, n_sel, SEL_BLOCK, N_KV, HEAD_DIM).transpose(0, 3, 1, 2, 4)
    v_blocks = kvh(vs).reshape(B, n_sel, SEL_BLOCK, N_KV, HEAD_DIM).transpose(0, 3, 1, 2, 4)
    n_chunk = S // SEL_Q_CHUNK
    q_ch = jnp.moveaxis(q.reshape(B, n_chunk, SEL_Q_CHUNK, N_KV, GQA_R, HEAD_DIM), 1, 0)
    idx_ch = jnp.moveaxis(sel_idx.reshape(B, N_KV, n_chunk, SEL_Q_CHUNK, top_n), 2, 0)
    c0 = jnp.arange(n_chunk) * SEL_Q_CHUNK
    b_ix = jnp.arange(B)[:, None, None, None]
    g_ix = jnp.arange(N_KV)[None, :, None, None]
    rb_g = rel_bias.reshape(NUM_BUCKETS, N_KV, GQA_R).transpose(1, 0, 2)
    n_keys = top_n * SEL_BLOCK

    def sel_chunk(args):
        qc, ic, start = args
        kg = k_blocks[b_ix, g_ix, ic].reshape(B, N_KV, SEL_Q_CHUNK, n_keys, HEAD_DIM)
        vg = v_blocks[b_ix, g_ix, ic].reshape(B, N_KV, SEL_Q_CHUNK, n_keys, HEAD_DIM)
        kpos = (ic[..., None] * SEL_BLOCK + jnp.arange(SEL_BLOCK)).reshape(B, N_KV, SEL_Q_CHUNK, n_keys)
        dist = (start + jnp.arange(SEL_Q_CHUNK))[:, None] - kpos
        bias = jnp.moveaxis(rb_g[g_ix, t5_bucket(dist)], -1, 3)
        s_sel = jnp.einsum('bcgrd,bgckd->bgcrk', qc, kg).astype(jnp.float32) * scale + bias
        p_sel, den, _ = masked_softmax(s_sel, (dist >= 0)[:, :, :, None, :])
        return jnp.einsum('bgcrk,bgckd->bcgrd', p_sel / jnp.maximum(den, 1e-30), vg.astype(jnp.float32))

    o_sel = jnp.moveaxis(lax.map(sel_chunk, (q_ch, idx_ch, c0)), 0, 1).reshape(B, S, N_KV, GQA_R, HEAD_DIM)

    o_win, _ = banded_attention(q, kvh(kwn), kvh(vwn), WIN - 1, 1, rel_bias)

    g = jax.nn.sigmoid((gl + b_gate).astype(jnp.float32)).reshape(B, S, N_KV, GQA_R, 3)
    o = g[..., 0:1] * o_cmp + g[..., 1:2] * o_sel + g[..., 2:3] * o_win
    return o.reshape(B, S, D_MODEL).astype(h.dtype) @ w_out


def dilated_attention(h, w_q, k_shared, v_shared, w_out, rel_bias):
    B, S, _ = h.shape
    q = (h @ w_q).reshape(B, S, N_DIL, N_KV, GQA_R, HEAD_DIM)
    outs, lses = [], []
    for gi, (window, dilation) in enumerate(DILATIONS):
        o, lse = dilated_group_attention(q[:, :, gi], k_shared[:, :, gi], v_shared[:, :, gi], dilation, window, rel_bias)
        outs.append(o)
        lses.append(lse)
    alpha = jax.nn.softmax(jnp.stack(lses, axis=-1), axis=-1)
    o = jnp.einsum('bsgrdi,bsgri->bsgrd', jnp.stack(outs, axis=-1), alpha)
    return o.reshape(B, S, D_MODEL).astype(h.dtype) @ w_out


def swiglu(h, w_up, w_down):
    a, b = jnp.split(h @ w_up, 2, axis=-1)
    return (jax.nn.silu(a) * b) @ w_down


def setup_inputs(seed: int = 0) -> dict:
    key = jax.random.key(seed)
    ks = jax.random.split(key, 20)
    f32 = jnp.float32
    nrm = lambda k, shape, s: s * jax.random.normal(k, shape, f32)
    w = lambda k, shape, fan_in: jax.random.normal(k, shape, f32) * fan_in ** -0.5
    gain = lambda k, shape: 1.0 + 0.05 * jax.random.normal(k, shape, f32)
    cmp_in = CMP_BLOCK * HEAD_DIM
    return {
        "x": jax.random.normal(ks[0], (BATCH, SEQ, D_MODEL), f32),
        "rel_bias": nrm(ks[1], (NUM_BUCKETS, N_HEADS), 0.5),
        "norm_mix": gain(ks[2], (DEPTH, D_MODEL)),
        "norm_ffn": gain(ks[3], (DEPTH, D_MODEL)),
        "a_w_in": w(ks[4], (N_A_LAYERS, D_MODEL, A_IN_DIM), D_MODEL),
        "a_b_gate": nrm(ks[5], (N_A_LAYERS, 3 * N_HEADS), 0.1),
        "a_pe_k": nrm(ks[6], (N_A_LAYERS, CMP_BLOCK, HEAD_DIM), 0.1),
        "a_w1_k": w(ks[7], (N_A_LAYERS, cmp_in, CMP_HIDDEN), cmp_in),
        "a_w2_k": w(ks[8], (N_A_LAYERS, CMP_HIDDEN, HEAD_DIM), CMP_HIDDEN),
        "a_pe_v": nrm(ks[9], (N_A_LAYERS, CMP_BLOCK, HEAD_DIM), 0.1),
        "a_w1_v": w(ks[10], (N_A_LAYERS, cmp_in, CMP_HIDDEN), cmp_in),
        "a_w2_v": w(ks[11], (N_A_LAYERS, CMP_HIDDEN, HEAD_DIM), CMP_HIDDEN),
        "a_w_out": w(ks[12], (N_A_LAYERS, D_MODEL, D_MODEL), D_MODEL),
        "kv_norm": gain(ks[13], (D_MODEL,)),
        "kv_w": w(ks[14], (D_MODEL, SHARED_KV_DIM), D_MODEL),
        "b_w_q": w(ks[15], (N_B_LAYERS, D_MODEL, B_Q_DIM), D_MODEL),
        "b_w_out": w(ks[16], (N_B_LAYERS, D_MODEL, D_MODEL), D_MODEL),
        "ffn_w_up": w(ks[17], (DEPTH, D_MODEL, 2 * FFN_HIDDEN), D_MODEL),
        "ffn_w_down": w(ks[18], (DEPTH, FFN_HIDDEN, D_MODEL), FFN_HIDDEN),
        "final_norm": gain(ks[19], (D_MODEL,)),
    }


def reference(x, rel_bias, norm_mix, norm_ffn, a_w_in, a_b_gate, a_pe_k, a_w1_k, a_w2_k, a_pe_v, a_w1_v, a_w2_v,
              a_w_out, kv_norm, kv_w, b_w_q, b_w_out, ffn_w_up, ffn_w_down, final_norm):
    B, S, _ = x.shape
    k_shared = v_shared = None
    for layer in range(DEPTH):
        h = rmsnorm(x, norm_mix[layer])
        if layer < N_A_LAYERS:
            i = layer
            mix = nsa_attention(h, a_w_in[i], a_b_gate[i], a_pe_k[i], a_w1_k[i], a_w2_k[i],
                                a_pe_v[i], a_w1_v[i], a_w2_v[i], a_w_out[i], rel_bias)
        else:
            if layer == N_A_LAYERS:
                kv = (rmsnorm(x, kv_norm) @ kv_w).reshape(B, S, N_DIL, 2, N_KV, HEAD_DIM)
                k_shared, v_shared = kv[:, :, :, 0], kv[:, :, :, 1]
            j = layer - N_A_LAYERS
            mix = dilated_attention(h, b_w_q[j], k_shared, v_shared, b_w_out[j], rel_bias)
        x = x + mix.astype(x.dtype)
        x = x + swiglu(rmsnorm(x, norm_ffn[layer]), ffn_w_up[layer], ffn_w_down[layer]).astype(x.dtype)
    return rmsnorm(x, final_norm)
```

```python
import os
import sys
import numpy as np
from contextlib import ExitStack
import concourse.bass as bass
import concourse.mybir as mybir
from concourse.bass_utils import run_bass_kernel_spmd

F32 = mybir.dt.float32
BF16 = mybir.dt.bfloat16
AF = mybir.ActivationFunctionType
ALU = mybir.AluOpType
AX = mybir.AxisListType

ENGS = ("tensor", "vector", "scalar", "gpsimd", "sync")
SEM_ROT = 6000


class Sched:
    def __init__(self, nc, n_dma_sems=(("sync", 20), ("gpsimd", 12), ("scalar", 6))):
        self.nc = nc
        self.ops = []
        self.top = ExitStack()
        self.scopes = [self.top]
        self.last_w = {}
        self.readers = {}
        self.npos = {e: 0 for e in ENGS}
        self.know = {e: {f: 0 for f in ENGS} for e in ENGS}
        self.dma_known = {e: set() for e in ENGS}
        self.done = []
        self.dma_sems = {}
        for e, n in n_dma_sems:
            self.dma_sems[e] = [[self._sem(f"d_{e}_{i}"), 0, None] for i in range(n)]
        self.dma_rr = {e: 0 for e in self.dma_sems}
        self.eng_sem = {e: [self._sem(f"c_{e}_0"), 0, 0] for e in ENGS}
        self.sig = {}
        self.psum_keys = set()
        self.block = None

    def _sem(self, name):
        return self.top.enter_context(self.nc.semaphore(name))

    def push(self):
        self.scopes.append(ExitStack())

    def pop(self):
        self.flush(barrier=True)
        self.scopes.pop().close()

    def _uniq(self, name):
        self._cnt = getattr(self, "_cnt", 0) + 1
        return f"{name}__{self._cnt}"

    def sb(self, name, shape, dtype):
        return self.scopes[-1].enter_context(self.nc.sbuf_tensor(self._uniq(name), list(shape), dtype))

    def ps(self, name, shape, dtype):
        return self.scopes[-1].enter_context(self.nc.psum_tensor(self._uniq(name), list(shape), dtype))

    def op(self, eng, fn, reads=(), writes=()):
        pw = tuple(k for k in reads if k in self.psum_keys and k not in writes)
        fr = sys._getframe(1)
        if fr.f_code.co_name in ("_mm", "_tr"):
            fr = fr.f_back
        self.ops.append(dict(eng=eng, fn=fn, reads=tuple(reads), writes=tuple(writes) + pw, wtrue=tuple(writes), dma=False,
                             site=(fr.f_code.co_name, fr.f_lineno)))

    def dma(self, eng, out, in_, reads=(), writes=(), **kw):
        self.ops.append(dict(eng=eng, fn=lambda e: e.dma_start(out=out, in_=in_, **kw),
                             reads=tuple(reads), writes=tuple(writes), wtrue=tuple(writes), dma=True,
                             site=(sys._getframe(1).f_code.co_name, sys._getframe(1).f_lineno)))

    def flush(self, barrier=False, final_keys=None):
        ops = self.ops
        self.ops = []
        base = len(self.done)
        for i, o in enumerate(ops):
            gid = base + i
            deps = set()
            for k in o["reads"]:
                if k in self.last_w:
                    deps.add(self.last_w[k])
            for k in o["writes"]:
                if k in self.last_w:
                    deps.add(self.last_w[k])
                deps.update(self.readers.get(k, ()))
            deps.discard(gid)
            o["deps"] = deps
            o["gid"] = gid
            for k in o["reads"]:
                self.readers.setdefault(k, []).append(gid)
            for k in o["writes"]:
                self.last_w[k] = gid
                self.readers[k] = []
            self.done.append(o)
        for o in ops:
            e = o["eng"]
            self.npos[e] += 1
            o["pos"] = self.npos[e]
            waits = []
            know = self.know[e]
            for d in sorted(o["deps"]):
                dd = self.done[d]
                if dd["dma"]:
                    if d in self.dma_known[e]:
                        continue
                    self.dma_known[e].add(d)
                    waits.append(d)
                else:
                    f = dd["eng"]
                    if f == e:
                        raw = any(k in dd["wtrue"] for k in o["reads"])
                        if not raw or know[f] >= dd["pos"]:
                            continue
                    if know[f] >= dd["pos"]:
                        continue
                    waits.append(d)
                    know[f] = max(know[f], dd["pos"])
                    for g, v in dd["ksnap"].items():
                        if v > know[g]:
                            know[g] = v
            o["waits"] = waits
            o["ksnap"] = dict(know)
            for d in waits:
                self.done[d]["signal"] = True
        if final_keys is not None:
            self.final_wait = []
            for k in final_keys:
                d = self.last_w[k]
                self.done[d]["signal"] = True
                self.final_wait.append(d)
        if barrier:
            self._barrier_prepare(ops)
        by_eng = {e: [o for o in ops if o["eng"] == e] for e in ENGS}
        self._emit(by_eng, barrier, final_keys is not None)

    def _barrier_prepare(self, ops):
        self.bar_targets = []
        for e in ENGS:
            lst = [o for o in ops if o["eng"] == e and not o["dma"]]
            if lst:
                lst[-1]["signal"] = True
                self.bar_targets.append(lst[-1]["gid"])
        for o in ops:
            if o["dma"]:
                o["signal"] = True
                self.bar_targets.append(o["gid"])

    def _assign_signal(self, o):
        e = o["eng"]
        if o["dma"]:
            return None
        if not o.get("signal"):
            return None
        rec = self.eng_sem[e]
        if rec[1] >= SEM_ROT:
            rec[2] += 1
            rec[0] = self._sem(f"c_{e}_{rec[2]}")
            rec[1] = 0
        rec[1] += 1
        self.sig[o["gid"]] = (rec[0], rec[1])
        return (rec[0], 1)

    def _emit(self, by_eng, barrier, final):
        nc = self.nc
        for e in ENGS:
            for o in by_eng[e]:
                if not o["dma"]:
                    o["_sig"] = self._assign_signal(o)
        for e in ENGS:
            for o in by_eng[e]:
                if o["dma"]:
                    pool = self.dma_sems[e]
                    idx = self.dma_rr[e]
                    self.dma_rr[e] = (idx + 1) % len(pool)
                    rec = pool[idx]
                    o["_prev"] = (rec[0], rec[1]) if rec[1] > 0 else None
                    rec[1] += 16
                    rec[2] = o["gid"]
                    self.sig[o["gid"]] = (rec[0], rec[1])
                    o["_sig"] = (rec[0], 16)
        bar = list(self.bar_targets) if barrier else []
        fin = list(self.final_wait) if final else []

        def run(e):
            def body(eng):
                for o in by_eng[e]:
                    for d in o["waits"]:
                        s, v = self.sig[d]
                        eng.wait_ge(s, v)
                    if o["dma"] and o["_prev"] is not None:
                        eng.wait_ge(*o["_prev"])
                    try:
                        ins = o["fn"](eng)
                    except Exception:
                        print("EMIT FAILED at", o.get("site"), flush=True)
                        raise
                    if o["_sig"] is not None:
                        ins.then_inc(o["_sig"][0], o["_sig"][1])
                for d in bar:
                    s, v = self.sig[d]
                    eng.wait_ge(s, v)
                if e == "sync":
                    for d in fin:
                        s, v = self.sig[d]
                        eng.wait_ge(s, v)
            return body

        with nc.Block() as block:
            for e in ENGS:
                getattr(block, e)(run(e))
        if barrier:
            for e in ENGS:
                for f in ENGS:
                    self.know[e][f] = self.npos[f]
                self.dma_known[e].update(o["gid"] for o in self.done if o["dma"])

    def finish(self, final_keys):
        self.flush(barrier=False, final_keys=final_keys)
        while len(self.scopes) > 1:
            self.scopes.pop().close()
        self.top.close()


NCORES = 8
NSEQ = 4
SL = 2048
DM = 1024
NT = 16
FF = 2816
NFC = 22
MASKV = -30000.0
DILS = ((128, 1), (512, 4), (2048, 16))
GELU_C = 1.5957691216057308


class Ring:
    def __init__(self, S, name, shape, dtype, n, psum=False):
        mk = S.ps if psum else S.sb
        self.t = [mk(f"{name}{k}", shape, dtype) for k in range(n)]
        self.k = [f"{name}{k}" for k in range(n)]
        if psum:
            nbytes = int(np.prod(shape[1:])) * (4 if dtype == F32 else 2)
            assert nbytes == 2048, (name, shape)
            S.psum_keys.update(self.k)
        self.n = n
        self.i = 0

    def next(self):
        j = self.i % self.n
        self.i += 1
        return self.t[j], self.k[j]


class Ctx:
    pass


def _mm(S, out, lhsT, rhs, start, stop, reads, writes, sgc=False):
    S.op("tensor", lambda e: e.matmul(out, lhsT, rhs, start=start, stop=stop, skip_group_check=sgc), reads=reads, writes=writes)


def _tr(S, out, in_, ident, reads, writes):
    S.op("tensor", lambda e: e.transpose(out, in_, ident), reads=reads, writes=writes)


def norm_tile(S, c, W, src_ap, src_keys, gains, xring):
    xt, xk = xring.next()
    S.dma("sync", xt[:], src_ap, reads=src_keys, writes=[xk])
    st, sk = W["ss"].next()
    S.op("gpsimd", lambda e: e.memset(st[:], 0.0), writes=[sk])
    S.op("scalar", lambda e: e.activation(W["junk"][:], xt[:], AF.Square, accum_out=st[:, 0:1]),
         reads=[xk, sk], writes=["junk", sk])
    S.op("vector", lambda e: e.tensor_scalar(st[:, 1:2], st[:, 0:1], 1.0 / DM, 1e-6, ALU.mult, ALU.add),
         reads=[sk], writes=[sk])
    S.op("scalar", lambda e: e.activation(st[:, 3:4], st[:, 1:2], AF.Sqrt), reads=[sk], writes=[sk])
    S.op("vector", lambda e: e.reciprocal(st[:, 2:3], st[:, 3:4]), reads=[sk], writes=[sk])
    outs = []
    for (gt, gk) in gains:
        hb, hk = W["hb"].next()
        S.op("vector", lambda e, hb=hb, gt=gt: e.scalar_tensor_tensor(hb[:], xt[:], st[:, 2:3], gt[:], ALU.mult, ALU.mult),
             reads=[xk, sk, gk], writes=[hk])
        outs.append((hb, hk))
    return xt, xk, st, sk, outs


def transpose_tile(S, c, W, hb, hk, dst_ap, dst_key, eng="scalar"):
    pt, pk = W["ptr"].next()
    for k in range(8):
        _tr(S, pt[:, k, :], hb[:, k * 128:(k + 1) * 128], c.identb[:], [hk, "identb"], [pk])
    if eng == "scalar":
        S.op("scalar", lambda e: e.copy(dst_ap, pt[:]), reads=[pk], writes=[dst_key])
    else:
        S.op("vector", lambda e: e.tensor_copy(dst_ap, pt[:]), reads=[pk], writes=[dst_key])


def load_gain(S, c, name, row):
    t = S.sb(name, [128, DM], F32)
    S.dma("sync", t[:], c.gains[row].partition_broadcast(128), reads=[], writes=[name])
    return t, name


def norm_work(S, pfx):
    return dict(ss=Ring(S, pfx + "ss", [128, 4], F32, 4), hb=Ring(S, pfx + "hb", [128, DM], BF16, 3),
                junk=S.sb(pfx + "junk", [128, DM], BF16), ptr=Ring(S, pfx + "ptr", [128, 8, 128], BF16, 2, psum=True))


def convert_all(S, c):
    S.push()
    NB = 8
    CW = 2048
    st = [S.sb(f"cvf{k}", [128, CW], F32) for k in range(NB)]
    sb = [S.sb(f"cvb{k}", [128, CW], BF16) for k in range(NB)]
    jobs = []
    for (src, dst, key) in c.conv_jobs:
        R, C = src.shape
        assert R % 128 == 0, (key, R)
        for r0 in range(0, R, 128):
            for c0 in range(0, C, CW):
                w = min(CW, C - c0)
                jobs.append((src[r0:r0 + 128, c0:c0 + w], dst[r0:r0 + 128, c0:c0 + w], w, key))
    for n, (src, dst, w, key) in enumerate(jobs):
        k = n % NB
        S.dma("sync", st[k][:, :w], src, reads=[], writes=[f"cvf{k}"])
        S.op("gpsimd" if n % 3 == 2 else "vector", lambda e, k=k, w=w: e.tensor_copy(sb[k][:, :w], st[k][:, :w]),
             reads=[f"cvf{k}"], writes=[f"cvb{k}"])
        S.dma("scalar", dst, sb[k][:, :w], reads=[f"cvb{k}"], writes=[key])
    S.pop()


def l0_proj(S, c, A, Bk, b):
    S.push()
    Win = S.sb("Win", [128, 8, 2608], BF16)
    for k in range(8):
        S.dma("sync", Win[:, k, :], c.w_in_b[k * 128:(k + 1) * 128, :], reads=[], writes=[("Win", k)])
    gmix = load_gain(S, c, "gmix0", 0)
    bg = S.sb("bgate", [128, 48], F32)
    S.dma("sync", bg[:], c.b_gate[0].partition_broadcast(128), reads=[], writes=["bgate"])
    W = norm_work(S, "p1")
    xring = Ring(S, "p1x", [128, DM], F32, 3)
    hT = Ring(S, "p1hT", [128, 8, 512], BF16, 2)
    pfm = Ring(S, "p1pfm", [128, 512], F32, 3, psum=True)
    ptm = Ring(S, "p1ptm", [128, 512], F32, 2, psum=True)
    glt = Ring(S, "p1gl", [128, 48], F32, 2)
    S.op("gpsimd", lambda e: e.memset(A["Vs"][:, :, :, 64:65], 1.0), writes=["Vs1"])
    S.op("gpsimd", lambda e: e.memset(A["Vw"][:, :, :, 64:65], 1.0), writes=["Vw1"])
    fmdst = []
    for fc in range(8):
        fmdst.append((A["QT"], fc // 4, fc % 4, 0.125))
    for nm in ("KcT", "VcT"):
        for j in range(2):
            fmdst.append((Bk[nm], j, None, 1.0))
    for nm in ("KsT", "KwT"):
        for j in range(2):
            fmdst.append((A[nm], j, None, 1.0))
    nev = 0
    for mt in range(4):
        hTt, hTk = hT.next()
        for t in range(4):
            i = mt * 4 + t
            xt, xk, st, sk, outs = norm_tile(S, c, W, c.x[b, i * 128:(i + 1) * 128, :], [], [gmix], xring)
            transpose_tile(S, c, W, outs[0][0], outs[0][1], hTt[:, :, t * 128:(t + 1) * 128], (hTk, t))
        allk = [(hTk, t) for t in range(4)]
        for fc in range(16):
            pt, pk = pfm.next()
            for k in range(8):
                _mm(S, pt[:], Win[:, k, fc * 128:(fc + 1) * 128], hTt[:, k, :], k == 0, k == 7,
                    [("Win", k)] + allk, [pk])
            T, j, r, sc = fmdst[fc]
            dst = T[:, j, r, mt * 512:(mt + 1) * 512] if r is not None else T[:, j, mt * 512:(mt + 1) * 512]
            if nev % 2 == 0:
                S.op("scalar", lambda e, dst=dst, pt=pt, sc=sc: e.mul(dst, pt[:], sc), reads=[pk], writes=[("fm", fc, mt)])
            else:
                S.op("vector", lambda e, dst=dst, pt=pt, sc=sc: e.tensor_scalar(dst, pt[:], sc, None, ALU.mult),
                     reads=[pk], writes=[("fm", fc, mt)])
            nev += 1
        for t in range(4):
            i = mt * 4 + t
            pt, pk = ptm.next()
            for k in range(8):
                _mm(S, pt[:], hTt[:, k, t * 128:(t + 1) * 128], Win[:, k, 2048:2560], k == 0, k == 7,
                    [("Win", k), (hTk, t)], [pk])
            S.op("vector", lambda e, pt=pt, i=i: e.tensor_copy(A["Vs"][:, i, :, 0:64], pt[:, 0:256].rearrange("p (g d) -> p g d", g=4)),
                 reads=[pk, "Vs1"], writes=[("Vs", i)])
            S.op("scalar", lambda e, pt=pt, i=i: e.copy(A["Vw"][:, i, :, 0:64], pt[:, 256:512].rearrange("p (g d) -> p g d", g=4)),
                 reads=[pk, "Vw1"], writes=[("Vw", i)])
            pt2, pk2 = ptm.next()
            for k in range(8):
                _mm(S, pt2[:, 0:48], hTt[:, k, t * 128:(t + 1) * 128], Win[:, k, 2560:2608], k == 0, k == 7,
                    [("Win", k), (hTk, t)], [pk2])
            gt, gk = glt.next()
            S.op("vector", lambda e, gt=gt, pt2=pt2: e.tensor_tensor(gt[:], pt2[:, 0:48], bg[:], ALU.add),
                 reads=[pk2, "bgate"], writes=[gk])
            S.op("scalar", lambda e, gt=gt, i=i: e.activation(A["G"][:, i, :], gt[:], AF.Sigmoid),
                 reads=[gk], writes=[("G", i)])
    S.pop()


def l0_compress(S, c, A, Bk):
    S.push()
    pgel = Ring(S, "p2ph", [128, 512], F32, 2, psum=True)
    pmisc = Ring(S, "p2pm", [128, 512], F32, 2, psum=True)
    for kv, (w1b, w2b, peT, src) in enumerate(((c.w1k_b, c.w2k_b, c.pekT, Bk["KcT"]), (c.w1v_b, c.w2v_b, c.pevT, Bk["VcT"]))):
        W1 = S.sb(f"W1_{kv}", [128, 32, 256], BF16)
        for u in range(2):
            S.dma("sync", W1[u * 64:(u + 1) * 64, :, :], w1b.rearrange("(t d) n -> d t n", d=64), reads=[], writes=[(f"W1_{kv}", u)])
        W2 = S.sb(f"W2_{kv}", [128, 2, 128], BF16)
        for h in range(2):
            S.dma("sync", W2[:, :, h * 64:(h + 1) * 64], w2b.rearrange("(c p) n -> p c n", p=128), reads=[], writes=[(f"W2_{kv}", h)])
        pef = S.sb(f"pef_{kv}", [64, 32], F32)
        S.dma("sync", pef[:], peT, reads=[], writes=[f"pef_{kv}"])
        peb = S.sb(f"peb_{kv}", [64, 32], BF16)
        S.op("vector", lambda e, peb=peb, pef=pef: e.tensor_copy(peb[:], pef[:]), reads=[f"pef_{kv}"], writes=[f"peb_{kv}"])
        pb, pbk = pmisc.next()
        for c2 in range(2):
            for t in range(32):
                _mm(S, pb[:, c2:c2 + 1], W1[0:64, t, c2 * 128:(c2 + 1) * 128], peb[0:64, t:t + 1], t == 0, t == 31,
                    [(f"W1_{kv}", 0), f"peb_{kv}"], [pbk])
        pbias = S.sb(f"pbias_{kv}", [128, 2], F32)
        S.op("vector", lambda e, pbias=pbias, pb=pb: e.tensor_copy(pbias[:], pb[:, 0:2]), reads=[pbk], writes=[f"pbias_{kv}"])
        gelT = Ring(S, f"p2gel{kv}", [128, 2, 128], BF16, 2)
        tmp = Ring(S, f"p2tmp{kv}", [128, 4, 128], F32, 2)
        for g in range(4):
            u, j = g // 2, g % 2
            prt = slice(u * 64, (u + 1) * 64)
            gl, glk = gelT.next()
            for c2 in range(2):
                ph, phk = pgel.next()
                for t in range(32):
                    _mm(S, ph[:, 0:127], W1[prt, t, c2 * 128:(c2 + 1) * 128], src[prt, j, t:t + 16 * 126 + 1:16], t == 0, t == 31,
                        [(f"W1_{kv}", u)], [phk])
                tm, tk = tmp.next()
                S.op("vector", lambda e, tm=tm, ph=ph, c2=c2, pbias=pbias: e.tensor_scalar(tm[:, 0, 0:127], ph[:, 0:127], pbias[:, c2:c2 + 1], None, ALU.add),
                     reads=[phk, f"pbias_{kv}"], writes=[(tk, 0)])
                S.op("vector", lambda e, tm=tm: e.tensor_tensor(tm[:, 1, 0:127], tm[:, 0, 0:127], tm[:, 0, 0:127], ALU.mult),
                     reads=[(tk, 0)], writes=[(tk, 1)])
                S.op("vector", lambda e, tm=tm: e.tensor_scalar(tm[:, 1, 0:127], tm[:, 1, 0:127], 0.044715, 1.0, ALU.mult, ALU.add),
                     reads=[(tk, 1)], writes=[(tk, 1)])
                S.op("vector", lambda e, tm=tm: e.tensor_tensor(tm[:, 2, 0:127], tm[:, 1, 0:127], tm[:, 0, 0:127], ALU.mult),
                     reads=[(tk, 1), (tk, 0)], writes=[(tk, 2)])
                S.op("scalar", lambda e, tm=tm: e.activation(tm[:, 3, 0:127], tm[:, 2, 0:127], AF.Sigmoid, scale=GELU_C),
                     reads=[(tk, 2)], writes=[(tk, 3)])
                S.op("vector", lambda e, tm=tm, gl=gl, c2=c2: e.tensor_tensor(gl[:, c2, 0:127], tm[:, 3, 0:127], tm[:, 0, 0:127], ALU.mult),
                     reads=[(tk, 3), (tk, 0)], writes=[(glk, c2)])
            po, pok = pmisc.next()
            if kv == 0:
                for c2 in range(2):
                    _mm(S, po[:, 0:127], W2[:, c2, :], gl[:, c2, 0:127], c2 == 0, c2 == 1,
                        [(f"W2_{kv}", 0), (f"W2_{kv}", 1), (glk, c2)], [pok])
                S.op("scalar", lambda e, po=po, prt=prt, j=j: e.copy(A["KcmpT"][prt, j, 0:127], po[prt, 0:127]),
                     reads=[pok], writes=[("KcmpT", g)])
            else:
                for c2 in range(2):
                    _mm(S, po[0:127, 0:64], gl[:, c2, 0:127], W2[:, c2, 0:64], c2 == 0, c2 == 1,
                        [(f"W2_{kv}", 0), (glk, c2)], [pok])
                S.op("scalar", lambda e, po=po, g=g: e.copy(A["Vcmp"][0:127, g, :], po[0:127, 0:64]),
                     reads=[pok], writes=[("Vcmp", g)])
    S.pop()


def run_attention(S, c, R, jobs):
    flat = []
    for jb in jobs:
        n = len(jb["items"])
        for m in range(n):
            flat.append((jb, m, n))

    gens = [jb for jb in jobs if jb.get("pregen")]
    pending = []

    def step(g_):
        try:
            next(g_)
            return True
        except StopIteration:
            return False

    def qk(ent):
        jb, m, n = ent
        if m == 0:
            if jb.get("pre"):
                jb["pre"]()
            if jb.get("pregen"):
                if "gen" not in jb:
                    jb["gen"] = jb["pregen"]()
                while step(jb["gen"]):
                    pass
                if jb["gen"] in pending:
                    pending.remove(jb["gen"])
                k = gens.index(jb)
                if k + 1 < len(gens):
                    nj = gens[k + 1]
                    nj["gen"] = nj["pregen"]()
                    pending.append(nj["gen"])
            jb["O"] = R["O"].next()
        elif pending:
            if not step(pending[0]):
                pending.pop(0)
        kT, bias, mask, v = jb["items"][m]
        ST, sk = R["ST"].next()
        _mm(S, ST[:], kT, jb["q"], True, False, jb.get("qk_reads", []), [sk])
        _mm(S, ST[:], c.identb[:], bias, False, mask is None, ["identb"] + jb.get("bias_reads", []), [sk])
        if mask is not None:
            _mm(S, ST[:], mask[0], mask[1], False, True, [mask[2]], [sk])
        return ST, sk

    LA = R["ST"].n - 1
    issued = []
    nis = 0
    for idx, ent in enumerate(flat):
        jb, m, n = ent
        while nis < len(flat) and nis <= idx + LA - 1:
            issued.append(qk(flat[nis]))
            nis += 1
        ST, sk = issued.pop(0)
        PT, pk = R["PT"].next()
        S.op("scalar", lambda e, PT=PT, ST=ST: e.activation(PT[:], ST[:], AF.Exp), reads=[sk], writes=[pk])
        O, ok = jb["O"]
        v = jb["items"][m][3]
        for r in range(4):
            _mm(S, O[:, r, 0:65], PT[:, r, :], v, m == 0 and r == 0, m == n - 1, [pk] + jb.get("v_reads", []), [ok], sgc=True)
        if m == n - 1:
            jb["done"](O, ok)


def l0_attention(S, c, A, b):
    S.push()
    R = dict(ST=Ring(S, "aST", [128, 4, 128], F32, 3, psum=True), O=Ring(S, "aO", [128, 4, 128], F32, 3, psum=True),
             PT=Ring(S, "aPT", [128, 4, 128], BF16, 3))
    cmpS = S.ps("cmpS", [128, 4, 128], F32)
    misc = S.ps("amisc", [128, 512], F32)
    tpB = cmpS[:].bitcast(BF16)
    S.psum_keys.update(["cmpS", "amisc"])
    MS = S.sb("MS", [128, 4, 2048], BF16)
    MW = S.sb("MW", [128, 4, 384], BF16)
    MC = S.sb("MC", [128, 4, 247], F32)
    Sc = Ring(S, "aSc", [128, 4, 127], F32, 2)
    Pn = Ring(S, "aPn", [128, 4, 127], F32, 2)
    Pnb = Ring(S, "aPnb", [128, 4, 128], BF16, 2)
    sm = Ring(S, "asm", [128, 64], F32, 2)
    Ps = Ring(S, "aPs", [128, 128], F32, 2)
    PsT = Ring(S, "aPsT", [128, 128], F32, 2)
    scb = Ring(S, "asc", [128, 2, 32], F32, 2)
    NMT = Ring(S, "aNMT", [32, 128], BF16, 2)
    PcT = Ring(S, "aPcT", [128, 4, 128], BF16, 2)
    cf = Ring(S, "acf", [128, 4, 4, 1], F32, 2)
    acc = Ring(S, "aacc", [128, 3, 4, 64], F32, 2)
    ocs = Ring(S, "aocs", [128, 4, 64], F32, 3)
    for g in range(4):
        u, j = g // 2, g % 2
        prt = slice(u * 64, (u + 1) * 64)
        for r in range(4):
            h = 4 * g + r
            S.dma("sync", MS[:, r, :], c.ms_b[h], reads=[], writes=[("MS", r)])
            S.dma("sync", MW[:, r, :], c.mw_b[h], reads=[], writes=[("MW", r)])
            S.dma("sync", MC[:, r, :], c.mc[h], reads=[], writes=[("MC", r)])
        MSk = [("MS", r) for r in range(4)]
        MWk = [("MW", r) for r in range(4)]
        MCk = [("MC", r) for r in range(4)]
        jobs = []
        for i in range(NT):
            qs = slice(i * 128, (i + 1) * 128)
            st = {}

            def pre(i=i, qs=qs, st=st):
                off = 120 - 8 * i
                for r in range(4):
                    _mm(S, cmpS[:, r, 0:127], A["QT"][prt, j, r, qs], A["KcmpT"][prt, j, 0:127], True, True, [], ["cmpS"])
                sc_, sck = Sc.next()
                S.op("vector", lambda e: e.tensor_tensor(sc_[:], cmpS[:, :, 0:127], MC[:, :, off:off + 127], ALU.add),
                     reads=["cmpS"] + MCk, writes=[sck])
                S.op("scalar", lambda e: e.activation(sc_[:], sc_[:], AF.Exp), reads=[sck], writes=[sck])
                s_, smk = sm.next()
                S.op("vector", lambda e: e.tensor_reduce(s_[:, 0:4], sc_[:], AX.X, ALU.add), reads=[sck], writes=[(smk, 0)])
                S.op("vector", lambda e: e.tensor_scalar_max(s_[:, 0:4], s_[:, 0:4], 1e-30), reads=[(smk, 0)], writes=[(smk, 0)])
                S.op("vector", lambda e: e.reciprocal(s_[:, 4:8], s_[:, 0:4]), reads=[(smk, 0)], writes=[(smk, 1)])
                pn, pnk = Pn.next()
                S.op("vector", lambda e: e.tensor_tensor(pn[:], sc_[:], s_[:, 4:8].unsqueeze(2).to_broadcast([128, 4, 127]), ALU.mult),
                     reads=[sck, (smk, 1)], writes=[pnk])
                pb, pbk = Pnb.next()
                S.op("scalar", lambda e: e.copy(pb[:, :, 0:127], pn[:]), reads=[pnk], writes=[pbk])
                ps_, psk = Ps.next()
                S.op("vector", lambda e: e.tensor_reduce(ps_[:, 0:127], pn[:].rearrange("p r n -> p n r"), AX.X, ALU.add),
                     reads=[pnk], writes=[psk])
                yield
                _tr(S, misc[0:127, 0:128], ps_[:, 0:127], c.identf[:], [psk, "identf"], ["amisc"])
                pt_, ptk = PsT.next()
                S.op("scalar", lambda e: e.copy(pt_[0:127, :], misc[0:127, 0:128]), reads=["amisc"], writes=[ptk])
                yield
                _mm(S, misc[:, 128:160], pt_[0:127, :], c.ovl[0:127, :], True, True, [ptk, "ovl"], ["amisc"])
                sb_, sbk = scb.next()
                S.op("vector", lambda e: e.tensor_tensor(sb_[:, 0, :], misc[:, 128:160], c.vtab[:, i, :], ALU.mult),
                     reads=["amisc", "vtab"], writes=[(sbk, 0)])
                S.op("vector", lambda e: e.tensor_tensor(sb_[:, 0, :], sb_[:, 0, :], c.ctab[:, i, :], ALU.add),
                     reads=[(sbk, 0), "ctab"], writes=[(sbk, 0)])
                S.op("vector", lambda e: e.max(out=s_[:, 8:16], in_=sb_[:, 0, :]), reads=[(sbk, 0)], writes=[(smk, 2)])
                S.op("vector", lambda e: e.tensor_scalar(sb_[:, 1, :], sb_[:, 0, :], s_[:, 15:16], MASKV, ALU.is_lt, ALU.mult),
                     reads=[(sbk, 0), (smk, 2)], writes=[(sbk, 1)])
                yield
                _tr(S, misc[0:32, 0:128], sb_[:, 1, :], c.identf[:], [(sbk, 1), "identf"], ["amisc"])
                nmt, nmk = NMT.next()
                S.op("scalar", lambda e: e.copy(nmt[:], misc[0:32, 0:128]), reads=["amisc"], writes=[nmk])
                st["nmt"] = (nmt, nmk)
                yield
                for r in range(4):
                    _tr(S, tpB[0:127, r, 0:128], pb[:, r, 0:127], c.identb[:], [pbk, "identb"], ["cmpS"])
                pc, pck = PcT.next()
                S.op("vector", lambda e: e.tensor_copy(pc[0:127], tpB[0:127, :, 0:128]), reads=["cmpS"], writes=[pck])
                yield
                for r in range(4):
                    _mm(S, misc[:, 160 + 64 * r:224 + 64 * r], pc[0:127, r, :], A["Vcmp"][0:127, g, :], True, True, [pck], ["amisc"])
                oc_, ock = ocs.next()
                S.op("scalar", lambda e: e.copy(oc_[:], misc[:, 160:416].rearrange("p (r d) -> p r d", r=4)), reads=["amisc"], writes=[ock])
                st["oc"] = (oc_, ock)

            qrhs = A["QT"][prt, j, :, qs]
            wj = [jj for jj in (i - 2, i - 1, i) if jj >= 0]
            witems = [(A["KwT"][prt, j, jj * 128:(jj + 1) * 128], MW[:, :, (i - jj) * 128:(i - jj + 1) * 128], None,
                       A["Vw"][:, jj, g, :]) for jj in wj]
            jobs.append(dict(q=qrhs, items=witems, pregen=pre, bias_reads=MWk,
                             done=(lambda O, ok, st=st: st.__setitem__("Ow", (O, ok)))))

            class _LazyItems(list):
                pass

            sitems = []
            for jj in range(i + 1):
                sitems.append([A["KsT"][prt, j, jj * 128:(jj + 1) * 128], MS[:, :, (i - jj) * 128:(i - jj + 1) * 128],
                               ("lazy", jj), A["Vs"][:, jj, g, :]])

            def done(Os, osk, i=i, st=st):
                Ow, owk = st["Ow"]
                cf_, cfk = cf.next()
                a_, ak = acc.next()
                Gv = A["G"][:, i, g * 12:(g + 1) * 12].rearrange("p (r k) -> p r k", k=3)
                S.op("vector", lambda e: e.reciprocal(cf_[:, 0, :, :], Os[:, :, 64:65]), reads=[osk], writes=[(cfk, 0)])
                S.op("vector", lambda e: e.reciprocal(cf_[:, 1, :, :], Ow[:, :, 64:65]), reads=[owk], writes=[(cfk, 1)])
                S.op("vector", lambda e: e.tensor_tensor(cf_[:, 2, :, :], cf_[:, 0, :, :], Gv[:, :, 1:2], ALU.mult),
                     reads=[(cfk, 0)], writes=[(cfk, 2)])
                S.op("vector", lambda e: e.tensor_tensor(cf_[:, 3, :, :], cf_[:, 1, :, :], Gv[:, :, 2:3], ALU.mult),
                     reads=[(cfk, 1)], writes=[(cfk, 3)])
                oc_, ock = st["oc"]
                S.op("vector", lambda e: e.tensor_tensor(a_[:, 0], oc_[:], Gv[:, :, 0:1].to_broadcast([128, 4, 64]), ALU.mult),
                     reads=[ock], writes=[(ak, 0)])
                S.op("vector", lambda e: e.tensor_tensor(a_[:, 1], Os[:, :, 0:64], cf_[:, 2, :, :].to_broadcast([128, 4, 64]), ALU.mult),
                     reads=[osk, (cfk, 2)], writes=[(ak, 1)])
                S.op("vector", lambda e: e.tensor_tensor(a_[:, 2], Ow[:, :, 0:64], cf_[:, 3, :, :].to_broadcast([128, 4, 64]), ALU.mult),
                     reads=[owk, (cfk, 3)], writes=[(ak, 2)])
                S.op("gpsimd", lambda e: e.tensor_tensor(a_[:, 0], a_[:, 0], a_[:, 1], ALU.add), reads=[(ak, 0), (ak, 1)], writes=[(ak, 0)])
                odst = A["Oall"][:, i, g * 256:(g + 1) * 256].rearrange("p (r d) -> p r d", r=4)
                S.op("gpsimd", lambda e: e.tensor_tensor(odst, a_[:, 0], a_[:, 2], ALU.add), reads=[(ak, 0), (ak, 2)], writes=[("Oall", i, odst.offset)])

            jobs.append(dict(q=qrhs, items=sitems, bias_reads=MSk, done=done, lazy=st))
        for jb in jobs:
            if "lazy" in jb:
                jb["items"] = _LazyMaskItems(jb["items"], jb["lazy"], c)
                jb["mask_reads_fn"] = True
        run_attention(S, c, R, jobs)
    S.pop()


class _LazyMaskItems:
    def __init__(self, items, st, c):
        self.items = items
        self.st = st
        self.c = c

    def __len__(self):
        return len(self.items)

    def __getitem__(self, m):
        kT, bias, mask, v = self.items[m]
        jj = mask[1]
        nmt, nmk = self.st["nmt"]
        return (kT, bias, (self.c.indb[:, jj, :], nmt[:, :].unsqueeze(1).to_broadcast([32, 4, 128]), nmk), v)


def out_proj(S, c, w_b, get_o, xin, xout, xin_keys, xout_key):
    S.push()
    Wo = S.sb("Wo", [128, 8, DM], BF16)
    for k in range(8):
        S.dma("sync", Wo[:, k, :], w_b[k * 128:(k + 1) * 128, :], reads=[], writes=[("Wo", k)])
    ptr = Ring(S, "opptr", [128, 8, 128], BF16, 2, psum=True)
    oT = Ring(S, "opoT", [128, 8, 128], BF16, 2)
    po = Ring(S, "oppo", [128, 512], F32, 4, psum=True)
    xr = Ring(S, "opx", [128, DM], F32, 3)
    W = {}
    setup = get_o(None, W)
    for i in range(NT):
        xt, xk = xr.next()
        S.dma("sync", xt[:], xin(i), reads=xin_keys(i), writes=[xk])
        ob, obk = get_o(i, W)
        pt, pk = ptr.next()
        for k in range(8):
            _tr(S, pt[:, k, :], ob[:, k * 128:(k + 1) * 128], c.identb[:], obk + ["identb"], [pk])
        ot, otk = oT.next()
        S.op("scalar", lambda e, ot=ot, pt=pt: e.copy(ot[:], pt[:]), reads=[pk], writes=[otk])
        for n in range(2):
            p_, pok = po.next()
            for k in range(8):
                _mm(S, p_[:], ot[:, k, :], Wo[:, k, n * 512:(n + 1) * 512], k == 0, k == 7, [otk, ("Wo", k)], [pok])
            S.op("vector", lambda e, xt=xt, p_=p_, n=n: e.tensor_tensor(xt[:, n * 512:(n + 1) * 512], xt[:, n * 512:(n + 1) * 512], p_[:], ALU.add),
                 reads=[pok, xk], writes=[xk])
        S.dma("gpsimd", xout(i), xt[:], reads=[xk], writes=[xout_key(i)])
    S.pop()


def ffn(S, c, wup_b, wdn_b, gain_row, xin, xout, xin_keys, xout_key, final_row=None):
    S.push()
    Wup = S.sb("Wup", [128, 8, 2 * FF], BF16)
    Wdn = S.sb("Wdn", [128, NFC, DM], BF16)
    for k in range(8):
        S.dma("sync", Wup[:, k, :], wup_b[k * 128:(k + 1) * 128, :], reads=[], writes=[("Wup", k)])
    for cc in range(NFC):
        S.dma("sync", Wdn[:, cc, :], wdn_b[cc * 128:(cc + 1) * 128, :], reads=[], writes=[("Wdn", cc)])
    gt = load_gain(S, c, "gffn", gain_row)
    gf = load_gain(S, c, "gfin", final_row) if final_row is not None else None
    W = dict(ss=Ring(S, "fss", [128, 4], F32, 4), hb=Ring(S, "fhb", [128, DM], BF16, 2),
             junk=S.sb("fjunk", [128, DM], BF16), ptr=Ring(S, "fptr", [128, 8, 128], BF16, 2, psum=True))
    xring = Ring(S, "fx", [128, DM], F32, 3)
    hT = Ring(S, "fhT", [128, 8, 256], BF16, 2)
    aT = Ring(S, "faT", [128, NFC, 256], BF16, 2)
    pa = Ring(S, "fpa", [128, 512], F32, 4, psum=True)
    po = Ring(S, "fpo", [128, 512], F32, 2, psum=True)
    sg = Ring(S, "fsg", [128, 256], F32, 3)
    nev = 0
    for mt in range(NT // 2):
        hTt, hTk = hT.next()
        xts = []
        for t in range(2):
            i = mt * 2 + t
            xt, xk, st, sk, outs = norm_tile(S, c, W, xin(i), xin_keys(i), [gt], xring)
            transpose_tile(S, c, W, outs[0][0], outs[0][1], hTt[:, :, t * 128:(t + 1) * 128], (hTk, t),
                           eng="scalar" if t == 0 else "vector")
            xts.append((xt, xk, st, sk))
        hk2 = [(hTk, 0), (hTk, 1)]
        at, atk = aT.next()
        for cc in range(NFC):
            p1, p1k = pa.next()
            p2, p2k = pa.next()
            for k in range(8):
                _mm(S, p1[:, 0:256], Wup[:, k, cc * 128:(cc + 1) * 128], hTt[:, k, :], k == 0, k == 7, [("Wup", k)] + hk2, [p1k])
            for k in range(8):
                _mm(S, p2[:, 0:256], Wup[:, k, FF + cc * 128:FF + (cc + 1) * 128], hTt[:, k, :], k == 0, k == 7, [("Wup", k)] + hk2, [p2k])
            s_, s_k = sg.next()
            S.op("scalar", lambda e, s_=s_, p1=p1: e.activation(s_[:], p1[:, 0:256], AF.Silu), reads=[p1k], writes=[s_k])
            S.op("vector", lambda e, at=at, cc=cc, s_=s_, p2=p2: e.tensor_tensor(at[:, cc, :], s_[:], p2[:, 0:256], ALU.mult),
                 reads=[s_k, p2k], writes=[(atk, cc)])
        for t in range(2):
            i = mt * 2 + t
            xt, xk, st, sk = xts[t]
            for n in range(2):
                p_, pok = po.next()
                for cc in range(NFC):
                    _mm(S, p_[:], at[:, cc, t * 128:(t + 1) * 128], Wdn[:, cc, n * 512:(n + 1) * 512], cc == 0, cc == NFC - 1,
                        [(atk, cc), ("Wdn", cc)], [pok])
                S.op("vector", lambda e, xt=xt, p_=p_, n=n: e.tensor_tensor(xt[:, n * 512:(n + 1) * 512], xt[:, n * 512:(n + 1) * 512], p_[:], ALU.add),
                     reads=[pok, xk], writes=[xk])
            if gf is not None:
                S.op("gpsimd", lambda e, st=st: e.memset(st[:], 0.0), reads=[sk], writes=[sk])
                S.op("scalar", lambda e, xt=xt, st=st: e.activation(W["junk"][:], xt[:], AF.Square, accum_out=st[:, 0:1]),
                     reads=[xk, sk], writes=["junk", sk])
                S.op("vector", lambda e, st=st: e.tensor_scalar(st[:, 1:2], st[:, 0:1], 1.0 / DM, 1e-6, ALU.mult, ALU.add), reads=[sk], writes=[sk])
                S.op("scalar", lambda e, st=st: e.activation(st[:, 3:4], st[:, 1:2], AF.Sqrt), reads=[sk], writes=[sk])
                S.op("vector", lambda e, st=st: e.reciprocal(st[:, 2:3], st[:, 3:4]), reads=[sk], writes=[sk])
                S.op("vector", lambda e, xt=xt, st=st: e.scalar_tensor_tensor(xt[:], xt[:], st[:, 2:3], gf[0][:], ALU.mult, ALU.mult),
                     reads=[xk, sk, gf[1]], writes=[xk])
            S.dma("gpsimd", xout(i), xt[:], reads=[xk], writes=[xout_key(i)])
    S.pop()


def l1_norm(S, c, Lb, xin, xin_keys):
    S.push()
    g1 = load_gain(S, c, "gmix1", 2)
    gk = load_gain(S, c, "gkv", 3)
    W = norm_work(S, "p6")
    xring = Ring(S, "p6x", [128, DM], F32, 3)
    for i in range(NT):
        xt, xk, st, sk, outs = norm_tile(S, c, W, xin(i), xin_keys(i), [g1, gk], xring)
        transpose_tile(S, c, W, outs[0][0], outs[0][1], Lb["HT"][:, :, i * 128:(i + 1) * 128], ("HT", i), eng="scalar")
        transpose_tile(S, c, W, outs[1][0], outs[1][1], Lb["HKT"][:, :, i * 128:(i + 1) * 128], ("HKT", i), eng="vector")
    S.pop()


def l1_group(S, c, Lb, gi):
    win, d = DILS[gi]
    L = SL // d
    tpc = L // 128
    S.push()
    Wq = S.sb("Wq", [128, 8, DM], BF16)
    Wkv = S.sb("Wkv", [128, 8, 512], BF16)
    for k in range(8):
        S.dma("sync", Wq[:, k, :], c.w_q_b[k * 128:(k + 1) * 128, gi * DM:(gi + 1) * DM], reads=[], writes=[("Wq", k)])
        S.dma("sync", Wkv[:, k, :], c.kv_w_b[k * 128:(k + 1) * 128, gi * 512:(gi + 1) * 512], reads=[], writes=[("Wkv", k)])
    MD = S.sb("MD", [128, 16, 256], BF16)
    for h in range(16):
        S.dma("sync", MD[:, h, :], c.md_b[gi * 16 + h], reads=[], writes=[("MD", h)])
    QT = S.sb("QT1", [128, 2, 4, SL], BF16)
    KT = S.sb("KT1", [128, 2, SL], BF16)
    Vp = S.sb("Vp1", [128, NT, 4, 65], BF16)
    S.op("gpsimd", lambda e: e.memset(Vp[:, :, :, 64:65], 1.0), writes=["Vp1_1"])
    pfm = Ring(S, "gpfm", [128, 512], F32, 2, psum=True)
    ptm = Ring(S, "gptm", [128, 512], F32, 1, psum=True)
    nev = 0
    for mt in range(4):
        ts = slice(mt * 512, (mt + 1) * 512)
        l0, ln = mt * 512 // d, 512 // d
        for fc in range(10):
            pt, pk = pfm.next()
            for k in range(8):
                if fc < 8:
                    _mm(S, pt[:], Wq[:, k, fc * 128:(fc + 1) * 128], Lb["HT"][:, k, ts], k == 0, k == 7, [("Wq", k)], [pk])
                else:
                    _mm(S, pt[:], Wkv[:, k, (fc - 8) * 128:(fc - 7) * 128], Lb["HKT"][:, k, ts], k == 0, k == 7, [("Wkv", k)], [pk])
            if fc < 8:
                dst = QT[:, fc // 4, fc % 4, :]
                sc = 0.125
            else:
                dst = KT[:, fc - 8, :]
                sc = 1.0
            dst = dst.rearrange("p (c l) -> p c l", c=d)[:, :, l0:l0 + ln]
            src = pt[:].rearrange("p (l c) -> p c l", c=d)
            if nev % 2 == 0:
                S.op("scalar", lambda e, dst=dst, src=src, sc=sc: e.mul(dst, src, sc), reads=[pk], writes=[("fm1", fc, mt)])
            else:
                S.op("vector", lambda e, dst=dst, src=src, sc=sc: e.tensor_scalar(dst, src, sc, None, ALU.mult), reads=[pk], writes=[("fm1", fc, mt)])
            nev += 1
    for T in range(NT):
        cls, lb = T // tpc, T % tpc
        start = cls + d * 128 * lb
        pt, pk = ptm.next()
        for k in range(8):
            _mm(S, pt[:, 0:256], Lb["HKT"][:, k, start:start + d * 127 + 1:d], Wkv[:, k, 256:512], k == 0, k == 7, [("Wkv", k)], [pk])
        S.op("vector" if T % 2 else "scalar",
             (lambda e, pt=pt, T=T: e.tensor_copy(Vp[:, T, :, 0:64], pt[:, 0:256].rearrange("p (g e) -> p g e", g=4))) if T % 2 else
             (lambda e, pt=pt, T=T: e.copy(Vp[:, T, :, 0:64], pt[:, 0:256].rearrange("p (g e) -> p g e", g=4))),
             reads=[pk, "Vp1_1"], writes=[("Vp1", T)])
    R = dict(ST=Ring(S, "gST", [128, 4, 128], F32, 2, psum=True), O=Ring(S, "gO", [128, 4, 128], F32, 3, psum=True),
             PT=Ring(S, "gPT", [128, 4, 128], BF16, 3))
    Ost = Ring(S, "gOst", [128, 16, 65], F32, 2)
    MDk = [("MD", h) for h in range(16)]
    allfm = [("fm1", fc, mt) for fc in range(10) for mt in range(4)]
    odv = c.od[gi].rearrange("(l c) f -> c l f", c=d)
    jobs = []
    for T in range(NT):
        cls, lb = T // tpc, T % tpc
        st = {}
        for g in range(4):
            u, j = g // 2, g % 2
            prt = slice(u * 64, (u + 1) * 64)
            items = []
            for TT in ([T - 1] if lb > 0 else []) + [T]:
                dl = T - TT
                items.append((KT[prt, j, TT * 128:(TT + 1) * 128], MD[:, 4 * g:4 * g + 4, dl * 128:(dl + 1) * 128], None, Vp[:, TT, g, :]))

            def pre(st=st):
                st["ost"] = Ost.next()

            def done(O, ok, T=T, g=g, st=st, cls=cls, lb=lb):
                ost, ostk = st["ost"]
                if g % 2 == 0:
                    S.op("vector", lambda e: e.tensor_copy(ost[:, 4 * g:4 * g + 4, :], O[:, :, 0:65]), reads=[ok], writes=[(ostk, g)])
                else:
                    S.op("scalar", lambda e: e.copy(ost[:, 4 * g:4 * g + 4, :], O[:, :, 0:65]), reads=[ok], writes=[(ostk, g)])
                if g == 3:
                    S.dma("gpsimd", odv[cls, lb * 128:(lb + 1) * 128, :], ost[:].rearrange("p h e -> p (h e)"),
                          reads=[(ostk, gg) for gg in range(4)], writes=[("od", gi, T)])

            jobs.append(dict(q=QT[prt, j, :, T * 128:(T + 1) * 128], items=items, pre=pre if g == 0 else None, done=done,
                             qk_reads=allfm, bias_reads=MDk, v_reads=[("Vp1", TT) for TT in range(NT)]))
    run_attention(S, c, R, jobs)
    S.pop()


def l1_merge_get_o(S, c):
    def get_o(i, W):
        if i is None:
            W["ld"] = Ring(S, "mld", [128, 16, 65], F32, 4)
            W["sm"] = Ring(S, "msm", [128, 16, 1], F32, 2)
            W["ob"] = Ring(S, "mob", [128, DM], BF16, 2)
            return
        ts = []
        for gi in range(3):
            t, k = W["ld"].next()
            S.dma("sync", t[:].rearrange("p h e -> p (h e)"), c.od[gi][i * 128:(i + 1) * 128, :],
                  reads=[("od", gi, T) for T in range(NT)], writes=[k])
            ts.append((t, k))
        (t0, k0), (t1, k1), (t2, k2) = ts
        S.op("gpsimd", lambda e: e.tensor_tensor(t0[:], t0[:], t1[:], ALU.add), reads=[k0, k1], writes=[k0])
        S.op("vector", lambda e: e.tensor_tensor(t0[:], t0[:], t2[:], ALU.add), reads=[k0, k2], writes=[k0])
        sm, smk = W["sm"].next()
        S.op("vector", lambda e: e.reciprocal(sm[:], t0[:, :, 64:65]), reads=[k0], writes=[smk])
        ob, obk = W["ob"].next()
        S.op("vector", lambda e: e.tensor_tensor(ob[:].rearrange("p (h e) -> p h e", h=16), t0[:, :, 0:64],
                                                 sm[:].to_broadcast([128, 16, 64]), ALU.mult), reads=[k0, smk], writes=[obk])
        return ob, [obk]
    return get_o


INPUT_SHAPES = dict(
    x=[NSEQ, SL, DM], w_in=[DM, 2608], w1k=[2048, 256], w1v=[2048, 256], w2k=[256, 64], w2v=[256, 64],
    pekT=[64, 32], pevT=[64, 32], w_out0=[DM, DM], w_out1=[DM, DM], w_up0=[DM, 2 * FF], w_up1=[DM, 2 * FF],
    w_dn0=[FF, DM], w_dn1=[FF, DM], kv_w=[DM, 1536], w_q=[DM, 3072], gains=[6, DM], b_gate=[1, 48],
    ms=[16, 128, 2048], mw=[16, 128, 384], mc=[16, 128, 247], md=[48, 128, 256],
    vtab=[128, NT, 32], ctab=[128, NT, 32], ovl=[127, 32], ind=[32, NT, 128],
)
CONV = ("w_in", "w1k", "w1v", "w2k", "w2v", "w_out0", "w_out1", "w_up0", "w_up1", "w_dn0", "w_dn1", "kv_w", "w_q",
        "ms", "mw", "md")


def build_program(nseq=NSEQ, max_phase=10 ** 9):
    nc = bass.Bass("TRN2", target_bir_lowering=False)
    c = Ctx()
    shapes = dict(INPUT_SHAPES)
    shapes["x"] = [nseq, SL, DM]
    for name, shp in shapes.items():
        setattr(c, name, nc.dram_tensor(name, list(shp), F32, kind="ExternalInput").ap())
    y = nc.dram_tensor("y", [nseq, SL, DM], F32, kind="ExternalOutput").ap()
    c.conv_jobs = []
    conv_only = os.environ.get("CONV_ONLY")
    for name in CONV:
        shp = shapes[name]
        dst = nc.dram_tensor(name + "_b", list(shp), BF16).ap()
        setattr(c, name + "_b", dst)
        src = getattr(c, name)
        if conv_only is not None and name not in conv_only.split(","):
            continue
        if len(shp) == 3:
            c.conv_jobs.append((src.rearrange("h p c -> (h p) c"), dst.rearrange("h p c -> (h p) c"), name + "_b"))
        else:
            c.conv_jobs.append((src, dst, name + "_b"))
    c.xres = nc.dram_tensor("xres", [SL, DM], F32).ap()
    c.od = nc.dram_tensor("od", [3, SL, 16 * 65], F32).ap()

    S = Sched(nc)
    c.identf = S.sb("identf", [128, 128], F32)
    c.identb = S.sb("identb", [128, 128], BF16)
    S.op("gpsimd", lambda e: e.memset(c.identf[:], 1.0), writes=["identf"])
    S.op("gpsimd", lambda e: e.affine_select(c.identf[:], c.identf[:], pattern=[[-1, 128]], compare_op=ALU.is_equal,
                                             fill=0.0, base=0, channel_multiplier=1), reads=["identf"], writes=["identf"])
    S.op("vector", lambda e: e.tensor_copy(c.identb[:], c.identf[:]), reads=["identf"], writes=["identb"])
    c.indb = S.sb("indb", [32, NT, 128], BF16)
    for nm, shp in (("vtab", [128, NT, 32]), ("ctab", [128, NT, 32]), ("ovl", [127, 32])):
        t = S.sb(nm + "_s", shp, F32)
        S.dma("sync", t[:], getattr(c, nm), reads=[], writes=[nm])
        setattr(c, nm, t)
    S.push()
    indf = S.sb("indf", [32, NT, 128], F32)
    S.dma("sync", indf[:], c.ind, reads=[], writes=["indf"])
    S.op("vector", lambda e: e.tensor_copy(c.indb[:], indf[:]), reads=["indf"], writes=["indb"])
    S.pop()
    convert_all(S, c)

    cnt = [0]

    def ph(fn, *a, **k):
        cnt[0] += 1
        if cnt[0] <= max_phase:
            fn(*a, **k)

    xres_in = lambda i: c.xres[i * 128:(i + 1) * 128, :]
    xres_k = lambda i: [("xres", i)]
    xres_k1 = lambda i: ("xres", i)
    for b in range(nseq):
        S.push()
        A = dict(QT=S.sb("QT", [128, 2, 4, SL], BF16), KsT=S.sb("KsT", [128, 2, SL], BF16), KwT=S.sb("KwT", [128, 2, SL], BF16),
                 Vs=S.sb("Vs", [128, NT, 4, 65], BF16), Vw=S.sb("Vw", [128, NT, 4, 65], BF16), G=S.sb("G", [128, NT, 48], F32),
                 KcmpT=S.sb("KcmpT", [128, 2, 128], BF16), Vcmp=S.sb("Vcmp", [128, 4, 64], BF16))
        S.push()
        Bk = dict(KcT=S.sb("KcT", [128, 2, SL], BF16), VcT=S.sb("VcT", [128, 2, SL], BF16))
        ph(l0_proj, S, c, A, Bk, b)
        ph(l0_compress, S, c, A, Bk)
        S.pop()
        S.push()
        A["Oall"] = S.sb("Oall", [128, NT, DM], BF16)
        ph(l0_attention, S, c, A, b)
        if os.environ.get("DBG_DUMP") == "Oall":
            S.push()
            dr = Ring(S, "dbgd", [128, DM], F32, 2)
            for i in range(NT):
                t_, k_ = dr.next()
                S.op("vector", lambda e, t_=t_, i=i: e.tensor_copy(t_[:], A["Oall"][:, i, :]), reads=[], writes=[k_])
                S.dma("sync", y[0, i * 128:(i + 1) * 128, :], t_[:], reads=[k_], writes=[("ydump", i)])
            S.pop()
        ph(out_proj, S, c, c.w_out0_b, lambda i, W: (A["Oall"][:, i, :], []) if i is not None else None,
                 lambda i: c.x[b, i * 128:(i + 1) * 128, :], xres_in, lambda i: [], xres_k1)
        S.pop()
        S.pop()
        ph(ffn, S, c, c.w_up0_b, c.w_dn0_b, 1, xres_in, xres_in, xres_k, xres_k1)
        S.push()
        Lb = dict(HT=S.sb("HT", [128, 8, SL], BF16), HKT=S.sb("HKT", [128, 8, SL], BF16))
        ph(l1_norm, S, c, Lb, xres_in, xres_k)
        for gi in range(3):
            ph(l1_group, S, c, Lb, gi)
        S.pop()
        ph(out_proj, S, c, c.w_out1_b, l1_merge_get_o(S, c), xres_in, xres_in, xres_k, xres_k1)
        ph(ffn, S, c, c.w_up1_b, c.w_dn1_b, 4, xres_in, lambda i: y[b, i * 128:(i + 1) * 128, :], xres_k, lambda i: ("y", b, i),
            final_row=5)
    if os.environ.get("DBG_DUMP"):
        S.finish(final_keys=[])
    elif max_phase < 10 ** 9:
        S.dma("sync", y[0], c.xres, reads=[("xres", i) for i in range(NT)], writes=["ydbg"])
        S.finish(final_keys=["ydbg"])
    else:
        S.finish(final_keys=[("y", b, i) for b in range(nseq) for i in range(NT)])
    return nc


def _t5_bucket(dist):
    d = np.maximum(dist, 0)
    lr = np.log(np.maximum(d, 16).astype(np.float32) / np.float32(16)) / np.float32(np.log(2048.0 / 16.0))
    large = 16 + (lr * np.float32(16)).astype(np.int32)
    return np.where(d < 16, d, np.minimum(large, 31)).astype(np.int64)


def _qperm():
    cols = []
    for j in range(2):
        for r in range(4):
            for g in (j, 2 + j):
                h = 4 * g + r
                cols.extend(range(h * 64, h * 64 + 64))
    return np.array(cols)


def _kperm():
    cols = []
    for j in range(2):
        for g in (j, 2 + j):
            cols.extend(range(g * 64, g * 64 + 64))
    return np.array(cols)


def host_prep(inp):
    f = lambda a: np.ascontiguousarray(np.asarray(a, dtype=np.float32))
    qp, kp = _qperm(), _kperm()
    nat = np.arange(256)
    w_in = f(inp["a_w_in"])[0]
    col = np.concatenate([qp, 1024 + kp, 1280 + kp, 1536 + kp, 2048 + kp, 1792 + nat, 2304 + nat, 2560 + np.arange(48)])
    sh = {}
    sh["w_in"] = f(w_in[:, col])
    sh["w1k"] = f(inp["a_w1_k"])[0]
    sh["w1v"] = f(inp["a_w1_v"])[0]
    sh["w2k"] = f(inp["a_w2_k"])[0]
    sh["w2v"] = f(inp["a_w2_v"])[0]
    sh["pekT"] = f(f(inp["a_pe_k"])[0].T)
    sh["pevT"] = f(f(inp["a_pe_v"])[0].T)
    sh["w_out0"] = f(inp["a_w_out"])[0]
    sh["w_out1"] = f(inp["b_w_out"])[0]
    sh["w_up0"], sh["w_up1"] = f(inp["ffn_w_up"])[0], f(inp["ffn_w_up"])[1]
    sh["w_dn0"], sh["w_dn1"] = f(inp["ffn_w_down"])[0], f(inp["ffn_w_down"])[1]
    kvw = f(inp["kv_w"])
    sh["kv_w"] = f(kvw[:, np.concatenate([np.concatenate([gi * 512 + kp, gi * 512 + 256 + nat]) for gi in range(3)])])
    wq = f(inp["b_w_q"])[0]
    sh["w_q"] = f(wq[:, np.concatenate([gi * 1024 + qp for gi in range(3)])])
    nm, nf = f(inp["norm_mix"]), f(inp["norm_ffn"])
    sh["gains"] = f(np.stack([nm[0], nf[0], nm[1], f(inp["kv_norm"]), nf[1], f(inp["final_norm"])]))
    sh["b_gate"] = f(inp["a_b_gate"]).reshape(1, 48)
    rb = f(inp["rel_bias"])
    rbT = np.ascontiguousarray(rb.T)
    p = np.arange(128)[:, None]

    def toep(ncol, maxd, scale):
        cc = np.arange(ncol)[None, :]
        dist = cc - p
        ok = (dist >= 0) & (dist <= maxd)
        g = rbT[:, _t5_bucket(dist * scale)]
        return f(np.where(ok[None], g, np.float32(MASKV)))

    sh["ms"] = toep(2048, 1 << 30, 1)
    sh["mw"] = toep(384, 255, 1)
    sh["md"] = f(np.concatenate([toep(256, 128, d) for (_, d) in DILS], axis=0))
    m = np.arange(247)[None, :]
    dist = p - 16 * (m - 120) - 31
    sh["mc"] = f(np.where((dist >= 0)[None], rbT[:, _t5_bucket(dist)], np.float32(MASKV)))
    i = np.arange(NT)[None, :, None]
    jb = np.arange(32)[None, None, :]
    cur = 2 * i + (np.arange(128)[:, None, None] // 64)
    valid = jb <= cur
    forced = (jb == 0) | (jb == cur) | (jb == cur - 1)
    sh["vtab"] = f(valid.astype(np.float32))
    sh["ctab"] = f(np.where(forced, 100.0, np.where(valid, 0.0, -1.0)))
    ci = np.arange(127)[:, None] * 16
    sj = np.arange(32)[None, :] * 64
    sh["ovl"] = f(((ci < sj + 64) & (ci + 32 > sj)).astype(np.float32))
    cblk = np.arange(32)[:, None, None]
    sh["ind"] = f((cblk == 2 * np.arange(NT)[None, :, None] + np.arange(128)[None, None, :] // 64).astype(np.float32))
    return sh


_PROGRAM = {}


def kernel(**inputs):
    x = np.asarray(inputs["x"], dtype=np.float32)
    sh = host_prep(inputs)
    if "nc" not in _PROGRAM:
        _PROGRAM["nc"] = build_program()
    nc = _PROGRAM["nc"]
    in_maps = []
    for cid in range(NCORES):
        d = dict(sh)
        d["x"] = np.ascontiguousarray(x[cid * NSEQ:(cid + 1) * NSEQ])
        in_maps.append(d)
    res = run_bass_kernel_spmd(nc, in_maps, core_ids=list(range(NCORES)))
    return np.concatenate([np.asarray(r["y"], dtype=np.float32) for r in res.results], axis=0)
```

```python
import os
import sys
import numpy as np
from contextlib import ExitStack
import concourse.bass as bass
import concourse.mybir as mybir
from concourse.bass_utils import run_bass_kernel_spmd

F32 = mybir.dt.float32
BF16 = mybir.dt.bfloat16
AF = mybir.ActivationFunctionType
ALU = mybir.AluOpType
AX = mybir.AxisListType

ENGS = ("tensor", "vector", "scalar", "gpsimd", "sync")
SEM_ROT = 6000


class Sched:
    def __init__(self, nc, n_dma_sems=(("sync", 20), ("gpsimd", 12), ("scalar", 6))):
        self.nc = nc
        self.ops = []
        self.top = ExitStack()
        self.scopes = [self.top]
        self.last_w = {}
        self.readers = {}
        self.npos = {e: 0 for e in ENGS}
        self.know = {e: {f: 0 for f in ENGS} for e in ENGS}
        self.dma_known = {e: set() for e in ENGS}
        self.done = []
        self.dma_sems = {}
        for e, n in n_dma_sems:
            self.dma_sems[e] = [[self._sem(f"d_{e}_{i}"), 0, None] for i in range(n)]
        self.dma_rr = {e: 0 for e in self.dma_sems}
        self.eng_sem = {e: [self._sem(f"c_{e}_0"), 0, 0] for e in ENGS}
        self.sig = {}
        self.psum_keys = set()
        self.block = None

    def _sem(self, name):
        return self.top.enter_context(self.nc.semaphore(name))

    def push(self):
        self.scopes.append(ExitStack())

    def pop(self):
        self.flush(barrier=True)
        self.scopes.pop().close()

    def _uniq(self, name):
        self._cnt = getattr(self, "_cnt", 0) + 1
        return f"{name}__{self._cnt}"

    def sb(self, name, shape, dtype):
        return self.scopes[-1].enter_context(self.nc.sbuf_tensor(self._uniq(name), list(shape), dtype))

    def ps(self, name, shape, dtype):
        return self.scopes[-1].enter_context(self.nc.psum_tensor(self._uniq(name), list(shape), dtype))

    def op(self, eng, fn, reads=(), writes=()):
        pw = tuple(k for k in reads if k in self.psum_keys and k not in writes)
        fr = sys._getframe(1)
        if fr.f_code.co_name in ("_mm", "_tr"):
            fr = fr.f_back
        self.ops.append(dict(eng=eng, fn=fn, reads=tuple(reads), writes=tuple(writes) + pw, wtrue=tuple(writes), dma=False,
                             site=(fr.f_code.co_name, fr.f_lineno)))

    def dma(self, eng, out, in_, reads=(), writes=(), **kw):
        self.ops.append(dict(eng=eng, fn=lambda e: e.dma_start(out=out, in_=in_, **kw),
                             reads=tuple(reads), writes=tuple(writes), wtrue=tuple(writes), dma=True,
                             site=(sys._getframe(1).f_code.co_name, sys._getframe(1).f_lineno)))

    def flush(self, barrier=False, final_keys=None):
        ops = self.ops
        self.ops = []
        base = len(self.done)
        for i, o in enumerate(ops):
            gid = base + i
            deps = set()
            for k in o["reads"]:
                if k in self.last_w:
                    deps.add(self.last_w[k])
            for k in o["writes"]:
                if k in self.last_w:
                    deps.add(self.last_w[k])
                deps.update(self.readers.get(k, ()))
            deps.discard(gid)
            o["deps"] = deps
            o["gid"] = gid
            for k in o["reads"]:
                self.readers.setdefault(k, []).append(gid)
            for k in o["writes"]:
                self.last_w[k] = gid
                self.readers[k] = []
            self.done.append(o)
        for o in ops:
            e = o["eng"]
            self.npos[e] += 1
            o["pos"] = self.npos[e]
            waits = []
            know = self.know[e]
            for d in sorted(o["deps"]):
                dd = self.done[d]
                if dd["dma"]:
                    if d in self.dma_known[e]:
                        continue
                    self.dma_known[e].add(d)
                    waits.append(d)
                else:
                    f = dd["eng"]
                    if f == e:
                        raw = any(k in dd["wtrue"] for k in o["reads"])
                        if not raw or know[f] >= dd["pos"]:
                            continue
                    if know[f] >= dd["pos"]:
                        continue
                    waits.append(d)
                    know[f] = max(know[f], dd["pos"])
                    for g, v in dd["ksnap"].items():
                        if v > know[g]:
                            know[g] = v
            o["waits"] = waits
            o["ksnap"] = dict(know)
            for d in waits:
                self.done[d]["signal"] = True
        if final_keys is not None:
            self.final_wait = []
            for k in final_keys:
                d = self.last_w[k]
                self.done[d]["signal"] = True
                self.final_wait.append(d)
        if barrier:
            self._barrier_prepare(ops)
        by_eng = {e: [o for o in ops if o["eng"] == e] for e in ENGS}
        self._emit(by_eng, barrier, final_keys is not None)

    def _barrier_prepare(self, ops):
        self.bar_targets = []
        for e in ENGS:
            lst = [o for o in ops if o["eng"] == e and not o["dma"]]
            if lst:
                lst[-1]["signal"] = True
                self.bar_targets.append(lst[-1]["gid"])
        for o in ops:
            if o["dma"]:
                o["signal"] = True
                self.bar_targets.append(o["gid"])

    def _assign_signal(self, o):
        e = o["eng"]
        if o["dma"]:
            return None
        if not o.get("signal"):
            return None
        rec = self.eng_sem[e]
        if rec[1] >= SEM_ROT:
            rec[2] += 1
            rec[0] = self._sem(f"c_{e}_{rec[2]}")
            rec[1] = 0
        rec[1] += 1
        self.sig[o["gid"]] = (rec[0], rec[1])
        return (rec[0], 1)

    def _emit(self, by_eng, barrier, final):
        nc = self.nc
        for e in ENGS:
            for o in by_eng[e]:
                if not o["dma"]:
                    o["_sig"] = self._assign_signal(o)
        for e in ENGS:
            for o in by_eng[e]:
                if o["dma"]:
                    pool = self.dma_sems[e]
                    idx = self.dma_rr[e]
                    self.dma_rr[e] = (idx + 1) % len(pool)
                    rec = pool[idx]
                    o["_prev"] = (rec[0], rec[1]) if rec[1] > 0 else None
                    rec[1] += 16
                    rec[2] = o["gid"]
                    self.sig[o["gid"]] = (rec[0], rec[1])
                    o["_sig"] = (rec[0], 16)
        bar = list(self.bar_targets) if barrier else []
        fin = list(self.final_wait) if final else []

        def run(e):
            def body(eng):
                for o in by_eng[e]:
                    for d in o["waits"]:
                        s, v = self.sig[d]
                        eng.wait_ge(s, v)
                    if o["dma"] and o["_prev"] is not None:
                        eng.wait_ge(*o["_prev"])
                    try:
                        ins = o["fn"](eng)
                    except Exception:
                        print("EMIT FAILED at", o.get("site"), flush=True)
                        raise
                    if o["_sig"] is not None:
                        ins.then_inc(o["_sig"][0], o["_sig"][1])
                for d in bar:
                    s, v = self.sig[d]
                    eng.wait_ge(s, v)
                if e == "sync":
                    for d in fin:
                        s, v = self.sig[d]
                        eng.wait_ge(s, v)
            return body

        with nc.Block() as block:
            for e in ENGS:
                getattr(block, e)(run(e))
        if barrier:
            for e in ENGS:
                for f in ENGS:
                    self.know[e][f] = self.npos[f]
                self.dma_known[e].update(o["gid"] for o in self.done if o["dma"])

    def finish(self, final_keys):
        self.flush(barrier=False, final_keys=final_keys)
        while len(self.scopes) > 1:
            self.scopes.pop().close()
        self.top.close()


NCORES = 8
NSEQ = 4
SL = 2048
DM = 1024
NT = 16
FF = 2816
NFC = 22
MASKV = -30000.0
DILS = ((128, 1), (512, 4), (2048, 16))
GELU_C = 1.5957691216057308


class Ring:
    def __init__(self, S, name, shape, dtype, n, psum=False):
        mk = S.ps if psum else S.sb
        self.t = [mk(f"{name}{k}", shape, dtype) for k in range(n)]
        self.k = [f"{name}{k}" for k in range(n)]
        if psum:
            nbytes = int(np.prod(shape[1:])) * (4 if dtype == F32 else 2)
            assert nbytes == 2048, (name, shape)
            S.psum_keys.update(self.k)
        self.n = n
        self.i = 0

    def next(self):
        j = self.i % self.n
        self.i += 1
        return self.t[j], self.k[j]


class Ctx:
    pass


def _mm(S, out, lhsT, rhs, start, stop, reads, writes, sgc=False):
    S.op("tensor", lambda e: e.matmul(out, lhsT, rhs, start=start, stop=stop, skip_group_check=sgc), reads=reads, writes=writes)


def _tr(S, out, in_, ident, reads, writes):
    S.op("tensor", lambda e: e.transpose(out, in_, ident), reads=reads, writes=writes)


def norm_tile(S, c, W, src_ap, src_keys, gains, xring):
    xt, xk = xring.next()
    S.dma("sync", xt[:], src_ap, reads=src_keys, writes=[xk])
    st, sk = W["ss"].next()
    S.op("gpsimd", lambda e: e.memset(st[:], 0.0), writes=[sk])
    S.op("scalar", lambda e: e.activation(W["junk"][:], xt[:], AF.Square, accum_out=st[:, 0:1]),
         reads=[xk, sk], writes=["junk", sk])
    S.op("vector", lambda e: e.tensor_scalar(st[:, 1:2], st[:, 0:1], 1.0 / DM, 1e-6, ALU.mult, ALU.add),
         reads=[sk], writes=[sk])
    S.op("scalar", lambda e: e.activation(st[:, 3:4], st[:, 1:2], AF.Sqrt), reads=[sk], writes=[sk])
    S.op("vector", lambda e: e.reciprocal(st[:, 2:3], st[:, 3:4]), reads=[sk], writes=[sk])
    outs = []
    for (gt, gk) in gains:
        hb, hk = W["hb"].next()
        S.op("vector", lambda e, hb=hb, gt=gt: e.scalar_tensor_tensor(hb[:], xt[:], st[:, 2:3], gt[:], ALU.mult, ALU.mult),
             reads=[xk, sk, gk], writes=[hk])
        outs.append((hb, hk))
    return xt, xk, st, sk, outs


def transpose_tile(S, c, W, hb, hk, dst_ap, dst_key, eng="scalar"):
    pt, pk = W["ptr"].next()
    for k in range(8):
        _tr(S, pt[:, k, :], hb[:, k * 128:(k + 1) * 128], c.identb[:], [hk, "identb"], [pk])
    if eng == "scalar":
        S.op("scalar", lambda e: e.copy(dst_ap, pt[:]), reads=[pk], writes=[dst_key])
    else:
        S.op("vector", lambda e: e.tensor_copy(dst_ap, pt[:]), reads=[pk], writes=[dst_key])


def load_gain(S, c, name, row):
    t = S.sb(name, [128, DM], F32)
    S.dma("sync", t[:], c.gains[row].partition_broadcast(128), reads=[], writes=[name])
    return t, name


def norm_work(S, pfx):
    return dict(ss=Ring(S, pfx + "ss", [128, 4], F32, 4), hb=Ring(S, pfx + "hb", [128, DM], BF16, 3),
                junk=S.sb(pfx + "junk", [128, DM], BF16), ptr=Ring(S, pfx + "ptr", [128, 8, 128], BF16, 2, psum=True))


def convert_all(S, c):
    S.push()
    NB = 4
    CW = 2048
    st = [S.sb(f"cvf{k}", [128, CW], F32) for k in range(NB)]
    sb = [S.sb(f"cvb{k}", [128, CW], BF16) for k in range(NB)]
    jobs = []
    for (src, dst, key) in c.conv_jobs:
        R, C = src.shape
        assert R % 128 == 0, (key, R)
        for r0 in range(0, R, 128):
            for c0 in range(0, C, CW):
                w = min(CW, C - c0)
                jobs.append((src[r0:r0 + 128, c0:c0 + w], dst[r0:r0 + 128, c0:c0 + w], w, key))
    for n, (src, dst, w, key) in enumerate(jobs):
        k = n % NB
        S.dma("sync", st[k][:, :w], src, reads=[], writes=[f"cvf{k}"])
        if n % 2 == 0:
            S.op("vector", lambda e, k=k, w=w: e.tensor_copy(sb[k][:, :w], st[k][:, :w]),
                 reads=[f"cvf{k}"], writes=[f"cvb{k}"])
        else:
            S.op("scalar", lambda e, k=k, w=w: e.copy(sb[k][:, :w], st[k][:, :w]),
                 reads=[f"cvf{k}"], writes=[f"cvb{k}"])
        S.dma("gpsimd", dst, sb[k][:, :w], reads=[f"cvb{k}"], writes=[key])
    S.pop()


def l0_proj(S, c, A, Bk, b):
    S.push()
    Win = S.sb("Win", [128, 8, 2608], BF16)
    for k in range(8):
        S.dma("sync", Win[:, k, :], c.w_in_b[k * 128:(k + 1) * 128, :], reads=[], writes=[("Win", k)])
    gmix = load_gain(S, c, "gmix0", 0)
    bg = S.sb("bgate", [128, 48], F32)
    S.dma("sync", bg[:], c.b_gate[0].partition_broadcast(128), reads=[], writes=["bgate"])
    W = norm_work(S, "p1")
    xring = Ring(S, "p1x", [128, DM], F32, 3)
    hT = Ring(S, "p1hT", [128, 8, 512], BF16, 2)
    pfm = Ring(S, "p1pfm", [128, 512], F32, 3, psum=True)
    ptm = Ring(S, "p1ptm", [128, 512], F32, 2, psum=True)
    glt = Ring(S, "p1gl", [128, 48], F32, 2)
    S.op("gpsimd", lambda e: e.memset(A["Vs"][:, :, :, 64:65], 1.0), writes=["Vs1"])
    S.op("gpsimd", lambda e: e.memset(A["Vw"][:, :, :, 64:65], 1.0), writes=["Vw1"])
    fmdst = []
    for fc in range(8):
        fmdst.append((A["QT"], fc // 4, fc % 4, 0.125))
    for nm in ("KcT", "VcT"):
        for j in range(2):
            fmdst.append((Bk[nm], j, None, 1.0))
    for nm in ("KsT", "KwT"):
        for j in range(2):
            fmdst.append((A[nm], j, None, 1.0))
    nev = 0
    for mt in range(4):
        hTt, hTk = hT.next()
        for t in range(4):
            i = mt * 4 + t
            xt, xk, st, sk, outs = norm_tile(S, c, W, c.x[b, i * 128:(i + 1) * 128, :], [], [gmix], xring)
            transpose_tile(S, c, W, outs[0][0], outs[0][1], hTt[:, :, t * 128:(t + 1) * 128], (hTk, t))
        allk = [(hTk, t) for t in range(4)]
        for fc in range(16):
            pt, pk = pfm.next()
            for k in range(8):
                _mm(S, pt[:], Win[:, k, fc * 128:(fc + 1) * 128], hTt[:, k, :], k == 0, k == 7,
                    [("Win", k)] + allk, [pk])
            T, j, r, sc = fmdst[fc]
            dst = T[:, j, r, mt * 512:(mt + 1) * 512] if r is not None else T[:, j, mt * 512:(mt + 1) * 512]
            if nev % 2 == 0:
                S.op("scalar", lambda e, dst=dst, pt=pt, sc=sc: e.mul(dst, pt[:], sc), reads=[pk], writes=[("fm", fc, mt)])
            else:
                S.op("vector", lambda e, dst=dst, pt=pt, sc=sc: e.tensor_scalar(dst, pt[:], sc, None, ALU.mult),
                     reads=[pk], writes=[("fm", fc, mt)])
            nev += 1
        for t in range(4):
            i = mt * 4 + t
            pt, pk = ptm.next()
            for k in range(8):
                _mm(S, pt[:], hTt[:, k, t * 128:(t + 1) * 128], Win[:, k, 2048:2560], k == 0, k == 7,
                    [("Win", k), (hTk, t)], [pk])
            S.op("vector", lambda e, pt=pt, i=i: e.tensor_copy(A["Vs"][:, i, :, 0:64], pt[:, 0:256].rearrange("p (g d) -> p g d", g=4)),
                 reads=[pk, "Vs1"], writes=[("Vs", i)])
            S.op("scalar", lambda e, pt=pt, i=i: e.copy(A["Vw"][:, i, :, 0:64], pt[:, 256:512].rearrange("p (g d) -> p g d", g=4)),
                 reads=[pk, "Vw1"], writes=[("Vw", i)])
            pt2, pk2 = ptm.next()
            for k in range(8):
                _mm(S, pt2[:, 0:48], hTt[:, k, t * 128:(t + 1) * 128], Win[:, k, 2560:2608], k == 0, k == 7,
                    [("Win", k), (hTk, t)], [pk2])
            gt, gk = glt.next()
            S.op("vector", lambda e, gt=gt, pt2=pt2: e.tensor_tensor(gt[:], pt2[:, 0:48], bg[:], ALU.add),
                 reads=[pk2, "bgate"], writes=[gk])
            S.op("scalar", lambda e, gt=gt, i=i: e.activation(A["G"][:, i, :], gt[:], AF.Sigmoid),
                 reads=[gk], writes=[("G", i)])
    S.pop()


def l0_compress(S, c, A, Bk):
    S.push()
    pgel = Ring(S, "p2ph", [128, 512], F32, 2, psum=True)
    pmisc = Ring(S, "p2pm", [128, 512], F32, 2, psum=True)
    for kv, (w1b, w2b, peT, src) in enumerate(((c.w1k_b, c.w2k_b, c.pekT, Bk["KcT"]), (c.w1v_b, c.w2v_b, c.pevT, Bk["VcT"]))):
        W1 = S.sb(f"W1_{kv}", [128, 32, 256], BF16)
        for u in range(2):
            S.dma("sync", W1[u * 64:(u + 1) * 64, :, :], w1b.rearrange("(t d) n -> d t n", d=64), reads=[], writes=[(f"W1_{kv}", u)])
        W2 = S.sb(f"W2_{kv}", [128, 2, 128], BF16)
        for h in range(2):
            S.dma("sync", W2[:, :, h * 64:(h + 1) * 64], w2b.rearrange("(c p) n -> p c n", p=128), reads=[], writes=[(f"W2_{kv}", h)])
        pef = S.sb(f"pef_{kv}", [64, 32], F32)
        S.dma("sync", pef[:], peT, reads=[], writes=[f"pef_{kv}"])
        peb = S.sb(f"peb_{kv}", [64, 32], BF16)
        S.op("vector", lambda e, peb=peb, pef=pef: e.tensor_copy(peb[:], pef[:]), reads=[f"pef_{kv}"], writes=[f"peb_{kv}"])
        pb, pbk = pmisc.next()
        for c2 in range(2):
            for t in range(32):
                _mm(S, pb[:, c2:c2 + 1], W1[0:64, t, c2 * 128:(c2 + 1) * 128], peb[0:64, t:t + 1], t == 0, t == 31,
                    [(f"W1_{kv}", 0), f"peb_{kv}"], [pbk])
        pbias = S.sb(f"pbias_{kv}", [128, 2], F32)
        S.op("vector", lambda e, pbias=pbias, pb=pb: e.tensor_copy(pbias[:], pb[:, 0:2]), reads=[pbk], writes=[f"pbias_{kv}"])
        gelT = Ring(S, f"p2gel{kv}", [128, 2, 128], BF16, 2)
        tmp = Ring(S, f"p2tmp{kv}", [128, 4, 128], F32, 2)
        for g in range(4):
            u, j = g // 2, g % 2
            prt = slice(u * 64, (u + 1) * 64)
            gl, glk = gelT.next()
            for c2 in range(2):
                ph, phk = pgel.next()
                for t in range(32):
                    _mm(S, ph[:, 0:127], W1[prt, t, c2 * 128:(c2 + 1) * 128], src[prt, j, t:t + 16 * 126 + 1:16], t == 0, t == 31,
                        [(f"W1_{kv}", u)], [phk])
                tm, tk = tmp.next()
                S.op("vector", lambda e, tm=tm, ph=ph, c2=c2, pbias=pbias: e.tensor_scalar(tm[:, 0, 0:127], ph[:, 0:127], pbias[:, c2:c2 + 1], None, ALU.add),
                     reads=[phk, f"pbias_{kv}"], writes=[(tk, 0)])
                S.op("vector", lambda e, tm=tm: e.tensor_tensor(tm[:, 1, 0:127], tm[:, 0, 0:127], tm[:, 0, 0:127], ALU.mult),
                     reads=[(tk, 0)], writes=[(tk, 1)])
                S.op("vector", lambda e, tm=tm: e.tensor_scalar(tm[:, 1, 0:127], tm[:, 1, 0:127], 0.044715, 1.0, ALU.mult, ALU.add),
                     reads=[(tk, 1)], writes=[(tk, 1)])
                S.op("vector", lambda e, tm=tm: e.tensor_tensor(tm[:, 2, 0:127], tm[:, 1, 0:127], tm[:, 0, 0:127], ALU.mult),
                     reads=[(tk, 1), (tk, 0)], writes=[(tk, 2)])
                S.op("scalar", lambda e, tm=tm: e.activation(tm[:, 3, 0:127], tm[:, 2, 0:127], AF.Sigmoid, scale=GELU_C),
                     reads=[(tk, 2)], writes=[(tk, 3)])
                S.op("vector", lambda e, tm=tm, gl=gl, c2=c2: e.tensor_tensor(gl[:, c2, 0:127], tm[:, 3, 0:127], tm[:, 0, 0:127], ALU.mult),
                     reads=[(tk, 3), (tk, 0)], writes=[(glk, c2)])
            po, pok = pmisc.next()
            if kv == 0:
                for c2 in range(2):
                    _mm(S, po[:, 0:127], W2[:, c2, :], gl[:, c2, 0:127], c2 == 0, c2 == 1,
                        [(f"W2_{kv}", 0), (f"W2_{kv}", 1), (glk, c2)], [pok])
                S.op("scalar", lambda e, po=po, prt=prt, j=j: e.copy(A["KcmpT"][prt, j, 0:127], po[prt, 0:127]),
                     reads=[pok], writes=[("KcmpT", g)])
            else:
                for c2 in range(2):
                    _mm(S, po[0:127, 0:64], gl[:, c2, 0:127], W2[:, c2, 0:64], c2 == 0, c2 == 1,
                        [(f"W2_{kv}", 0), (glk, c2)], [pok])
                S.op("scalar", lambda e, po=po, g=g: e.copy(A["Vcmp"][0:127, g, :], po[0:127, 0:64]),
                     reads=[pok], writes=[("Vcmp", g)])
    S.pop()


def run_attention(S, c, R, jobs):
    flat = []
    for jb in jobs:
        n = len(jb["items"])
        for m in range(n):
            flat.append((jb, m, n))

    gens = [jb for jb in jobs if jb.get("pregen")]
    pending = []

    def step(g_):
        try:
            next(g_)
            return True
        except StopIteration:
            return False

    def qk(ent):
        jb, m, n = ent
        if m == 0:
            if jb.get("pre"):
                jb["pre"]()
            if jb.get("pregen"):
                if "gen" not in jb:
                    jb["gen"] = jb["pregen"]()
                while step(jb["gen"]):
                    pass
                if jb["gen"] in pending:
                    pending.remove(jb["gen"])
                k = gens.index(jb)
                if k + 1 < len(gens):
                    nj = gens[k + 1]
                    nj["gen"] = nj["pregen"]()
                    pending.append(nj["gen"])
            jb["O"] = R["O"].next()
        elif pending:
            if not step(pending[0]):
                pending.pop(0)
        kT, bias, mask, v = jb["items"][m]
        ST, sk = R["ST"].next()
        _mm(S, ST[:], kT, jb["q"], True, False, jb.get("qk_reads", []), [sk])
        _mm(S, ST[:], c.identb[:], bias, False, mask is None, ["identb"] + jb.get("bias_reads", []), [sk])
        if mask is not None:
            _mm(S, ST[:], mask[0], mask[1], False, True, [mask[2]], [sk])
        return ST, sk

    cur = qk(flat[0]) if flat else None
    for idx, ent in enumerate(flat):
        jb, m, n = ent
        nxt = qk(flat[idx + 1]) if idx + 1 < len(flat) else None
        ST, sk = cur
        PT, pk = R["PT"].next()
        S.op("scalar", lambda e, PT=PT, ST=ST: e.activation(PT[:], ST[:], AF.Exp), reads=[sk], writes=[pk])
        O, ok = jb["O"]
        v = jb["items"][m][3]
        for r in range(4):
            _mm(S, O[:, r, 0:65], PT[:, r, :], v, m == 0 and r == 0, m == n - 1, [pk] + jb.get("v_reads", []), [ok], sgc=True)
        if m == n - 1:
            jb["done"](O, ok)
        cur = nxt


def l0_attention(S, c, A, b):
    S.push()
    R = dict(ST=Ring(S, "aST", [128, 4, 128], F32, 2, psum=True), O=Ring(S, "aO", [128, 4, 128], F32, 3, psum=True),
             PT=Ring(S, "aPT", [128, 4, 128], BF16, 3))
    cmpS = S.ps("cmpS", [128, 4, 128], F32)
    misc = S.ps("amisc", [128, 512], F32)
    tpB = S.ps("atpB", [128, 8, 128], BF16)
    S.psum_keys.update(["cmpS", "amisc", "tpB"])
    MS = S.sb("MS", [128, 4, 2048], BF16)
    MW = S.sb("MW", [128, 4, 384], BF16)
    MC = S.sb("MC", [128, 4, 247], F32)
    Sc = Ring(S, "aSc", [128, 4, 127], F32, 2)
    Pn = Ring(S, "aPn", [128, 4, 127], F32, 2)
    Pnb = Ring(S, "aPnb", [128, 4, 128], BF16, 2)
    sm = Ring(S, "asm", [128, 64], F32, 2)
    Ps = Ring(S, "aPs", [128, 128], F32, 2)
    PsT = Ring(S, "aPsT", [128, 128], F32, 2)
    scb = Ring(S, "asc", [128, 2, 32], F32, 2)
    NMT = Ring(S, "aNMT", [32, 128], BF16, 2)
    PcT = Ring(S, "aPcT", [128, 4, 128], BF16, 2)
    cf = Ring(S, "acf", [128, 4, 4, 1], F32, 2)
    acc = Ring(S, "aacc", [128, 3, 4, 64], F32, 2)
    ocs = Ring(S, "aocs", [128, 4, 64], F32, 3)
    for g in range(4):
        u, j = g // 2, g % 2
        prt = slice(u * 64, (u + 1) * 64)
        for r in range(4):
            h = 4 * g + r
            S.dma("sync", MS[:, r, :], c.ms_b[h], reads=[], writes=[("MS", r)])
            S.dma("sync", MW[:, r, :], c.mw_b[h], reads=[], writes=[("MW", r)])
            S.dma("sync", MC[:, r, :], c.mc[h], reads=[], writes=[("MC", r)])
        MSk = [("MS", r) for r in range(4)]
        MWk = [("MW", r) for r in range(4)]
        MCk = [("MC", r) for r in range(4)]
        jobs = []
        for i in range(NT):
            qs = slice(i * 128, (i + 1) * 128)
            st = {}

            def pre(i=i, qs=qs, st=st):
                off = 120 - 8 * i
                for r in range(4):
                    _mm(S, cmpS[:, r, 0:127], A["QT"][prt, j, r, qs], A["KcmpT"][prt, j, 0:127], True, True, [], ["cmpS"])
                sc_, sck = Sc.next()
                S.op("vector", lambda e: e.tensor_tensor(sc_[:], cmpS[:, :, 0:127], MC[:, :, off:off + 127], ALU.add),
                     reads=["cmpS"] + MCk, writes=[sck])
                S.op("scalar", lambda e: e.activation(sc_[:], sc_[:], AF.Exp), reads=[sck], writes=[sck])
                s_, smk = sm.next()
                S.op("vector", lambda e: e.tensor_reduce(s_[:, 0:4], sc_[:], AX.X, ALU.add), reads=[sck], writes=[(smk, 0)])
                S.op("vector", lambda e: e.tensor_scalar_max(s_[:, 0:4], s_[:, 0:4], 1e-30), reads=[(smk, 0)], writes=[(smk, 0)])
                S.op("vector", lambda e: e.reciprocal(s_[:, 4:8], s_[:, 0:4]), reads=[(smk, 0)], writes=[(smk, 1)])
                pn, pnk = Pn.next()
                S.op("vector", lambda e: e.tensor_tensor(pn[:], sc_[:], s_[:, 4:8].unsqueeze(2).to_broadcast([128, 4, 127]), ALU.mult),
                     reads=[sck, (smk, 1)], writes=[pnk])
                pb, pbk = Pnb.next()
                S.op("scalar", lambda e: e.copy(pb[:, :, 0:127], pn[:]), reads=[pnk], writes=[pbk])
                ps_, psk = Ps.next()
                S.op("vector", lambda e: e.tensor_reduce(ps_[:, 0:127], pn[:].rearrange("p r n -> p n r"), AX.X, ALU.add),
                     reads=[pnk], writes=[psk])
                yield
                _tr(S, misc[0:127, 0:128], ps_[:, 0:127], c.identf[:], [psk, "identf"], ["amisc"])
                pt_, ptk = PsT.next()
                S.op("scalar", lambda e: e.copy(pt_[0:127, :], misc[0:127, 0:128]), reads=["amisc"], writes=[ptk])
                yield
                _mm(S, misc[:, 128:160], pt_[0:127, :], c.ovl[0:127, :], True, True, [ptk, "ovl"], ["amisc"])
                sb_, sbk = scb.next()
                S.op("vector", lambda e: e.tensor_tensor(sb_[:, 0, :], misc[:, 128:160], c.vtab[:, i, :], ALU.mult),
                     reads=["amisc", "vtab"], writes=[(sbk, 0)])
                S.op("vector", lambda e: e.tensor_tensor(sb_[:, 0, :], sb_[:, 0, :], c.ctab[:, i, :], ALU.add),
                     reads=[(sbk, 0), "ctab"], writes=[(sbk, 0)])
                S.op("vector", lambda e: e.max(out=s_[:, 8:16], in_=sb_[:, 0, :]), reads=[(sbk, 0)], writes=[(smk, 2)])
                S.op("vector", lambda e: e.tensor_scalar(sb_[:, 1, :], sb_[:, 0, :], s_[:, 15:16], MASKV, ALU.is_lt, ALU.mult),
                     reads=[(sbk, 0), (smk, 2)], writes=[(sbk, 1)])
                yield
                _tr(S, misc[0:32, 0:128], sb_[:, 1, :], c.identf[:], [(sbk, 1), "identf"], ["amisc"])
                nmt, nmk = NMT.next()
                S.op("scalar", lambda e: e.copy(nmt[:], misc[0:32, 0:128]), reads=["amisc"], writes=[nmk])
                st["nmt"] = (nmt, nmk)
                yield
                for r in range(4):
                    _tr(S, tpB[0:127, r, :], pb[:, r, 0:127], c.identb[:], [pbk, "identb"], ["tpB"])
                pc, pck = PcT.next()
                S.op("vector", lambda e: e.tensor_copy(pc[0:127], tpB[0:127, 0:4]), reads=["tpB"], writes=[pck])
                yield
                for r in range(4):
                    _mm(S, misc[:, 160 + 64 * r:224 + 64 * r], pc[0:127, r, :], A["Vcmp"][0:127, g, :], True, True, [pck], ["amisc"])
                oc_, ock = ocs.next()
                S.op("scalar", lambda e: e.copy(oc_[:], misc[:, 160:416].rearrange("p (r d) -> p r d", r=4)), reads=["amisc"], writes=[ock])
                st["oc"] = (oc_, ock)

            qrhs = A["QT"][prt, j, :, qs]
            wj = [jj for jj in (i - 2, i - 1, i) if jj >= 0]
            witems = [(A["KwT"][prt, j, jj * 128:(jj + 1) * 128], MW[:, :, (i - jj) * 128:(i - jj + 1) * 128], None,
                       A["Vw"][:, jj, g, :]) for jj in wj]
            jobs.append(dict(q=qrhs, items=witems, pregen=pre, bias_reads=MWk,
                             done=(lambda O, ok, st=st: st.__setitem__("Ow", (O, ok)))))

            class _LazyItems(list):
                pass

            sitems = []
            for jj in range(i + 1):
                sitems.append([A["KsT"][prt, j, jj * 128:(jj + 1) * 128], MS[:, :, (i - jj) * 128:(i - jj + 1) * 128],
                               ("lazy", jj), A["Vs"][:, jj, g, :]])

            def done(Os, osk, i=i, st=st):
                Ow, owk = st["Ow"]
                cf_, cfk = cf.next()
                a_, ak = acc.next()
                Gv = A["G"][:, i, g * 12:(g + 1) * 12].rearrange("p (r k) -> p r k", k=3)
                S.op("vector", lambda e: e.reciprocal(cf_[:, 0, :, :], Os[:, :, 64:65]), reads=[osk], writes=[(cfk, 0)])
                S.op("vector", lambda e: e.reciprocal(cf_[:, 1, :, :], Ow[:, :, 64:65]), reads=[owk], writes=[(cfk, 1)])
                S.op("vector", lambda e: e.tensor_tensor(cf_[:, 2, :, :], cf_[:, 0, :, :], Gv[:, :, 1:2], ALU.mult),
                     reads=[(cfk, 0)], writes=[(cfk, 2)])
                S.op("vector", lambda e: e.tensor_tensor(cf_[:, 3, :, :], cf_[:, 1, :, :], Gv[:, :, 2:3], ALU.mult),
                     reads=[(cfk, 1)], writes=[(cfk, 3)])
                oc_, ock = st["oc"]
                S.op("vector", lambda e: e.tensor_tensor(a_[:, 0], oc_[:], Gv[:, :, 0:1].to_broadcast([128, 4, 64]), ALU.mult),
                     reads=[ock], writes=[(ak, 0)])
                S.op("vector", lambda e: e.tensor_tensor(a_[:, 1], Os[:, :, 0:64], cf_[:, 2, :, :].to_broadcast([128, 4, 64]), ALU.mult),
                     reads=[osk, (cfk, 2)], writes=[(ak, 1)])
                S.op("vector", lambda e: e.tensor_tensor(a_[:, 2], Ow[:, :, 0:64], cf_[:, 3, :, :].to_broadcast([128, 4, 64]), ALU.mult),
                     reads=[owk, (cfk, 3)], writes=[(ak, 2)])
                S.op("gpsimd", lambda e: e.tensor_tensor(a_[:, 0], a_[:, 0], a_[:, 1], ALU.add), reads=[(ak, 0), (ak, 1)], writes=[(ak, 0)])
                odst = A["Oall"][:, i, g * 256:(g + 1) * 256].rearrange("p (r d) -> p r d", r=4)
                S.op("gpsimd", lambda e: e.tensor_tensor(odst, a_[:, 0], a_[:, 2], ALU.add), reads=[(ak, 0), (ak, 2)], writes=[("Oall", i, odst.offset)])

            jobs.append(dict(q=qrhs, items=sitems, bias_reads=MSk, done=done, lazy=st))
        for jb in jobs:
            if "lazy" in jb:
                jb["items"] = _LazyMaskItems(jb["items"], jb["lazy"], c)
                jb["mask_reads_fn"] = True
        run_attention(S, c, R, jobs)
    S.pop()


class _LazyMaskItems:
    def __init__(self, items, st, c):
        self.items = items
        self.st = st
        self.c = c

    def __len__(self):
        return len(self.items)

    def __getitem__(self, m):
        kT, bias, mask, v = self.items[m]
        jj = mask[1]
        nmt, nmk = self.st["nmt"]
        return (kT, bias, (self.c.indb[:, jj, :], nmt[:, :].unsqueeze(1).to_broadcast([32, 4, 128]), nmk), v)


def out_proj(S, c, w_b, get_o, xin, xout, xin_keys, xout_key):
    S.push()
    Wo = S.sb("Wo", [128, 8, DM], BF16)
    for k in range(8):
        S.dma("sync", Wo[:, k, :], w_b[k * 128:(k + 1) * 128, :], reads=[], writes=[("Wo", k)])
    ptr = Ring(S, "opptr", [128, 8, 128], BF16, 2, psum=True)
    oT = Ring(S, "opoT", [128, 8, 128], BF16, 2)
    po = Ring(S, "oppo", [128, 512], F32, 4, psum=True)
    xr = Ring(S, "opx", [128, DM], F32, 3)
    W = {}
    setup = get_o(None, W)
    for i in range(NT):
        xt, xk = xr.next()
        S.dma("sync", xt[:], xin(i), reads=xin_keys(i), writes=[xk])
        ob, obk = get_o(i, W)
        pt, pk = ptr.next()
        for k in range(8):
            _tr(S, pt[:, k, :], ob[:, k * 128:(k + 1) * 128], c.identb[:], obk + ["identb"], [pk])
        ot, otk = oT.next()
        S.op("scalar", lambda e, ot=ot, pt=pt: e.copy(ot[:], pt[:]), reads=[pk], writes=[otk])
        for n in range(2):
            p_, pok = po.next()
            for k in range(8):
                _mm(S, p_[:], ot[:, k, :], Wo[:, k, n * 512:(n + 1) * 512], k == 0, k == 7, [otk, ("Wo", k)], [pok])
            S.op("vector", lambda e, xt=xt, p_=p_, n=n: e.tensor_tensor(xt[:, n * 512:(n + 1) * 512], xt[:, n * 512:(n + 1) * 512], p_[:], ALU.add),
                 reads=[pok, xk], writes=[xk])
        S.dma("gpsimd", xout(i), xt[:], reads=[xk], writes=[xout_key(i)])
    S.pop()


def ffn(S, c, wup_b, wdn_b, gain_row, xin, xout, xin_keys, xout_key, final_row=None):
    S.push()
    Wup = S.sb("Wup", [128, 8, 2 * FF], BF16)
    Wdn = S.sb("Wdn", [128, NFC, DM], BF16)
    wup_v = wup_b.rearrange("(c p) n -> p c n", p=128)

    def load_weights():
        n = 0
        for p_ in range(NFC // 2):
            for half, base in (("a", 0), ("b", FF)):
                cs = slice(base + p_ * 256, base + (p_ + 1) * 256)
                S.dma("sync" if n % 2 == 0 else "scalar", Wup[:, :, cs], wup_v[:, :, cs], reads=[], writes=[("Wup", half, p_)])
                n += 1
        for cc in range(NFC):
            S.dma("sync" if cc % 2 == 0 else "scalar", Wdn[:, cc, :], wdn_b[cc * 128:(cc + 1) * 128, :], reads=[], writes=[("Wdn", cc)])
    gt = load_gain(S, c, "gffn", gain_row)
    gf = load_gain(S, c, "gfin", final_row) if final_row is not None else None
    W = dict(ss=Ring(S, "fss", [128, 4], F32, 4), hb=Ring(S, "fhb", [128, DM], BF16, 2),
             junk=S.sb("fjunk", [128, DM], BF16), ptr=Ring(S, "fptr", [128, 8, 128], BF16, 2, psum=True))
    xring = Ring(S, "fx", [128, DM], F32, 3)
    hT = Ring(S, "fhT", [128, 8, 256], BF16, 2)
    aT = Ring(S, "faT", [128, NFC, 256], BF16, 2)
    pa = Ring(S, "fpa", [128, 512], F32, 4, psum=True)
    po = Ring(S, "fpo", [128, 512], F32, 2, psum=True)
    sg = Ring(S, "fsg", [128, 256], F32, 3)
    nev = 0
    for mt in range(NT // 2):
        hTt, hTk = hT.next()
        xts = []
        for t in range(2):
            i = mt * 2 + t
            xt, xk, st, sk, outs = norm_tile(S, c, W, xin(i), xin_keys(i), [gt], xring)
            transpose_tile(S, c, W, outs[0][0], outs[0][1], hTt[:, :, t * 128:(t + 1) * 128], (hTk, t),
                           eng="scalar" if t == 0 else "vector")
            xts.append((xt, xk, st, sk))
        if mt == 0:
            load_weights()
        hk2 = [(hTk, 0), (hTk, 1)]
        at, atk = aT.next()
        for cc in range(NFC):
            p1, p1k = pa.next()
            p2, p2k = pa.next()
            for k in range(8):
                _mm(S, p1[:, 0:256], Wup[:, k, cc * 128:(cc + 1) * 128], hTt[:, k, :], k == 0, k == 7, [("Wup", "a", cc // 2)] + hk2, [p1k])
            for k in range(8):
                _mm(S, p2[:, 0:256], Wup[:, k, FF + cc * 128:FF + (cc + 1) * 128], hTt[:, k, :], k == 0, k == 7, [("Wup", "b", cc // 2)] + hk2, [p2k])
            s_, s_k = sg.next()
            S.op("scalar", lambda e, s_=s_, p1=p1: e.activation(s_[:], p1[:, 0:256], AF.Silu), reads=[p1k], writes=[s_k])
            S.op("vector", lambda e, at=at, cc=cc, s_=s_, p2=p2: e.tensor_tensor(at[:, cc, :], s_[:], p2[:, 0:256], ALU.mult),
                 reads=[s_k, p2k], writes=[(atk, cc)])
        for t in range(2):
            i = mt * 2 + t
            xt, xk, st, sk = xts[t]
            for n in range(2):
                p_, pok = po.next()
                for cc in range(NFC):
                    _mm(S, p_[:], at[:, cc, t * 128:(t + 1) * 128], Wdn[:, cc, n * 512:(n + 1) * 512], cc == 0, cc == NFC - 1,
                        [(atk, cc), ("Wdn", cc)], [pok])
                S.op("vector", lambda e, xt=xt, p_=p_, n=n: e.tensor_tensor(xt[:, n * 512:(n + 1) * 512], xt[:, n * 512:(n + 1) * 512], p_[:], ALU.add),
                     reads=[pok, xk], writes=[xk])
            if gf is not None:
                S.op("gpsimd", lambda e, st=st: e.memset(st[:], 0.0), reads=[sk], writes=[sk])
                S.op("scalar", lambda e, xt=xt, st=st: e.activation(W["junk"][:], xt[:], AF.Square, accum_out=st[:, 0:1]),
                     reads=[xk, sk], writes=["junk", sk])
                S.op("vector", lambda e, st=st: e.tensor_scalar(st[:, 1:2], st[:, 0:1], 1.0 / DM, 1e-6, ALU.mult, ALU.add), reads=[sk], writes=[sk])
                S.op("scalar", lambda e, st=st: e.activation(st[:, 3:4], st[:, 1:2], AF.Sqrt), reads=[sk], writes=[sk])
                S.op("vector", lambda e, st=st: e.reciprocal(st[:, 2:3], st[:, 3:4]), reads=[sk], writes=[sk])
                S.op("vector", lambda e, xt=xt, st=st: e.scalar_tensor_tensor(xt[:], xt[:], st[:, 2:3], gf[0][:], ALU.mult, ALU.mult),
                     reads=[xk, sk, gf[1]], writes=[xk])
            S.dma("gpsimd", xout(i), xt[:], reads=[xk], writes=[xout_key(i)])
    S.pop()


def l1_norm(S, c, Lb, xin, xin_keys):
    S.push()
    g1 = load_gain(S, c, "gmix1", 2)
    gk = load_gain(S, c, "gkv", 3)
    W = norm_work(S, "p6")
    xring = Ring(S, "p6x", [128, DM], F32, 3)
    for i in range(NT):
        xt, xk, st, sk, outs = norm_tile(S, c, W, xin(i), xin_keys(i), [g1, gk], xring)
        transpose_tile(S, c, W, outs[0][0], outs[0][1], Lb["HT"][:, :, i * 128:(i + 1) * 128], ("HT", i), eng="scalar")
        transpose_tile(S, c, W, outs[1][0], outs[1][1], Lb["HKT"][:, :, i * 128:(i + 1) * 128], ("HKT", i), eng="vector")
    S.pop()


def l1_group(S, c, Lb, gi):
    win, d = DILS[gi]
    L = SL // d
    tpc = L // 128
    S.push()
    Wq = S.sb("Wq", [128, 8, DM], BF16)
    Wkv = S.sb("Wkv", [128, 8, 512], BF16)
    for k in range(8):
        S.dma("sync", Wq[:, k, :], c.w_q_b[k * 128:(k + 1) * 128, gi * DM:(gi + 1) * DM], reads=[], writes=[("Wq", k)])
        S.dma("sync", Wkv[:, k, :], c.kv_w_b[k * 128:(k + 1) * 128, gi * 512:(gi + 1) * 512], reads=[], writes=[("Wkv", k)])
    MD = S.sb("MD", [128, 16, 256], BF16)
    for h in range(16):
        S.dma("sync", MD[:, h, :], c.md_b[gi * 16 + h], reads=[], writes=[("MD", h)])
    QT = S.sb("QT1", [128, 2, 4, SL], BF16)
    KT = S.sb("KT1", [128, 2, SL], BF16)
    Vp = S.sb("Vp1", [128, NT, 4, 65], BF16)
    S.op("gpsimd", lambda e: e.memset(Vp[:, :, :, 64:65], 1.0), writes=["Vp1_1"])
    pfm = Ring(S, "gpfm", [128, 512], F32, 2, psum=True)
    ptm = Ring(S, "gptm", [128, 512], F32, 1, psum=True)
    nev = 0
    for mt in range(4):
        ts = slice(mt * 512, (mt + 1) * 512)
        l0, ln = mt * 512 // d, 512 // d
        for fc in range(10):
            pt, pk = pfm.next()
            for k in range(8):
                if fc < 8:
                    _mm(S, pt[:], Wq[:, k, fc * 128:(fc + 1) * 128], Lb["HT"][:, k, ts], k == 0, k == 7, [("Wq", k)], [pk])
                else:
                    _mm(S, pt[:], Wkv[:, k, (fc - 8) * 128:(fc - 7) * 128], Lb["HKT"][:, k, ts], k == 0, k == 7, [("Wkv", k)], [pk])
            if fc < 8:
                dst = QT[:, fc // 4, fc % 4, :]
                sc = 0.125
            else:
                dst = KT[:, fc - 8, :]
                sc = 1.0
            dst = dst.rearrange("p (c l) -> p c l", c=d)[:, :, l0:l0 + ln]
            src = pt[:].rearrange("p (l c) -> p c l", c=d)
            if nev % 2 == 0:
                S.op("scalar", lambda e, dst=dst, src=src, sc=sc: e.mul(dst, src, sc), reads=[pk], writes=[("fm1", fc, mt)])
            else:
                S.op("vector", lambda e, dst=dst, src=src, sc=sc: e.tensor_scalar(dst, src, sc, None, ALU.mult), reads=[pk], writes=[("fm1", fc, mt)])
            nev += 1
    for T in range(NT):
        cls, lb = T // tpc, T % tpc
        start = cls + d * 128 * lb
        pt, pk = ptm.next()
        for k in range(8):
            _mm(S, pt[:, 0:256], Lb["HKT"][:, k, start:start + d * 127 + 1:d], Wkv[:, k, 256:512], k == 0, k == 7, [("Wkv", k)], [pk])
        S.op("vector" if T % 2 else "scalar",
             (lambda e, pt=pt, T=T: e.tensor_copy(Vp[:, T, :, 0:64], pt[:, 0:256].rearrange("p (g e) -> p g e", g=4))) if T % 2 else
             (lambda e, pt=pt, T=T: e.copy(Vp[:, T, :, 0:64], pt[:, 0:256].rearrange("p (g e) -> p g e", g=4))),
             reads=[pk, "Vp1_1"], writes=[("Vp1", T)])
    R = dict(ST=Ring(S, "gST", [128, 4, 128], F32, 2, psum=True), O=Ring(S, "gO", [128, 4, 128], F32, 3, psum=True),
             PT=Ring(S, "gPT", [128, 4, 128], BF16, 3))
    Ost = Ring(S, "gOst", [128, 16, 65], F32, 2)
    MDk = [("MD", h) for h in range(16)]
    allfm = [("fm1", fc, mt) for fc in range(10) for mt in range(4)]
    odv = c.od[gi].rearrange("(l c) f -> c l f", c=d)
    jobs = []
    for T in range(NT):
        cls, lb = T // tpc, T % tpc
        st = {}
        for g in range(4):
            u, j = g // 2, g % 2
            prt = slice(u * 64, (u + 1) * 64)
            items = []
            for TT in ([T - 1] if lb > 0 else []) + [T]:
                dl = T - TT
                items.append((KT[prt, j, TT * 128:(TT + 1) * 128], MD[:, 4 * g:4 * g + 4, dl * 128:(dl + 1) * 128], None, Vp[:, TT, g, :]))

            def pre(st=st):
                st["ost"] = Ost.next()

            def done(O, ok, T=T, g=g, st=st, cls=cls, lb=lb):
                ost, ostk = st["ost"]
                if g % 2 == 0:
                    S.op("vector", lambda e: e.tensor_copy(ost[:, 4 * g:4 * g + 4, :], O[:, :, 0:65]), reads=[ok], writes=[(ostk, g)])
                else:
                    S.op("scalar", lambda e: e.copy(ost[:, 4 * g:4 * g + 4, :], O[:, :, 0:65]), reads=[ok], writes=[(ostk, g)])
                if g == 3:
                    S.dma("gpsimd", odv[cls, lb * 128:(lb + 1) * 128, :], ost[:].rearrange("p h e -> p (h e)"),
                          reads=[(ostk, gg) for gg in range(4)], writes=[("od", gi, T)])

            jobs.append(dict(q=QT[prt, j, :, T * 128:(T + 1) * 128], items=items, pre=pre if g == 0 else None, done=done,
                             qk_reads=allfm, bias_reads=MDk, v_reads=[("Vp1", TT) for TT in range(NT)]))
    run_attention(S, c, R, jobs)
    S.pop()


def l1_merge_get_o(S, c):
    def get_o(i, W):
        if i is None:
            W["ld"] = Ring(S, "mld", [128, 16, 65], F32, 4)
            W["sm"] = Ring(S, "msm", [128, 16, 1], F32, 2)
            W["ob"] = Ring(S, "mob", [128, DM], BF16, 2)
            return
        ts = []
        for gi in range(3):
            t, k = W["ld"].next()
            S.dma("sync", t[:].rearrange("p h e -> p (h e)"), c.od[gi][i * 128:(i + 1) * 128, :],
                  reads=[("od", gi, T) for T in range(NT)], writes=[k])
            ts.append((t, k))
        (t0, k0), (t1, k1), (t2, k2) = ts
        S.op("gpsimd", lambda e: e.tensor_tensor(t0[:], t0[:], t1[:], ALU.add), reads=[k0, k1], writes=[k0])
        S.op("vector", lambda e: e.tensor_tensor(t0[:], t0[:], t2[:], ALU.add), reads=[k0, k2], writes=[k0])
        sm, smk = W["sm"].next()
        S.op("vector", lambda e: e.reciprocal(sm[:], t0[:, :, 64:65]), reads=[k0], writes=[smk])
        ob, obk = W["ob"].next()
        S.op("vector", lambda e: e.tensor_tensor(ob[:].rearrange("p (h e) -> p h e", h=16), t0[:, :, 0:64],
                                                 sm[:].to_broadcast([128, 16, 64]), ALU.mult), reads=[k0, smk], writes=[obk])
        return ob, [obk]
    return get_o


INPUT_SHAPES = dict(
    x=[NSEQ, SL, DM], w_in=[DM, 2608], w1k=[2048, 256], w1v=[2048, 256], w2k=[256, 64], w2v=[256, 64],
    pekT=[64, 32], pevT=[64, 32], w_out0=[DM, DM], w_out1=[DM, DM], w_up0=[DM, 2 * FF], w_up1=[DM, 2 * FF],
    w_dn0=[FF, DM], w_dn1=[FF, DM], kv_w=[DM, 1536], w_q=[DM, 3072], gains=[6, DM], b_gate=[1, 48],
    ms=[16, 128, 2048], mw=[16, 128, 384], mc=[16, 128, 247], md=[48, 128, 256],
    vtab=[128, NT, 32], ctab=[128, NT, 32], ovl=[127, 32], ind=[32, NT, 128],
)
CONV = ("w_in", "w1k", "w1v", "w2k", "w2v", "w_out0", "w_out1", "w_up0", "w_up1", "w_dn0", "w_dn1", "kv_w", "w_q",
        "ms", "mw", "md")


def build_program(nseq=NSEQ, max_phase=10 ** 9):
    nc = bass.Bass("TRN2", target_bir_lowering=False)
    c = Ctx()
    shapes = dict(INPUT_SHAPES)
    shapes["x"] = [nseq, SL, DM]
    for name, shp in shapes.items():
        setattr(c, name, nc.dram_tensor(name, list(shp), F32, kind="ExternalInput").ap())
    y = nc.dram_tensor("y", [nseq, SL, DM], F32, kind="ExternalOutput").ap()
    c.conv_jobs = []
    conv_only = os.environ.get("CONV_ONLY")
    for name in CONV:
        shp = shapes[name]
        dst = nc.dram_tensor(name + "_b", list(shp), BF16).ap()
        setattr(c, name + "_b", dst)
        src = getattr(c, name)
        if conv_only is not None and name not in conv_only.split(","):
            continue
        if len(shp) == 3:
            c.conv_jobs.append((src.rearrange("h p c -> (h p) c"), dst.rearrange("h p c -> (h p) c"), name + "_b"))
        else:
            c.conv_jobs.append((src, dst, name + "_b"))
    c.xres = nc.dram_tensor("xres", [SL, DM], F32).ap()
    c.od = nc.dram_tensor("od", [3, SL, 16 * 65], F32).ap()

    S = Sched(nc)
    c.identf = S.sb("identf", [128, 128], F32)
    c.identb = S.sb("identb", [128, 128], BF16)
    S.op("gpsimd", lambda e: e.memset(c.identf[:], 1.0), writes=["identf"])
    S.op("gpsimd", lambda e: e.affine_select(c.identf[:], c.identf[:], pattern=[[-1, 128]], compare_op=ALU.is_equal,
                                             fill=0.0, base=0, channel_multiplier=1), reads=["identf"], writes=["identf"])
    S.op("vector", lambda e: e.tensor_copy(c.identb[:], c.identf[:]), reads=["identf"], writes=["identb"])
    c.indb = S.sb("indb", [32, NT, 128], BF16)
    for nm, shp in (("vtab", [128, NT, 32]), ("ctab", [128, NT, 32]), ("ovl", [127, 32])):
        t = S.sb(nm + "_s", shp, F32)
        S.dma("sync", t[:], getattr(c, nm), reads=[], writes=[nm])
        setattr(c, nm, t)
    S.push()
    indf = S.sb("indf", [32, NT, 128], F32)
    S.dma("sync", indf[:], c.ind, reads=[], writes=["indf"])
    S.op("vector", lambda e: e.tensor_copy(c.indb[:], indf[:]), reads=["indf"], writes=["indb"])
    S.pop()
    convert_all(S, c)

    cnt = [0]

    def ph(fn, *a, **k):
        cnt[0] += 1
        if cnt[0] <= max_phase:
            fn(*a, **k)

    xres_in = lambda i: c.xres[i * 128:(i + 1) * 128, :]
    xres_k = lambda i: [("xres", i)]
    xres_k1 = lambda i: ("xres", i)
    for b in range(nseq):
        S.push()
        A = dict(QT=S.sb("QT", [128, 2, 4, SL], BF16), KsT=S.sb("KsT", [128, 2, SL], BF16), KwT=S.sb("KwT", [128, 2, SL], BF16),
                 Vs=S.sb("Vs", [128, NT, 4, 65], BF16), Vw=S.sb("Vw", [128, NT, 4, 65], BF16), G=S.sb("G", [128, NT, 48], F32),
                 KcmpT=S.sb("KcmpT", [128, 2, 128], BF16), Vcmp=S.sb("Vcmp", [128, 4, 64], BF16))
        S.push()
        Bk = dict(KcT=S.sb("KcT", [128, 2, SL], BF16), VcT=S.sb("VcT", [128, 2, SL], BF16))
        ph(l0_proj, S, c, A, Bk, b)
        ph(l0_compress, S, c, A, Bk)
        S.pop()
        S.push()
        A["Oall"] = S.sb("Oall", [128, NT, DM], BF16)
        ph(l0_attention, S, c, A, b)
        if os.environ.get("DBG_DUMP") == "Oall":
            S.push()
            dr = Ring(S, "dbgd", [128, DM], F32, 2)
            for i in range(NT):
                t_, k_ = dr.next()
                S.op("vector", lambda e, t_=t_, i=i: e.tensor_copy(t_[:], A["Oall"][:, i, :]), reads=[], writes=[k_])
                S.dma("sync", y[0, i * 128:(i + 1) * 128, :], t_[:], reads=[k_], writes=[("ydump", i)])
            S.pop()
        ph(out_proj, S, c, c.w_out0_b, lambda i, W: (A["Oall"][:, i, :], []) if i is not None else None,
                 lambda i: c.x[b, i * 128:(i + 1) * 128, :], xres_in, lambda i: [], xres_k1)
        S.pop()
        S.pop()
        ph(ffn, S, c, c.w_up0_b, c.w_dn0_b, 1, xres_in, xres_in, xres_k, xres_k1)
        S.push()
        Lb = dict(HT=S.sb("HT", [128, 8, SL], BF16), HKT=S.sb("HKT", [128, 8, SL], BF16))
        ph(l1_norm, S, c, Lb, xres_in, xres_k)
        for gi in range(3):
            ph(l1_group, S, c, Lb, gi)
        S.pop()
        ph(out_proj, S, c, c.w_out1_b, l1_merge_get_o(S, c), xres_in, xres_in, xres_k, xres_k1)
        ph(ffn, S, c, c.w_up1_b, c.w_dn1_b, 4, xres_in, lambda i: y[b, i * 128:(i + 1) * 128, :], xres_k, lambda i: ("y", b, i),
            final_row=5)
    if os.environ.get("DBG_DUMP"):
        S.finish(final_keys=[])
    elif max_phase < 10 ** 9:
        S.dma("sync", y[0], c.xres, reads=[("xres", i) for i in range(NT)], writes=["ydbg"])
        S.finish(final_keys=["ydbg"])
    else:
        S.finish(final_keys=[("y", b, i) for b in range(nseq) for i in range(NT)])
    return nc


def _t5_bucket(dist):
    d = np.maximum(dist, 0)
    lr = np.log(np.maximum(d, 16).astype(np.float32) / np.float32(16)) / np.float32(np.log(2048.0 / 16.0))
    large = 16 + (lr * np.float32(16)).astype(np.int32)
    return np.where(d < 16, d, np.minimum(large, 31)).astype(np.int64)


def _qperm():
    cols = []
    for j in range(2):
        for r in range(4):
            for g in (j, 2 + j):
                h = 4 * g + r
                cols.extend(range(h * 64, h * 64 + 64))
    return np.array(cols)


def _kperm():
    cols = []
    for j in range(2):
        for g in (j, 2 + j):
            cols.extend(range(g * 64, g * 64 + 64))
    return np.array(cols)


def host_prep(inp):
    f = lambda a: np.ascontiguousarray(np.asarray(a, dtype=np.float32))
    qp, kp = _qperm(), _kperm()
    nat = np.arange(256)
    w_in = f(inp["a_w_in"])[0]
    col = np.concatenate([qp, 1024 + kp, 1280 + kp, 1536 + kp, 2048 + kp, 1792 + nat, 2304 + nat, 2560 + np.arange(48)])
    sh = {}
    sh["w_in"] = f(w_in[:, col])
    sh["w1k"] = f(inp["a_w1_k"])[0]
    sh["w1v"] = f(inp["a_w1_v"])[0]
    sh["w2k"] = f(inp["a_w2_k"])[0]
    sh["w2v"] = f(inp["a_w2_v"])[0]
    sh["pekT"] = f(f(inp["a_pe_k"])[0].T)
    sh["pevT"] = f(f(inp["a_pe_v"])[0].T)
    sh["w_out0"] = f(inp["a_w_out"])[0]
    sh["w_out1"] = f(inp["b_w_out"])[0]
    sh["w_up0"], sh["w_up1"] = f(inp["ffn_w_up"])[0], f(inp["ffn_w_up"])[1]
    sh["w_dn0"], sh["w_dn1"] = f(inp["ffn_w_down"])[0], f(inp["ffn_w_down"])[1]
    kvw = f(inp["kv_w"])
    sh["kv_w"] = f(kvw[:, np.concatenate([np.concatenate([gi * 512 + kp, gi * 512 + 256 + nat]) for gi in range(3)])])
    wq = f(inp["b_w_q"])[0]
    sh["w_q"] = f(wq[:, np.concatenate([gi * 1024 + qp for gi in range(3)])])
    nm, nf = f(inp["norm_mix"]), f(inp["norm_ffn"])
    sh["gains"] = f(np.stack([nm[0], nf[0], nm[1], f(inp["kv_norm"]), nf[1], f(inp["final_norm"])]))
    sh["b_gate"] = f(inp["a_b_gate"]).reshape(1, 48)
    rb = f(inp["rel_bias"])
    rbT = np.ascontiguousarray(rb.T)
    p = np.arange(128)[:, None]

    def toep(ncol, maxd, scale):
        cc = np.arange(ncol)[None, :]
        dist = cc - p
        ok = (dist >= 0) & (dist <= maxd)
        g = rbT[:, _t5_bucket(dist * scale)]
        return f(np.where(ok[None], g, np.float32(MASKV)))

    sh["ms"] = toep(2048, 1 << 30, 1)
    sh["mw"] = toep(384, 255, 1)
    sh["md"] = f(np.concatenate([toep(256, 128, d) for (_, d) in DILS], axis=0))
    m = np.arange(247)[None, :]
    dist = p - 16 * (m - 120) - 31
    sh["mc"] = f(np.where((dist >= 0)[None], rbT[:, _t5_bucket(dist)], np.float32(MASKV)))
    i = np.arange(NT)[None, :, None]
    jb = np.arange(32)[None, None, :]
    cur = 2 * i + (np.arange(128)[:, None, None] // 64)
    valid = jb <= cur
    forced = (jb == 0) | (jb == cur) | (jb == cur - 1)
    sh["vtab"] = f(valid.astype(np.float32))
    sh["ctab"] = f(np.where(forced, 100.0, np.where(valid, 0.0, -1.0)))
    ci = np.arange(127)[:, None] * 16
    sj = np.arange(32)[None, :] * 64
    sh["ovl"] = f(((ci < sj + 64) & (ci + 32 > sj)).astype(np.float32))
    cblk = np.arange(32)[:, None, None]
    sh["ind"] = f((cblk == 2 * np.arange(NT)[None, :, None] + np.arange(128)[None, None, :] // 64).astype(np.float32))
    return sh


_PROGRAM = {}


def kernel(**inputs):
    x = np.asarray(inputs["x"], dtype=np.float32)
    sh = host_prep(inputs)
    if "nc" not in _PROGRAM:
        _PROGRAM["nc"] = build_program()
    nc = _PROGRAM["nc"]
    in_maps = []
    for cid in range(NCORES):
        d = dict(sh)
        d["x"] = np.ascontiguousarray(x[cid * NSEQ:(cid + 1) * NSEQ])
        in_maps.append(d)
    res = run_bass_kernel_spmd(nc, in_maps, core_ids=list(range(NCORES)))
    return np.concatenate([np.asarray(r["y"], dtype=np.float32) for r in res.results], axis=0)
```

```python
import os
import sys
import numpy as np
from contextlib import ExitStack
import concourse.bass as bass
import concourse.mybir as mybir
from concourse.bass_utils import run_bass_kernel_spmd

F32 = mybir.dt.float32
BF16 = mybir.dt.bfloat16
AF = mybir.ActivationFunctionType
ALU = mybir.AluOpType
AX = mybir.AxisListType

ENGS = ("tensor", "vector", "scalar", "gpsimd", "sync")
SEM_ROT = 6000


class Sched:
    def __init__(self, nc, n_dma_sems=(("sync", 20), ("gpsimd", 12), ("scalar", 6))):
        self.nc = nc
        self.ops = []
        self.top = ExitStack()
        self.scopes = [self.top]
        self.last_w = {}
        self.readers = {}
        self.npos = {e: 0 for e in ENGS}
        self.know = {e: {f: 0 for f in ENGS} for e in ENGS}
        self.dma_known = {e: set() for e in ENGS}
        self.done = []
        self.dma_sems = {}
        for e, n in n_dma_sems:
            self.dma_sems[e] = [[self._sem(f"d_{e}_{i}"), 0, None] for i in range(n)]
        self.dma_rr = {e: 0 for e in self.dma_sems}
        self.eng_sem = {e: [self._sem(f"c_{e}_0"), 0, 0] for e in ENGS}
        self.sig = {}
        self.psum_keys = set()
        self.block = None

    def _sem(self, name):
        return self.top.enter_context(self.nc.semaphore(name))

    def push(self):
        self.scopes.append(ExitStack())

    def pop(self):
        self.flush(barrier=True)
        self.scopes.pop().close()

    def _uniq(self, name):
        self._cnt = getattr(self, "_cnt", 0) + 1
        return f"{name}__{self._cnt}"

    def sb(self, name, shape, dtype):
        return self.scopes[-1].enter_context(self.nc.sbuf_tensor(self._uniq(name), list(shape), dtype))

    def ps(self, name, shape, dtype):
        return self.scopes[-1].enter_context(self.nc.psum_tensor(self._uniq(name), list(shape), dtype))

    def op(self, eng, fn, reads=(), writes=()):
        pw = tuple(k for k in reads if k in self.psum_keys and k not in writes)
        fr = sys._getframe(1)
        if fr.f_code.co_name in ("_mm", "_tr"):
            fr = fr.f_back
        self.ops.append(dict(eng=eng, fn=fn, reads=tuple(reads), writes=tuple(writes) + pw, wtrue=tuple(writes), dma=False,
                             site=(fr.f_code.co_name, fr.f_lineno)))

    def dma(self, eng, out, in_, reads=(), writes=(), **kw):
        self.ops.append(dict(eng=eng, fn=lambda e: e.dma_start(out=out, in_=in_, **kw),
                             reads=tuple(reads), writes=tuple(writes), wtrue=tuple(writes), dma=True,
                             site=(sys._getframe(1).f_code.co_name, sys._getframe(1).f_lineno)))

    def flush(self, barrier=False, final_keys=None):
        ops = self.ops
        self.ops = []
        base = len(self.done)
        for i, o in enumerate(ops):
            gid = base + i
            deps = set()
            for k in o["reads"]:
                if k in self.last_w:
                    deps.add(self.last_w[k])
            for k in o["writes"]:
                if k in self.last_w:
                    deps.add(self.last_w[k])
                deps.update(self.readers.get(k, ()))
            deps.discard(gid)
            o["deps"] = deps
            o["gid"] = gid
            for k in o["reads"]:
                self.readers.setdefault(k, []).append(gid)
            for k in o["writes"]:
                self.last_w[k] = gid
                self.readers[k] = []
            self.done.append(o)
        for o in ops:
            e = o["eng"]
            self.npos[e] += 1
            o["pos"] = self.npos[e]
            waits = []
            know = self.know[e]
            for d in sorted(o["deps"]):
                dd = self.done[d]
                if dd["dma"]:
                    if d in self.dma_known[e]:
                        continue
                    self.dma_known[e].add(d)
                    waits.append(d)
                else:
                    f = dd["eng"]
                    if f == e:
                        raw = any(k in dd["wtrue"] for k in o["reads"])
                        if not raw or know[f] >= dd["pos"]:
                            continue
                    if know[f] >= dd["pos"]:
                        continue
                    waits.append(d)
                    know[f] = max(know[f], dd["pos"])
                    for g, v in dd["ksnap"].items():
                        if v > know[g]:
                            know[g] = v
            o["waits"] = waits
            o["ksnap"] = dict(know)
            for d in waits:
                self.done[d]["signal"] = True
        if final_keys is not None:
            self.final_wait = []
            for k in final_keys:
                d = self.last_w[k]
                self.done[d]["signal"] = True
                self.final_wait.append(d)
        if barrier:
            self._barrier_prepare(ops)
        by_eng = {e: [o for o in ops if o["eng"] == e] for e in ENGS}
        self._emit(by_eng, barrier, final_keys is not None)

    def _barrier_prepare(self, ops):
        self.bar_targets = []
        for e in ENGS:
            lst = [o for o in ops if o["eng"] == e and not o["dma"]]
            if lst:
                lst[-1]["signal"] = True
                self.bar_targets.append(lst[-1]["gid"])
        for o in ops:
            if o["dma"]:
                o["signal"] = True
                self.bar_targets.append(o["gid"])

    def _assign_signal(self, o):
        e = o["eng"]
        if o["dma"]:
            return None
        if not o.get("signal"):
            return None
        rec = self.eng_sem[e]
        if rec[1] >= SEM_ROT:
            rec[2] += 1
            rec[0] = self._sem(f"c_{e}_{rec[2]}")
            rec[1] = 0
        rec[1] += 1
        self.sig[o["gid"]] = (rec[0], rec[1])
        return (rec[0], 1)

    def _emit(self, by_eng, barrier, final):
        nc = self.nc
        for e in ENGS:
            for o in by_eng[e]:
                if not o["dma"]:
                    o["_sig"] = self._assign_signal(o)
        for e in ENGS:
            for o in by_eng[e]:
                if o["dma"]:
                    pool = self.dma_sems[e]
                    idx = self.dma_rr[e]
                    self.dma_rr[e] = (idx + 1) % len(pool)
                    rec = pool[idx]
                    o["_prev"] = (rec[0], rec[1]) if rec[1] > 0 else None
                    rec[1] += 16
                    rec[2] = o["gid"]
                    self.sig[o["gid"]] = (rec[0], rec[1])
                    o["_sig"] = (rec[0], 16)
        bar = list(self.bar_targets) if barrier else []
        fin = list(self.final_wait) if final else []

        def run(e):
            def body(eng):
                for o in by_eng[e]:
                    for d in o["waits"]:
                        s, v = self.sig[d]
                        eng.wait_ge(s, v)
                    if o["dma"] and o["_prev"] is not None:
                        eng.wait_ge(*o["_prev"])
                    try:
                        ins = o["fn"](eng)
                    except Exception:
                        print("EMIT FAILED at", o.get("site"), flush=True)
                        raise
                    if o["_sig"] is not None:
                        ins.then_inc(o["_sig"][0], o["_sig"][1])
                for d in bar:
                    s, v = self.sig[d]
                    eng.wait_ge(s, v)
                if e == "sync":
                    for d in fin:
                        s, v = self.sig[d]
                        eng.wait_ge(s, v)
            return body

        with nc.Block() as block:
            for e in ENGS:
                getattr(block, e)(run(e))
        if barrier:
            for e in ENGS:
                for f in ENGS:
                    self.know[e][f] = self.npos[f]
                self.dma_known[e].update(o["gid"] for o in self.done if o["dma"])

    def finish(self, final_keys):
        self.flush(barrier=False, final_keys=final_keys)
        while len(self.scopes) > 1:
            self.scopes.pop().close()
        self.top.close()


NCORES = 8
NSEQ = 4
SL = 2048
DM = 1024
NT = 16
FF = 2816
NFC = 22
MASKV = -30000.0
DILS = ((128, 1), (512, 4), (2048, 16))
GELU_C = 1.5957691216057308


class Ring:
    def __init__(self, S, name, shape, dtype, n, psum=False):
        mk = S.ps if psum else S.sb
        self.t = [mk(f"{name}{k}", shape, dtype) for k in range(n)]
        self.k = [f"{name}{k}" for k in range(n)]
        if psum:
            nbytes = int(np.prod(shape[1:])) * (4 if dtype == F32 else 2)
            assert nbytes == 2048, (name, shape)
            S.psum_keys.update(self.k)
        self.n = n
        self.i = 0

    def next(self):
        j = self.i % self.n
        self.i += 1
        return self.t[j], self.k[j]


class Ctx:
    pass


def _mm(S, out, lhsT, rhs, start, stop, reads, writes, sgc=False):
    S.op("tensor", lambda e: e.matmul(out, lhsT, rhs, start=start, stop=stop, skip_group_check=sgc), reads=reads, writes=writes)


def _tr(S, out, in_, ident, reads, writes):
    S.op("tensor", lambda e: e.transpose(out, in_, ident), reads=reads, writes=writes)


def norm_tile(S, c, W, src_ap, src_keys, gains, xring):
    xt, xk = xring.next()
    S.dma("sync", xt[:], src_ap, reads=src_keys, writes=[xk])
    st, sk = W["ss"].next()
    S.op("gpsimd", lambda e: e.memset(st[:], 0.0), writes=[sk])
    S.op("scalar", lambda e: e.activation(W["junk"][:], xt[:], AF.Square, accum_out=st[:, 0:1]),
         reads=[xk, sk], writes=["junk", sk])
    S.op("vector", lambda e: e.tensor_scalar(st[:, 1:2], st[:, 0:1], 1.0 / DM, 1e-6, ALU.mult, ALU.add),
         reads=[sk], writes=[sk])
    S.op("scalar", lambda e: e.activation(st[:, 3:4], st[:, 1:2], AF.Sqrt), reads=[sk], writes=[sk])
    S.op("vector", lambda e: e.reciprocal(st[:, 2:3], st[:, 3:4]), reads=[sk], writes=[sk])
    outs = []
    for (gt, gk) in gains:
        hb, hk = W["hb"].next()
        S.op("vector", lambda e, hb=hb, gt=gt: e.scalar_tensor_tensor(hb[:], xt[:], st[:, 2:3], gt[:], ALU.mult, ALU.mult),
             reads=[xk, sk, gk], writes=[hk])
        outs.append((hb, hk))
    return xt, xk, st, sk, outs


def transpose_tile(S, c, W, hb, hk, dst_ap, dst_key, eng="scalar"):
    pt, pk = W["ptr"].next()
    for k in range(8):
        _tr(S, pt[:, k, :], hb[:, k * 128:(k + 1) * 128], c.identb[:], [hk, "identb"], [pk])
    if eng == "scalar":
        S.op("scalar", lambda e: e.copy(dst_ap, pt[:]), reads=[pk], writes=[dst_key])
    else:
        S.op("vector", lambda e: e.tensor_copy(dst_ap, pt[:]), reads=[pk], writes=[dst_key])


def load_gain(S, c, name, row):
    t = S.sb(name, [128, DM], F32)
    S.dma("sync", t[:], c.gains[row].partition_broadcast(128), reads=[], writes=[name])
    return t, name


def norm_work(S, pfx):
    return dict(ss=Ring(S, pfx + "ss", [128, 4], F32, 4), hb=Ring(S, pfx + "hb", [128, DM], BF16, 3),
                junk=S.sb(pfx + "junk", [128, DM], BF16), ptr=Ring(S, pfx + "ptr", [128, 8, 128], BF16, 2, psum=True))


def convert_all(S, c):
    S.push()
    NB = 4
    CW = 2048
    st = [S.sb(f"cvf{k}", [128, CW], F32) for k in range(NB)]
    sb = [S.sb(f"cvb{k}", [128, CW], BF16) for k in range(NB)]
    jobs = []
    for (src, dst, key) in c.conv_jobs:
        R, C = src.shape
        assert R % 128 == 0, (key, R)
        for r0 in range(0, R, 128):
            for c0 in range(0, C, CW):
                w = min(CW, C - c0)
                jobs.append((src[r0:r0 + 128, c0:c0 + w], dst[r0:r0 + 128, c0:c0 + w], w, key))
    for n, (src, dst, w, key) in enumerate(jobs):
        k = n % NB
        S.dma("sync", st[k][:, :w], src, reads=[], writes=[f"cvf{k}"])
        if n % 2 == 0:
            S.op("vector", lambda e, k=k, w=w: e.tensor_copy(sb[k][:, :w], st[k][:, :w]),
                 reads=[f"cvf{k}"], writes=[f"cvb{k}"])
        else:
            S.op("scalar", lambda e, k=k, w=w: e.copy(sb[k][:, :w], st[k][:, :w]),
                 reads=[f"cvf{k}"], writes=[f"cvb{k}"])
        S.dma("gpsimd", dst, sb[k][:, :w], reads=[f"cvb{k}"], writes=[key])
    S.pop()


def l0_proj(S, c, A, Bk, b):
    S.push()
    Win = S.sb("Win", [128, 8, 2608], BF16)
    for k in range(8):
        S.dma("sync", Win[:, k, :], c.w_in_b[k * 128:(k + 1) * 128, :], reads=[], writes=[("Win", k)])
    gmix = load_gain(S, c, "gmix0", 0)
    bg = S.sb("bgate", [128, 48], F32)
    S.dma("sync", bg[:], c.b_gate[0].partition_broadcast(128), reads=[], writes=["bgate"])
    W = norm_work(S, "p1")
    xring = Ring(S, "p1x", [128, DM], F32, 3)
    hT = Ring(S, "p1hT", [128, 8, 512], BF16, 2)
    pfm = Ring(S, "p1pfm", [128, 512], F32, 3, psum=True)
    ptm = Ring(S, "p1ptm", [128, 512], F32, 2, psum=True)
    glt = Ring(S, "p1gl", [128, 48], F32, 2)
    S.op("gpsimd", lambda e: e.memset(A["Vs"][:, :, :, 64:65], 1.0), writes=["Vs1"])
    S.op("gpsimd", lambda e: e.memset(A["Vw"][:, :, :, 64:65], 1.0), writes=["Vw1"])
    fmdst = []
    for fc in range(8):
        fmdst.append((A["QT"], fc // 4, fc % 4, 0.125))
    for nm in ("KcT", "VcT"):
        for j in range(2):
            fmdst.append((Bk[nm], j, None, 1.0))
    for nm in ("KsT", "KwT"):
        for j in range(2):
            fmdst.append((A[nm], j, None, 1.0))
    nev = 0
    for mt in range(4):
        hTt, hTk = hT.next()
        for t in range(4):
            i = mt * 4 + t
            xt, xk, st, sk, outs = norm_tile(S, c, W, c.x[b, i * 128:(i + 1) * 128, :], [], [gmix], xring)
            transpose_tile(S, c, W, outs[0][0], outs[0][1], hTt[:, :, t * 128:(t + 1) * 128], (hTk, t))
        allk = [(hTk, t) for t in range(4)]
        for fc in range(16):
            pt, pk = pfm.next()
            for k in range(8):
                _mm(S, pt[:], Win[:, k, fc * 128:(fc + 1) * 128], hTt[:, k, :], k == 0, k == 7,
                    [("Win", k)] + allk, [pk])
            T, j, r, sc = fmdst[fc]
            dst = T[:, j, r, mt * 512:(mt + 1) * 512] if r is not None else T[:, j, mt * 512:(mt + 1) * 512]
            if nev % 2 == 0:
                S.op("scalar", lambda e, dst=dst, pt=pt, sc=sc: e.mul(dst, pt[:], sc), reads=[pk], writes=[("fm", fc, mt)])
            else:
                S.op("vector", lambda e, dst=dst, pt=pt, sc=sc: e.tensor_scalar(dst, pt[:], sc, None, ALU.mult),
                     reads=[pk], writes=[("fm", fc, mt)])
            nev += 1
        for t in range(4):
            i = mt * 4 + t
            pt, pk = ptm.next()
            for k in range(8):
                _mm(S, pt[:], hTt[:, k, t * 128:(t + 1) * 128], Win[:, k, 2048:2560], k == 0, k == 7,
                    [("Win", k), (hTk, t)], [pk])
            S.op("vector", lambda e, pt=pt, i=i: e.tensor_copy(A["Vs"][:, i, :, 0:64], pt[:, 0:256].rearrange("p (g d) -> p g d", g=4)),
                 reads=[pk, "Vs1"], writes=[("Vs", i)])
            S.op("scalar", lambda e, pt=pt, i=i: e.copy(A["Vw"][:, i, :, 0:64], pt[:, 256:512].rearrange("p (g d) -> p g d", g=4)),
                 reads=[pk, "Vw1"], writes=[("Vw", i)])
            pt2, pk2 = ptm.next()
            for k in range(8):
                _mm(S, pt2[:, 0:48], hTt[:, k, t * 128:(t + 1) * 128], Win[:, k, 2560:2608], k == 0, k == 7,
                    [("Win", k), (hTk, t)], [pk2])
            gt, gk = glt.next()
            S.op("vector", lambda e, gt=gt, pt2=pt2: e.tensor_tensor(gt[:], pt2[:, 0:48], bg[:], ALU.add),
                 reads=[pk2, "bgate"], writes=[gk])
            S.op("scalar", lambda e, gt=gt, i=i: e.activation(A["G"][:, i, :], gt[:], AF.Sigmoid),
                 reads=[gk], writes=[("G", i)])
    S.pop()


def l0_compress(S, c, A, Bk):
    S.push()
    pgel = Ring(S, "p2ph", [128, 512], F32, 2, psum=True)
    pmisc = Ring(S, "p2pm", [128, 512], F32, 2, psum=True)
    for kv, (w1b, w2b, peT, src) in enumerate(((c.w1k_b, c.w2k_b, c.pekT, Bk["KcT"]), (c.w1v_b, c.w2v_b, c.pevT, Bk["VcT"]))):
        W1 = S.sb(f"W1_{kv}", [128, 32, 256], BF16)
        for u in range(2):
            S.dma("sync", W1[u * 64:(u + 1) * 64, :, :], w1b.rearrange("(t d) n -> d t n", d=64), reads=[], writes=[(f"W1_{kv}", u)])
        W2 = S.sb(f"W2_{kv}", [128, 2, 128], BF16)
        for h in range(2):
            S.dma("sync", W2[:, :, h * 64:(h + 1) * 64], w2b.rearrange("(c p) n -> p c n", p=128), reads=[], writes=[(f"W2_{kv}", h)])
        pef = S.sb(f"pef_{kv}", [64, 32], F32)
        S.dma("sync", pef[:], peT, reads=[], writes=[f"pef_{kv}"])
        peb = S.sb(f"peb_{kv}", [64, 32], BF16)
        S.op("vector", lambda e, peb=peb, pef=pef: e.tensor_copy(peb[:], pef[:]), reads=[f"pef_{kv}"], writes=[f"peb_{kv}"])
        pb, pbk = pmisc.next()
        for c2 in range(2):
            for t in range(32):
                _mm(S, pb[:, c2:c2 + 1], W1[0:64, t, c2 * 128:(c2 + 1) * 128], peb[0:64, t:t + 1], t == 0, t == 31,
                    [(f"W1_{kv}", 0), f"peb_{kv}"], [pbk])
        pbias = S.sb(f"pbias_{kv}", [128, 2], F32)
        S.op("vector", lambda e, pbias=pbias, pb=pb: e.tensor_copy(pbias[:], pb[:, 0:2]), reads=[pbk], writes=[f"pbias_{kv}"])
        gelT = Ring(S, f"p2gel{kv}", [128, 2, 128], BF16, 2)
        tmp = Ring(S, f"p2tmp{kv}", [128, 4, 128], F32, 2)
        for g in range(4):
            u, j = g // 2, g % 2
            prt = slice(u * 64, (u + 1) * 64)
            gl, glk = gelT.next()
            for c2 in range(2):
                ph, phk = pgel.next()
                for t in range(32):
                    _mm(S, ph[:, 0:127], W1[prt, t, c2 * 128:(c2 + 1) * 128], src[prt, j, t:t + 16 * 126 + 1:16], t == 0, t == 31,
                        [(f"W1_{kv}", u)], [phk])
                tm, tk = tmp.next()
                S.op("vector", lambda e, tm=tm, ph=ph, c2=c2, pbias=pbias: e.tensor_scalar(tm[:, 0, 0:127], ph[:, 0:127], pbias[:, c2:c2 + 1], None, ALU.add),
                     reads=[phk, f"pbias_{kv}"], writes=[(tk, 0)])
                S.op("vector", lambda e, tm=tm: e.tensor_tensor(tm[:, 1, 0:127], tm[:, 0, 0:127], tm[:, 0, 0:127], ALU.mult),
                     reads=[(tk, 0)], writes=[(tk, 1)])
                S.op("vector", lambda e, tm=tm: e.tensor_scalar(tm[:, 1, 0:127], tm[:, 1, 0:127], 0.044715, 1.0, ALU.mult, ALU.add),
                     reads=[(tk, 1)], writes=[(tk, 1)])
                S.op("vector", lambda e, tm=tm: e.tensor_tensor(tm[:, 2, 0:127], tm[:, 1, 0:127], tm[:, 0, 0:127], ALU.mult),
                     reads=[(tk, 1), (tk, 0)], writes=[(tk, 2)])
                S.op("scalar", lambda e, tm=tm: e.activation(tm[:, 3, 0:127], tm[:, 2, 0:127], AF.Sigmoid, scale=GELU_C),
                     reads=[(tk, 2)], writes=[(tk, 3)])
                S.op("vector", lambda e, tm=tm, gl=gl, c2=c2: e.tensor_tensor(gl[:, c2, 0:127], tm[:, 3, 0:127], tm[:, 0, 0:127], ALU.mult),
                     reads=[(tk, 3), (tk, 0)], writes=[(glk, c2)])
            po, pok = pmisc.next()
            if kv == 0:
                for c2 in range(2):
                    _mm(S, po[:, 0:127], W2[:, c2, :], gl[:, c2, 0:127], c2 == 0, c2 == 1,
                        [(f"W2_{kv}", 0), (f"W2_{kv}", 1), (glk, c2)], [pok])
                S.op("scalar", lambda e, po=po, prt=prt, j=j: e.copy(A["KcmpT"][prt, j, 0:127], po[prt, 0:127]),
                     reads=[pok], writes=[("KcmpT", g)])
            else:
                for c2 in range(2):
                    _mm(S, po[0:127, 0:64], gl[:, c2, 0:127], W2[:, c2, 0:64], c2 == 0, c2 == 1,
                        [(f"W2_{kv}", 0), (glk, c2)], [pok])
                S.op("scalar", lambda e, po=po, g=g: e.copy(A["Vcmp"][0:127, g, :], po[0:127, 0:64]),
                     reads=[pok], writes=[("Vcmp", g)])
    S.pop()


def run_attention(S, c, R, jobs):
    flat = []
    for jb in jobs:
        n = len(jb["items"])
        for m in range(n):
            flat.append((jb, m, n))

    gens = [jb for jb in jobs if jb.get("pregen")]
    pending = []

    def step(g_):
        try:
            next(g_)
            return True
        except StopIteration:
            return False

    def qk(ent):
        jb, m, n = ent
        if m == 0:
            if jb.get("pre"):
                jb["pre"]()
            if jb.get("pregen"):
                if "gen" not in jb:
                    jb["gen"] = jb["pregen"]()
                while step(jb["gen"]):
                    pass
                if jb["gen"] in pending:
                    pending.remove(jb["gen"])
                k = gens.index(jb)
                if k + 1 < len(gens):
                    nj = gens[k + 1]
                    nj["gen"] = nj["pregen"]()
                    pending.append(nj["gen"])
            jb["O"] = R["O"].next()
        elif pending:
            if not step(pending[0]):
                pending.pop(0)
        kT, bias, mask, v = jb["items"][m]
        ST, sk = R["ST"].next()
        _mm(S, ST[:], kT, jb["q"], True, False, jb.get("qk_reads", []), [sk])
        _mm(S, ST[:], c.identb[:], bias, False, mask is None, ["identb"] + jb.get("bias_reads", []), [sk])
        if mask is not None:
            _mm(S, ST[:], mask[0], mask[1], False, True, [mask[2]], [sk])
        return ST, sk

    cur = qk(flat[0]) if flat else None
    for idx, ent in enumerate(flat):
        jb, m, n = ent
        nxt = qk(flat[idx + 1]) if idx + 1 < len(flat) else None
        ST, sk = cur
        PT, pk = R["PT"].next()
        S.op("scalar", lambda e, PT=PT, ST=ST: e.activation(PT[:], ST[:], AF.Exp), reads=[sk], writes=[pk])
        O, ok = jb["O"]
        v = jb["items"][m][3]
        for r in range(4):
            _mm(S, O[:, r, 0:65], PT[:, r, :], v, m == 0 and r == 0, m == n - 1, [pk] + jb.get("v_reads", []), [ok], sgc=True)
        if m == n - 1:
            jb["done"](O, ok)
        cur = nxt


def l0_attention(S, c, A, b):
    S.push()
    R = dict(ST=Ring(S, "aST", [128, 4, 128], F32, 2, psum=True), O=Ring(S, "aO", [128, 4, 128], F32, 3, psum=True),
             PT=Ring(S, "aPT", [128, 4, 128], BF16, 3))
    cmpS = S.ps("cmpS", [128, 4, 128], F32)
    misc = S.ps("amisc", [128, 512], F32)
    tpB = S.ps("atpB", [128, 8, 128], BF16)
    S.psum_keys.update(["cmpS", "amisc", "tpB"])
    MS = S.sb("MS", [128, 4, 2048], BF16)
    MW = S.sb("MW", [128, 4, 384], BF16)
    MC = S.sb("MC", [128, 4, 247], F32)
    Sc = Ring(S, "aSc", [128, 4, 127], F32, 2)
    Pn = Ring(S, "aPn", [128, 4, 127], F32, 2)
    Pnb = Ring(S, "aPnb", [128, 4, 128], BF16, 2)
    sm = Ring(S, "asm", [128, 64], F32, 2)
    Ps = Ring(S, "aPs", [128, 128], F32, 2)
    PsT = Ring(S, "aPsT", [128, 128], F32, 2)
    scb = Ring(S, "asc", [128, 2, 32], F32, 2)
    NMT = Ring(S, "aNMT", [32, 128], BF16, 2)
    PcT = Ring(S, "aPcT", [128, 4, 128], BF16, 2)
    cf = Ring(S, "acf", [128, 4, 4, 1], F32, 2)
    acc = Ring(S, "aacc", [128, 3, 4, 64], F32, 2)
    ocs = Ring(S, "aocs", [128, 4, 64], F32, 3)
    for g in range(4):
        u, j = g // 2, g % 2
        prt = slice(u * 64, (u + 1) * 64)
        for r in range(4):
            h = 4 * g + r
            S.dma("sync", MS[:, r, :], c.ms_b[h], reads=[], writes=[("MS", r)])
            S.dma("sync", MW[:, r, :], c.mw_b[h], reads=[], writes=[("MW", r)])
            S.dma("sync", MC[:, r, :], c.mc[h], reads=[], writes=[("MC", r)])
        MSk = [("MS", r) for r in range(4)]
        MWk = [("MW", r) for r in range(4)]
        MCk = [("MC", r) for r in range(4)]
        jobs = []
        for i in range(NT):
            qs = slice(i * 128, (i + 1) * 128)
            st = {}

            def pre(i=i, qs=qs, st=st):
                off = 120 - 8 * i
                for r in range(4):
                    _mm(S, cmpS[:, r, 0:127], A["QT"][prt, j, r, qs], A["KcmpT"][prt, j, 0:127], True, True, [], ["cmpS"])
                sc_, sck = Sc.next()
                S.op("vector", lambda e: e.tensor_tensor(sc_[:], cmpS[:, :, 0:127], MC[:, :, off:off + 127], ALU.add),
                     reads=["cmpS"] + MCk, writes=[sck])
                S.op("scalar", lambda e: e.activation(sc_[:], sc_[:], AF.Exp), reads=[sck], writes=[sck])
                s_, smk = sm.next()
                S.op("vector", lambda e: e.tensor_reduce(s_[:, 0:4], sc_[:], AX.X, ALU.add), reads=[sck], writes=[(smk, 0)])
                S.op("vector", lambda e: e.tensor_scalar_max(s_[:, 0:4], s_[:, 0:4], 1e-30), reads=[(smk, 0)], writes=[(smk, 0)])
                S.op("vector", lambda e: e.reciprocal(s_[:, 4:8], s_[:, 0:4]), reads=[(smk, 0)], writes=[(smk, 1)])
                pn, pnk = Pn.next()
                S.op("vector", lambda e: e.tensor_tensor(pn[:], sc_[:], s_[:, 4:8].unsqueeze(2).to_broadcast([128, 4, 127]), ALU.mult),
                     reads=[sck, (smk, 1)], writes=[pnk])
                pb, pbk = Pnb.next()
                S.op("scalar", lambda e: e.copy(pb[:, :, 0:127], pn[:]), reads=[pnk], writes=[pbk])
                ps_, psk = Ps.next()
                S.op("vector", lambda e: e.tensor_reduce(ps_[:, 0:127], pn[:].rearrange("p r n -> p n r"), AX.X, ALU.add),
                     reads=[pnk], writes=[psk])
                yield
                _tr(S, misc[0:127, 0:128], ps_[:, 0:127], c.identf[:], [psk, "identf"], ["amisc"])
                pt_, ptk = PsT.next()
                S.op("scalar", lambda e: e.copy(pt_[0:127, :], misc[0:127, 0:128]), reads=["amisc"], writes=[ptk])
                yield
                _mm(S, misc[:, 128:160], pt_[0:127, :], c.ovl[0:127, :], True, True, [ptk, "ovl"], ["amisc"])
                sb_, sbk = scb.next()
                S.op("vector", lambda e: e.tensor_tensor(sb_[:, 0, :], misc[:, 128:160], c.vtab[:, i, :], ALU.mult),
                     reads=["amisc", "vtab"], writes=[(sbk, 0)])
                S.op("vector", lambda e: e.tensor_tensor(sb_[:, 0, :], sb_[:, 0, :], c.ctab[:, i, :], ALU.add),
                     reads=[(sbk, 0), "ctab"], writes=[(sbk, 0)])
                S.op("vector", lambda e: e.max(out=s_[:, 8:16], in_=sb_[:, 0, :]), reads=[(sbk, 0)], writes=[(smk, 2)])
                S.op("vector", lambda e: e.tensor_scalar(sb_[:, 1, :], sb_[:, 0, :], s_[:, 15:16], MASKV, ALU.is_lt, ALU.mult),
                     reads=[(sbk, 0), (smk, 2)], writes=[(sbk, 1)])
                yield
                _tr(S, misc[0:32, 0:128], sb_[:, 1, :], c.identf[:], [(sbk, 1), "identf"], ["amisc"])
                nmt, nmk = NMT.next()
                S.op("scalar", lambda e: e.copy(nmt[:], misc[0:32, 0:128]), reads=["amisc"], writes=[nmk])
                st["nmt"] = (nmt, nmk)
                yield
                for r in range(4):
                    _tr(S, tpB[0:127, r, :], pb[:, r, 0:127], c.identb[:], [pbk, "identb"], ["tpB"])
                pc, pck = PcT.next()
                S.op("vector", lambda e: e.tensor_copy(pc[0:127], tpB[0:127, 0:4]), reads=["tpB"], writes=[pck])
                yield
                for r in range(4):
                    _mm(S, misc[:, 160 + 64 * r:224 + 64 * r], pc[0:127, r, :], A["Vcmp"][0:127, g, :], True, True, [pck], ["amisc"])
                oc_, ock = ocs.next()
                S.op("scalar", lambda e: e.copy(oc_[:], misc[:, 160:416].rearrange("p (r d) -> p r d", r=4)), reads=["amisc"], writes=[ock])
                st["oc"] = (oc_, ock)

            qrhs = A["QT"][prt, j, :, qs]
            wj = [jj for jj in (i - 2, i - 1, i) if jj >= 0]
            witems = [(A["KwT"][prt, j, jj * 128:(jj + 1) * 128], MW[:, :, (i - jj) * 128:(i - jj + 1) * 128], None,
                       A["Vw"][:, jj, g, :]) for jj in wj]
            jobs.append(dict(q=qrhs, items=witems, pregen=pre, bias_reads=MWk,
                             done=(lambda O, ok, st=st: st.__setitem__("Ow", (O, ok)))))

            class _LazyItems(list):
                pass

            sitems = []
            for jj in range(i + 1):
                sitems.append([A["KsT"][prt, j, jj * 128:(jj + 1) * 128], MS[:, :, (i - jj) * 128:(i - jj + 1) * 128],
                               ("lazy", jj), A["Vs"][:, jj, g, :]])

            def done(Os, osk, i=i, st=st):
                Ow, owk = st["Ow"]
                cf_, cfk = cf.next()
                a_, ak = acc.next()
                Gv = A["G"][:, i, g * 12:(g + 1) * 12].rearrange("p (r k) -> p r k", k=3)
                S.op("vector", lambda e: e.reciprocal(cf_[:, 0, :, :], Os[:, :, 64:65]), reads=[osk], writes=[(cfk, 0)])
                S.op("vector", lambda e: e.reciprocal(cf_[:, 1, :, :], Ow[:, :, 64:65]), reads=[owk], writes=[(cfk, 1)])
                S.op("vector", lambda e: e.tensor_tensor(cf_[:, 2, :, :], cf_[:, 0, :, :], Gv[:, :, 1:2], ALU.mult),
                     reads=[(cfk, 0)], writes=[(cfk, 2)])
                S.op("vector", lambda e: e.tensor_tensor(cf_[:, 3, :, :], cf_[:, 1, :, :], Gv[:, :, 2:3], ALU.mult),
                     reads=[(cfk, 1)], writes=[(cfk, 3)])
                oc_, ock = st["oc"]
                S.op("vector", lambda e: e.tensor_tensor(a_[:, 0], oc_[:], Gv[:, :, 0:1].to_broadcast([128, 4, 64]), ALU.mult),
                     reads=[ock], writes=[(ak, 0)])
                S.op("vector", lambda e: e.tensor_tensor(a_[:, 1], Os[:, :, 0:64], cf_[:, 2, :, :].to_broadcast([128, 4, 64]), ALU.mult),
                     reads=[osk, (cfk, 2)], writes=[(ak, 1)])
                S.op("vector", lambda e: e.tensor_tensor(a_[:, 2], Ow[:, :, 0:64], cf_[:, 3, :, :].to_broadcast([128, 4, 64]), ALU.mult),
                     reads=[owk, (cfk, 3)], writes=[(ak, 2)])
                S.op("gpsimd", lambda e: e.tensor_tensor(a_[:, 0], a_[:, 0], a_[:, 1], ALU.add), reads=[(ak, 0), (ak, 1)], writes=[(ak, 0)])
                odst = A["Oall"][:, i, g * 256:(g + 1) * 256].rearrange("p (r d) -> p r d", r=4)
                S.op("gpsimd", lambda e: e.tensor_tensor(odst, a_[:, 0], a_[:, 2], ALU.add), reads=[(ak, 0), (ak, 2)], writes=[("Oall", i, odst.offset)])

            jobs.append(dict(q=qrhs, items=sitems, bias_reads=MSk, done=done, lazy=st))
        for jb in jobs:
            if "lazy" in jb:
                jb["items"] = _LazyMaskItems(jb["items"], jb["lazy"], c)
                jb["mask_reads_fn"] = True
        run_attention(S, c, R, jobs)
    S.pop()


class _LazyMaskItems:
    def __init__(self, items, st, c):
        self.items = items
        self.st = st
        self.c = c

    def __len__(self):
        return len(self.items)

    def __getitem__(self, m):
        kT, bias, mask, v = self.items[m]
        jj = mask[1]
        nmt, nmk = self.st["nmt"]
        return (kT, bias, (self.c.indb[:, jj, :], nmt[:, :].unsqueeze(1).to_broadcast([32, 4, 128]), nmk), v)


def out_proj(S, c, w_b, get_o, xin, xout, xin_keys, xout_key):
    S.push()
    Wo = S.sb("Wo", [128, 8, DM], BF16)
    for k in range(8):
        S.dma("sync", Wo[:, k, :], w_b[k * 128:(k + 1) * 128, :], reads=[], writes=[("Wo", k)])
    ptr = Ring(S, "opptr", [128, 8, 128], BF16, 2, psum=True)
    oT = Ring(S, "opoT", [128, 8, 128], BF16, 2)
    po = Ring(S, "oppo", [128, 512], F32, 4, psum=True)
    xr = Ring(S, "opx", [128, DM], F32, 3)
    W = {}
    setup = get_o(None, W)
    for i in range(NT):
        xt, xk = xr.next()
        S.dma("sync", xt[:], xin(i), reads=xin_keys(i), writes=[xk])
        ob, obk = get_o(i, W)
        pt, pk = ptr.next()
        for k in range(8):
            _tr(S, pt[:, k, :], ob[:, k * 128:(k + 1) * 128], c.identb[:], obk + ["identb"], [pk])
        ot, otk = oT.next()
        S.op("scalar", lambda e, ot=ot, pt=pt: e.copy(ot[:], pt[:]), reads=[pk], writes=[otk])
        for n in range(2):
            p_, pok = po.next()
            for k in range(8):
                _mm(S, p_[:], ot[:, k, :], Wo[:, k, n * 512:(n + 1) * 512], k == 0, k == 7, [otk, ("Wo", k)], [pok])
            S.op("vector", lambda e, xt=xt, p_=p_, n=n: e.tensor_tensor(xt[:, n * 512:(n + 1) * 512], xt[:, n * 512:(n + 1) * 512], p_[:], ALU.add),
                 reads=[pok, xk], writes=[xk])
        S.dma("gpsimd", xout(i), xt[:], reads=[xk], writes=[xout_key(i)])
    S.pop()


def ffn(S, c, wup_b, wdn_b, gain_row, xin, xout, xin_keys, xout_key, final_row=None):
    S.push()
    Wup = S.sb("Wup", [128, 8, 2 * FF], BF16)
    Wdn = S.sb("Wdn", [128, NFC, DM], BF16)
    wup_v = wup_b.rearrange("(c p) n -> p c n", p=128)

    def load_weights():
        n = 0
        for p_ in range(NFC // 2):
            for half, base in (("a", 0), ("b", FF)):
                cs = slice(base + p_ * 256, base + (p_ + 1) * 256)
                S.dma("sync", Wup[:, :, cs], wup_v[:, :, cs], reads=[], writes=[("Wup", half, p_)])
                n += 1
        for cc in range(NFC):
            S.dma("sync", Wdn[:, cc, :], wdn_b[cc * 128:(cc + 1) * 128, :], reads=[], writes=[("Wdn", cc)])
    gt = load_gain(S, c, "gffn", gain_row)
    gf = load_gain(S, c, "gfin", final_row) if final_row is not None else None
    W = dict(ss=Ring(S, "fss", [128, 4], F32, 4), hb=Ring(S, "fhb", [128, DM], BF16, 2),
             junk=S.sb("fjunk", [128, DM], BF16), ptr=Ring(S, "fptr", [128, 8, 128], BF16, 2, psum=True))
    xring = Ring(S, "fx", [128, DM], F32, 3)
    hT = Ring(S, "fhT", [128, 8, 256], BF16, 2)
    aT = Ring(S, "faT", [128, NFC, 256], BF16, 2)
    pa = Ring(S, "fpa", [128, 512], F32, 4, psum=True)
    po = Ring(S, "fpo", [128, 512], F32, 2, psum=True)
    sg = Ring(S, "fsg", [128, 256], F32, 3)
    nev = 0
    for mt in range(NT // 2):
        hTt, hTk = hT.next()
        xts = []
        for t in range(2):
            i = mt * 2 + t
            xt, xk, st, sk, outs = norm_tile(S, c, W, xin(i), xin_keys(i), [gt], xring)
            transpose_tile(S, c, W, outs[0][0], outs[0][1], hTt[:, :, t * 128:(t + 1) * 128], (hTk, t),
                           eng="scalar" if t == 0 else "vector")
            xts.append((xt, xk, st, sk))
        if mt == 0:
            load_weights()
        hk2 = [(hTk, 0), (hTk, 1)]
        at, atk = aT.next()
        for cc in range(NFC):
            p1, p1k = pa.next()
            p2, p2k = pa.next()
            for k in range(8):
                _mm(S, p1[:, 0:256], Wup[:, k, cc * 128:(cc + 1) * 128], hTt[:, k, :], k == 0, k == 7, [("Wup", "a", cc // 2)] + hk2, [p1k])
            for k in range(8):
                _mm(S, p2[:, 0:256], Wup[:, k, FF + cc * 128:FF + (cc + 1) * 128], hTt[:, k, :], k == 0, k == 7, [("Wup", "b", cc // 2)] + hk2, [p2k])
            s_, s_k = sg.next()
            S.op("scalar", lambda e, s_=s_, p1=p1: e.activation(s_[:], p1[:, 0:256], AF.Silu), reads=[p1k], writes=[s_k])
            S.op("vector", lambda e, at=at, cc=cc, s_=s_, p2=p2: e.tensor_tensor(at[:, cc, :], s_[:], p2[:, 0:256], ALU.mult),
                 reads=[s_k, p2k], writes=[(atk, cc)])
        for t in range(2):
            i = mt * 2 + t
            xt, xk, st, sk = xts[t]
            for n in range(2):
                p_, pok = po.next()
                for cc in range(NFC):
                    _mm(S, p_[:], at[:, cc, t * 128:(t + 1) * 128], Wdn[:, cc, n * 512:(n + 1) * 512], cc == 0, cc == NFC - 1,
                        [(atk, cc), ("Wdn", cc)], [pok])
                S.op("vector", lambda e, xt=xt, p_=p_, n=n: e.tensor_tensor(xt[:, n * 512:(n + 1) * 512], xt[:, n * 512:(n + 1) * 512], p_[:], ALU.add),
                     reads=[pok, xk], writes=[xk])
            if gf is not None:
                S.op("gpsimd", lambda e, st=st: e.memset(st[:], 0.0), reads=[sk], writes=[sk])
                S.op("scalar", lambda e, xt=xt, st=st: e.activation(W["junk"][:], xt[:], AF.Square, accum_out=st[:, 0:1]),
                     reads=[xk, sk], writes=["junk", sk])
                S.op("vector", lambda e, st=st: e.tensor_scalar(st[:, 1:2], st[:, 0:1], 1.0 / DM, 1e-6, ALU.mult, ALU.add), reads=[sk], writes=[sk])
                S.op("scalar", lambda e, st=st: e.activation(st[:, 3:4], st[:, 1:2], AF.Sqrt), reads=[sk], writes=[sk])
                S.op("vector", lambda e, st=st: e.reciprocal(st[:, 2:3], st[:, 3:4]), reads=[sk], writes=[sk])
                S.op("vector", lambda e, xt=xt, st=st: e.scalar_tensor_tensor(xt[:], xt[:], st[:, 2:3], gf[0][:], ALU.mult, ALU.mult),
                     reads=[xk, sk, gf[1]], writes=[xk])
            S.dma("gpsimd", xout(i), xt[:], reads=[xk], writes=[xout_key(i)])
    S.pop()


def l1_norm(S, c, Lb, xin, xin_keys):
    S.push()
    g1 = load_gain(S, c, "gmix1", 2)
    gk = load_gain(S, c, "gkv", 3)
    W = norm_work(S, "p6")
    xring = Ring(S, "p6x", [128, DM], F32, 3)
    for i in range(NT):
        xt, xk, st, sk, outs = norm_tile(S, c, W, xin(i), xin_keys(i), [g1, gk], xring)
        transpose_tile(S, c, W, outs[0][0], outs[0][1], Lb["HT"][:, :, i * 128:(i + 1) * 128], ("HT", i), eng="scalar")
        transpose_tile(S, c, W, outs[1][0], outs[1][1], Lb["HKT"][:, :, i * 128:(i + 1) * 128], ("HKT", i), eng="vector")
    S.pop()


def l1_group(S, c, Lb, gi):
    win, d = DILS[gi]
    L = SL // d
    tpc = L // 128
    S.push()
    Wq = S.sb("Wq", [128, 8, DM], BF16)
    Wkv = S.sb("Wkv", [128, 8, 512], BF16)
    for k in range(8):
        S.dma("sync", Wq[:, k, :], c.w_q_b[k * 128:(k + 1) * 128, gi * DM:(gi + 1) * DM], reads=[], writes=[("Wq", k)])
        S.dma("sync", Wkv[:, k, :], c.kv_w_b[k * 128:(k + 1) * 128, gi * 512:(gi + 1) * 512], reads=[], writes=[("Wkv", k)])
    MD = S.sb("MD", [128, 16, 256], BF16)
    for h in range(16):
        S.dma("sync", MD[:, h, :], c.md_b[gi * 16 + h], reads=[], writes=[("MD", h)])
    QT = S.sb("QT1", [128, 2, 4, SL], BF16)
    KT = S.sb("KT1", [128, 2, SL], BF16)
    Vp = S.sb("Vp1", [128, NT, 4, 65], BF16)
    S.op("gpsimd", lambda e: e.memset(Vp[:, :, :, 64:65], 1.0), writes=["Vp1_1"])
    pfm = Ring(S, "gpfm", [128, 512], F32, 2, psum=True)
    ptm = Ring(S, "gptm", [128, 512], F32, 1, psum=True)
    nev = 0
    for mt in range(4):
        ts = slice(mt * 512, (mt + 1) * 512)
        l0, ln = mt * 512 // d, 512 // d
        for fc in range(10):
            pt, pk = pfm.next()
            for k in range(8):
                if fc < 8:
                    _mm(S, pt[:], Wq[:, k, fc * 128:(fc + 1) * 128], Lb["HT"][:, k, ts], k == 0, k == 7, [("Wq", k)], [pk])
                else:
                    _mm(S, pt[:], Wkv[:, k, (fc - 8) * 128:(fc - 7) * 128], Lb["HKT"][:, k, ts], k == 0, k == 7, [("Wkv", k)], [pk])
            if fc < 8:
                dst = QT[:, fc // 4, fc % 4, :]
                sc = 0.125
            else:
                dst = KT[:, fc - 8, :]
                sc = 1.0
            dst = dst.rearrange("p (c l) -> p c l", c=d)[:, :, l0:l0 + ln]
            src = pt[:].rearrange("p (l c) -> p c l", c=d)
            if nev % 2 == 0:
                S.op("scalar", lambda e, dst=dst, src=src, sc=sc: e.mul(dst, src, sc), reads=[pk], writes=[("fm1", fc, mt)])
            else:
                S.op("vector", lambda e, dst=dst, src=src, sc=sc: e.tensor_scalar(dst, src, sc, None, ALU.mult), reads=[pk], writes=[("fm1", fc, mt)])
            nev += 1
    for T in range(NT):
        cls, lb = T // tpc, T % tpc
        start = cls + d * 128 * lb
        pt, pk = ptm.next()
        for k in range(8):
            _mm(S, pt[:, 0:256], Lb["HKT"][:, k, start:start + d * 127 + 1:d], Wkv[:, k, 256:512], k == 0, k == 7, [("Wkv", k)], [pk])
        S.op("vector" if T % 2 else "scalar",
             (lambda e, pt=pt, T=T: e.tensor_copy(Vp[:, T, :, 0:64], pt[:, 0:256].rearrange("p (g e) -> p g e", g=4))) if T % 2 else
             (lambda e, pt=pt, T=T: e.copy(Vp[:, T, :, 0:64], pt[:, 0:256].rearrange("p (g e) -> p g e", g=4))),
             reads=[pk, "Vp1_1"], writes=[("Vp1", T)])
    R = dict(ST=Ring(S, "gST", [128, 4, 128], F32, 2, psum=True), O=Ring(S, "gO", [128, 4, 128], F32, 3, psum=True),
             PT=Ring(S, "gPT", [128, 4, 128], BF16, 3))
    Ost = Ring(S, "gOst", [128, 16, 65], F32, 2)
    MDk = [("MD", h) for h in range(16)]
    allfm = [("fm1", fc, mt) for fc in range(10) for mt in range(4)]
    odv = c.od[gi].rearrange("(l c) f -> c l f", c=d)
    jobs = []
    for T in range(NT):
        cls, lb = T // tpc, T % tpc
        st = {}
        for g in range(4):
            u, j = g // 2, g % 2
            prt = slice(u * 64, (u + 1) * 64)
            items = []
            for TT in ([T - 1] if lb > 0 else []) + [T]:
                dl = T - TT
                items.append((KT[prt, j, TT * 128:(TT + 1) * 128], MD[:, 4 * g:4 * g + 4, dl * 128:(dl + 1) * 128], None, Vp[:, TT, g, :]))

            def pre(st=st):
                st["ost"] = Ost.next()

            def done(O, ok, T=T, g=g, st=st, cls=cls, lb=lb):
                ost, ostk = st["ost"]
                if g % 2 == 0:
                    S.op("vector", lambda e: e.tensor_copy(ost[:, 4 * g:4 * g + 4, :], O[:, :, 0:65]), reads=[ok], writes=[(ostk, g)])
                else:
                    S.op("scalar", lambda e: e.copy(ost[:, 4 * g:4 * g + 4, :], O[:, :, 0:65]), reads=[ok], writes=[(ostk, g)])
                if g == 3:
                    S.dma("gpsimd", odv[cls, lb * 128:(lb + 1) * 128, :], ost[:].rearrange("p h e -> p (h e)"),
                          reads=[(ostk, gg) for gg in range(4)], writes=[("od", gi, T)])

            jobs.append(dict(q=QT[prt, j, :, T * 128:(T + 1) * 128], items=items, pre=pre if g == 0 else None, done=done,
                             qk_reads=allfm, bias_reads=MDk, v_reads=[("Vp1", TT) for TT in range(NT)]))
    run_attention(S, c, R, jobs)
    S.pop()


def l1_merge_get_o(S, c):
    def get_o(i, W):
        if i is None:
            W["ld"] = Ring(S, "mld", [128, 16, 65], F32, 4)
            W["sm"] = Ring(S, "msm", [128, 16, 1], F32, 2)
            W["ob"] = Ring(S, "mob", [128, DM], BF16, 2)
            return
        ts = []
        for gi in range(3):
            t, k = W["ld"].next()
            S.dma("sync", t[:].rearrange("p h e -> p (h e)"), c.od[gi][i * 128:(i + 1) * 128, :],
                  reads=[("od", gi, T) for T in range(NT)], writes=[k])
            ts.append((t, k))
        (t0, k0), (t1, k1), (t2, k2) = ts
        S.op("gpsimd", lambda e: e.tensor_tensor(t0[:], t0[:], t1[:], ALU.add), reads=[k0, k1], writes=[k0])
        S.op("vector", lambda e: e.tensor_tensor(t0[:], t0[:], t2[:], ALU.add), reads=[k0, k2], writes=[k0])
        sm, smk = W["sm"].next()
        S.op("vector", lambda e: e.reciprocal(sm[:], t0[:, :, 64:65]), reads=[k0], writes=[smk])
        ob, obk = W["ob"].next()
        S.op("vector", lambda e: e.tensor_tensor(ob[:].rearrange("p (h e) -> p h e", h=16), t0[:, :, 0:64],
                                                 sm[:].to_broadcast([128, 16, 64]), ALU.mult), reads=[k0, smk], writes=[obk])
        return ob, [obk]
    return get_o


INPUT_SHAPES = dict(
    x=[NSEQ, SL, DM], w_in=[DM, 2608], w1k=[2048, 256], w1v=[2048, 256], w2k=[256, 64], w2v=[256, 64],
    pekT=[64, 32], pevT=[64, 32], w_out0=[DM, DM], w_out1=[DM, DM], w_up0=[DM, 2 * FF], w_up1=[DM, 2 * FF],
    w_dn0=[FF, DM], w_dn1=[FF, DM], kv_w=[DM, 1536], w_q=[DM, 3072], gains=[6, DM], b_gate=[1, 48],
    ms=[16, 128, 2048], mw=[16, 128, 384], mc=[16, 128, 247], md=[48, 128, 256],
    vtab=[128, NT, 32], ctab=[128, NT, 32], ovl=[127, 32], ind=[32, NT, 128],
)
CONV = ("w_in", "w1k", "w1v", "w2k", "w2v", "w_out0", "w_out1", "w_up0", "w_up1", "w_dn0", "w_dn1", "kv_w", "w_q",
        "ms", "mw", "md")


def build_program(nseq=NSEQ, max_phase=10 ** 9):
    nc = bass.Bass("TRN2", target_bir_lowering=False)
    c = Ctx()
    shapes = dict(INPUT_SHAPES)
    shapes["x"] = [nseq, SL, DM]
    for name, shp in shapes.items():
        setattr(c, name, nc.dram_tensor(name, list(shp), F32, kind="ExternalInput").ap())
    y = nc.dram_tensor("y", [nseq, SL, DM], F32, kind="ExternalOutput").ap()
    c.conv_jobs = []
    conv_only = os.environ.get("CONV_ONLY")
    for name in CONV:
        shp = shapes[name]
        dst = nc.dram_tensor(name + "_b", list(shp), BF16).ap()
        setattr(c, name + "_b", dst)
        src = getattr(c, name)
        if conv_only is not None and name not in conv_only.split(","):
            continue
        if len(shp) == 3:
            c.conv_jobs.append((src.rearrange("h p c -> (h p) c"), dst.rearrange("h p c -> (h p) c"), name + "_b"))
        else:
            c.conv_jobs.append((src, dst, name + "_b"))
    c.xres = nc.dram_tensor("xres", [SL, DM], F32).ap()
    c.od = nc.dram_tensor("od", [3, SL, 16 * 65], F32).ap()

    S = Sched(nc)
    c.identf = S.sb("identf", [128, 128], F32)
    c.identb = S.sb("identb", [128, 128], BF16)
    S.op("gpsimd", lambda e: e.memset(c.identf[:], 1.0), writes=["identf"])
    S.op("gpsimd", lambda e: e.affine_select(c.identf[:], c.identf[:], pattern=[[-1, 128]], compare_op=ALU.is_equal,
                                             fill=0.0, base=0, channel_multiplier=1), reads=["identf"], writes=["identf"])
    S.op("vector", lambda e: e.tensor_copy(c.identb[:], c.identf[:]), reads=["identf"], writes=["identb"])
    c.indb = S.sb("indb", [32, NT, 128], BF16)
    for nm, shp in (("vtab", [128, NT, 32]), ("ctab", [128, NT, 32]), ("ovl", [127, 32])):
        t = S.sb(nm + "_s", shp, F32)
        S.dma("sync", t[:], getattr(c, nm), reads=[], writes=[nm])
        setattr(c, nm, t)
    S.push()
    indf = S.sb("indf", [32, NT, 128], F32)
    S.dma("sync", indf[:], c.ind, reads=[], writes=["indf"])
    S.op("vector", lambda e: e.tensor_copy(c.indb[:], indf[:]), reads=["indf"], writes=["indb"])
    S.pop()
    convert_all(S, c)

    cnt = [0]

    def ph(fn, *a, **k):
        cnt[0] += 1
        if cnt[0] <= max_phase:
            fn(*a, **k)

    xres_in = lambda i: c.xres[i * 128:(i + 1) * 128, :]
    xres_k = lambda i: [("xres", i)]
    xres_k1 = lambda i: ("xres", i)
    for b in range(nseq):
        S.push()
        A = dict(QT=S.sb("QT", [128, 2, 4, SL], BF16), KsT=S.sb("KsT", [128, 2, SL], BF16), KwT=S.sb("KwT", [128, 2, SL], BF16),
                 Vs=S.sb("Vs", [128, NT, 4, 65], BF16), Vw=S.sb("Vw", [128, NT, 4, 65], BF16), G=S.sb("G", [128, NT, 48], F32),
                 KcmpT=S.sb("KcmpT", [128, 2, 128], BF16), Vcmp=S.sb("Vcmp", [128, 4, 64], BF16))
        S.push()
        Bk = dict(KcT=S.sb("KcT", [128, 2, SL], BF16), VcT=S.sb("VcT", [128, 2, SL], BF16))
        ph(l0_proj, S, c, A, Bk, b)
        ph(l0_compress, S, c, A, Bk)
        S.pop()
        S.push()
        A["Oall"] = S.sb("Oall", [128, NT, DM], BF16)
        ph(l0_attention, S, c, A, b)
        if os.environ.get("DBG_DUMP") == "Oall":
            S.push()
            dr = Ring(S, "dbgd", [128, DM], F32, 2)
            for i in range(NT):
                t_, k_ = dr.next()
                S.op("vector", lambda e, t_=t_, i=i: e.tensor_copy(t_[:], A["Oall"][:, i, :]), reads=[], writes=[k_])
                S.dma("sync", y[0, i * 128:(i + 1) * 128, :], t_[:], reads=[k_], writes=[("ydump", i)])
            S.pop()
        ph(out_proj, S, c, c.w_out0_b, lambda i, W: (A["Oall"][:, i, :], []) if i is not None else None,
                 lambda i: c.x[b, i * 128:(i + 1) * 128, :], xres_in, lambda i: [], xres_k1)
        S.pop()
        S.pop()
        ph(ffn, S, c, c.w_up0_b, c.w_dn0_b, 1, xres_in, xres_in, xres_k, xres_k1)
        S.push()
        Lb = dict(HT=S.sb("HT", [128, 8, SL], BF16), HKT=S.sb("HKT", [128, 8, SL], BF16))
        ph(l1_norm, S, c, Lb, xres_in, xres_k)
        for gi in range(3):
            ph(l1_group, S, c, Lb, gi)
        S.pop()
        ph(out_proj, S, c, c.w_out1_b, l1_merge_get_o(S, c), xres_in, xres_in, xres_k, xres_k1)
        ph(ffn, S, c, c.w_up1_b, c.w_dn1_b, 4, xres_in, lambda i: y[b, i * 128:(i + 1) * 128, :], xres_k, lambda i: ("y", b, i),
            final_row=5)
    if os.environ.get("DBG_DUMP"):
        S.finish(final_keys=[])
    elif max_phase < 10 ** 9:
        S.dma("sync", y[0], c.xres, reads=[("xres", i) for i in range(NT)], writes=["ydbg"])
        S.finish(final_keys=["ydbg"])
    else:
        S.finish(final_keys=[("y", b, i) for b in range(nseq) for i in range(NT)])
    return nc


def _t5_bucket(dist):
    d = np.maximum(dist, 0)
    lr = np.log(np.maximum(d, 16).astype(np.float32) / np.float32(16)) / np.float32(np.log(2048.0 / 16.0))
    large = 16 + (lr * np.float32(16)).astype(np.int32)
    return np.where(d < 16, d, np.minimum(large, 31)).astype(np.int64)


def _qperm():
    cols = []
    for j in range(2):
        for r in range(4):
            for g in (j, 2 + j):
                h = 4 * g + r
                cols.extend(range(h * 64, h * 64 + 64))
    return np.array(cols)


def _kperm():
    cols = []
    for j in range(2):
        for g in (j, 2 + j):
            cols.extend(range(g * 64, g * 64 + 64))
    return np.array(cols)


def host_prep(inp):
    f = lambda a: np.ascontiguousarray(np.asarray(a, dtype=np.float32))
    qp, kp = _qperm(), _kperm()
    nat = np.arange(256)
    w_in = f(inp["a_w_in"])[0]
    col = np.concatenate([qp, 1024 + kp, 1280 + kp, 1536 + kp, 2048 + kp, 1792 + nat, 2304 + nat, 2560 + np.arange(48)])
    sh = {}
    sh["w_in"] = f(w_in[:, col])
    sh["w1k"] = f(inp["a_w1_k"])[0]
    sh["w1v"] = f(inp["a_w1_v"])[0]
    sh["w2k"] = f(inp["a_w2_k"])[0]
    sh["w2v"] = f(inp["a_w2_v"])[0]
    sh["pekT"] = f(f(inp["a_pe_k"])[0].T)
    sh["pevT"] = f(f(inp["a_pe_v"])[0].T)
    sh["w_out0"] = f(inp["a_w_out"])[0]
    sh["w_out1"] = f(inp["b_w_out"])[0]
    sh["w_up0"], sh["w_up1"] = f(inp["ffn_w_up"])[0], f(inp["ffn_w_up"])[1]
    sh["w_dn0"], sh["w_dn1"] = f(inp["ffn_w_down"])[0], f(inp["ffn_w_down"])[1]
    kvw = f(inp["kv_w"])
    sh["kv_w"] = f(kvw[:, np.concatenate([np.concatenate([gi * 512 + kp, gi * 512 + 256 + nat]) for gi in range(3)])])
    wq = f(inp["b_w_q"])[0]
    sh["w_q"] = f(wq[:, np.concatenate([gi * 1024 + qp for gi in range(3)])])
    nm, nf = f(inp["norm_mix"]), f(inp["norm_ffn"])
    sh["gains"] = f(np.stack([nm[0], nf[0], nm[1], f(inp["kv_norm"]), nf[1], f(inp["final_norm"])]))
    sh["b_gate"] = f(inp["a_b_gate"]).reshape(1, 48)
    rb = f(inp["rel_bias"])
    rbT = np.ascontiguousarray(rb.T)
    p = np.arange(128)[:, None]

    def toep(ncol, maxd, scale):
        cc = np.arange(ncol)[None, :]
        dist = cc - p
        ok = (dist >= 0) & (dist <= maxd)
        g = rbT[:, _t5_bucket(dist * scale)]
        return f(np.where(ok[None], g, np.float32(MASKV)))

    sh["ms"] = toep(2048, 1 << 30, 1)
    sh["mw"] = toep(384, 255, 1)
    sh["md"] = f(np.concatenate([toep(256, 128, d) for (_, d) in DILS], axis=0))
    m = np.arange(247)[None, :]
    dist = p - 16 * (m - 120) - 31
    sh["mc"] = f(np.where((dist >= 0)[None], rbT[:, _t5_bucket(dist)], np.float32(MASKV)))
    i = np.arange(NT)[None, :, None]
    jb = np.arange(32)[None, None, :]
    cur = 2 * i + (np.arange(128)[:, None, None] // 64)
    valid = jb <= cur
    forced = (jb == 0) | (jb == cur) | (jb == cur - 1)
    sh["vtab"] = f(valid.astype(np.float32))
    sh["ctab"] = f(np.where(forced, 100.0, np.where(valid, 0.0, -1.0)))
    ci = np.arange(127)[:, None] * 16
    sj = np.arange(32)[None, :] * 64
    sh["ovl"] = f(((ci < sj + 64) & (ci + 32 > sj)).astype(np.float32))
    cblk = np.arange(32)[:, None, None]
    sh["ind"] = f((cblk == 2 * np.arange(NT)[None, :, None] + np.arange(128)[None, None, :] // 64).astype(np.float32))
    return sh


_PROGRAM = {}


def kernel(**inputs):
    x = np.asarray(inputs["x"], dtype=np.float32)
    sh = host_prep(inputs)
    if "nc" not in _PROGRAM:
        _PROGRAM["nc"] = build_program()
    nc = _PROGRAM["nc"]
    in_maps = []
    for cid in range(NCORES):
        d = dict(sh)
        d["x"] = np.ascontiguousarray(x[cid * NSEQ:(cid + 1) * NSEQ])
        in_maps.append(d)
    res = run_bass_kernel_spmd(nc, in_maps, core_ids=list(range(NCORES)))
    return np.concatenate([np.asarray(r["y"], dtype=np.float32) for r in res.results], axis=0)
```
